# Optimizing a Trainium2 kernel written in Bass

```python
import math
import jax, jax.numpy as jnp
from jax import lax
import numpy as np

D_MODEL = 4096
BATCH = 1
SEQ = 8192
DEPTH = 2

D_MIX = D_MODEL
A_HEADS = 8
A_HDIM = 64
A_WIDTH = A_HEADS * 2 * A_HDIM
B_HDIM = 64
B_WIDTH = 2048
B_HEADS = B_WIDTH // B_HDIM
B_DECAY_LORA = 96
B_AAA_LORA = 96
B_MV_LORA = 64
B_GATE_LORA = 256
B_GN_EPS = 64e-5
C_HEADS = 8
C_HDIM = 128
C_WIDTH = C_HEADS * C_HDIM
IDX_HEADS = 16
IDX_HDIM = 64
TOPK_MAX = 256
D_FF = 4 * D_MODEL
Q_BLOCK = 128
EPS = 1e-6
SPLITS = (A_WIDTH, A_WIDTH, A_WIDTH,
          B_WIDTH, B_WIDTH, B_WIDTH,
          C_WIDTH, C_WIDTH, C_WIDTH,
          IDX_HEADS * IDX_HDIM, IDX_HDIM, IDX_HEADS)
D_IN = 3 * A_WIDTH + 3 * B_WIDTH + 3 * C_WIDTH + IDX_HEADS * IDX_HDIM + IDX_HDIM + IDX_HEADS

kernel_name = "hymba_diffattn_rwkv7_dsa_trunk"


def rms_norm(x, g, eps=EPS):
    xf = x.astype(jnp.float32)
    y = xf * lax.rsqrt(jnp.mean(xf * xf, axis=-1, keepdims=True) + eps)
    return (y * g.astype(jnp.float32)).astype(x.dtype)


def token_shift(t):
    return jnp.pad(t, ((0, 0), (1, 0), (0, 0)))[:, :-1]


def diff_attention(q, k, v, lam, lam_init, subln_g):
    B, S, _ = q.shape
    nb = S // Q_BLOCK
    scale = A_HDIM ** -0.5
    q = q.reshape(B, S, A_HEADS, 2, A_HDIM)
    k = k.reshape(B, S, A_HEADS, 2, A_HDIM)
    v = v.reshape(B, S, A_HEADS, 2 * A_HDIM)
    qb = q.reshape(B, nb, Q_BLOCK, A_HEADS, 2, A_HDIM).transpose(1, 0, 2, 3, 4, 5)
    kpos = jnp.arange(S)

    def block(args):
        qi, i = args
        qpos = i * Q_BLOCK + jnp.arange(Q_BLOCK)
        s = jnp.einsum('bqhmd,bkhmd->bhmqk', qi, k).astype(jnp.float32) * scale
        s = jnp.where(kpos[None, :] <= qpos[:, None], s, -jnp.inf)
        p = jax.nn.softmax(s, axis=-1)
        p = p[:, :, 0] - lam * p[:, :, 1]
        return jnp.einsum('bhqk,bkhe->bqhe', p.astype(v.dtype), v)

    o = lax.map(block, (qb, jnp.arange(nb)))
    o = o.transpose(1, 0, 2, 3, 4).reshape(B, S, A_HEADS, 2 * A_HDIM)
    o = rms_norm(o, subln_g) * (1.0 - lam_init)
    return o.reshape(B, S, A_WIDTH)


def wkv7_scan(r, w, k, v, a, b):
    Bsz, S, H, N = r.shape

    def step(state, inp):
        r_t, w_t, k_t, v_t, a_t, b_t = inp
        sa = jnp.einsum('bhvk,bhk->bhv', state, a_t)
        state = (state * w_t[:, :, None, :]
                 + sa[..., None] * b_t[:, :, None, :]
                 + v_t[..., None] * k_t[:, :, None, :])
        y = jnp.einsum('bhvk,bhk->bhv', state, r_t)
        return state, y

    xs = (r.transpose(1, 0, 2, 3), w.transpose(1, 0, 2, 3), k.transpose(1, 0, 2, 3),
          v.transpose(1, 0, 2, 3), a.transpose(1, 0, 2, 3), b.transpose(1, 0, 2, 3))
    s0 = jnp.zeros((Bsz, H, N, N), jnp.float32)
    _, y = lax.scan(step, s0, xs)
    return y.transpose(1, 0, 2, 3)


def rwkv7_mixer(h, r, k, v, mu_rkv, mu_wag, w0, w1, w2, a0, a1, a2, g1, g2,
                k_k, k_a, r_k, ln_w, ln_b, v_res):
    B, S, _ = h.shape
    f32 = jnp.float32
    r = r + (token_shift(r) - r) * mu_rkv[0]
    k = k + (token_shift(k) - k) * mu_rkv[1]
    v = v + (token_shift(v) - v) * mu_rkv[2]
    dh = token_shift(h) - h
    xw = h + dh * mu_wag[0]
    xa = h + dh * mu_wag[1]
    xg = h + dh * mu_wag[2]
    wlog = -jax.nn.softplus(-(w0 + jnp.tanh(xw @ w1) @ w2).astype(f32)) - 0.5
    decay = jnp.exp(-jnp.exp(wlog))
    a = jax.nn.sigmoid((a0 + (xa @ a1) @ a2).astype(f32))
    g = jax.nn.sigmoid(xg @ g1) @ g2
    if v_res is not None:
        v_first, v_mu, v0, v1, v2 = v_res
        xv = h + dh * v_mu
        v = v + (v_first - v) * jax.nn.sigmoid(v0 + (xv @ v1) @ v2)
    v_out = v

    def heads(t):
        return t.reshape(B, S, B_HEADS, B_HDIM).astype(f32)

    kk = heads(k * k_k)
    kk = kk / jnp.maximum(jnp.sqrt(jnp.sum(kk * kk, axis=-1, keepdims=True)), 1e-12)
    k_new = k.astype(f32) * (1.0 + (a - 1.0) * k_a.astype(f32))
    rh, kh, vh, ah = heads(r), heads(k_new), heads(v), heads(a)
    y = wkv7_scan(rh, heads(decay), kh, vh, -kk, kk * ah)
    mean = jnp.mean(y, axis=-1, keepdims=True)
    var = jnp.mean(jnp.square(y - mean), axis=-1, keepdims=True)
    yn = ((y - mean) * lax.rsqrt(var + B_GN_EPS)).reshape(B, S, B_WIDTH)
    yn = yn * ln_w.astype(f32) + ln_b.astype(f32)
    bonus = jnp.sum(rh * kh * r_k.astype(f32), axis=-1, keepdims=True) * vh
    out = (yn + bonus.reshape(B, S, B_WIDTH)) * g.astype(f32)
    return out.astype(h.dtype), v_out


def dsa_mixer(q, k, v, qi, ki, wi):
    B, S, _ = q.shape
    nb = S // Q_BLOCK
    topk = min(TOPK_MAX, S // 4)
    scale = C_HDIM ** -0.5
    q = q.reshape(B, S, C_HEADS, C_HDIM)
    k = k.reshape(B, S, C_HEADS, C_HDIM)
    v = v.reshape(B, S, C_HEADS, C_HDIM)
    qi = qi.reshape(B, S, IDX_HEADS, IDX_HDIM)
    wi = wi * (IDX_HEADS ** -0.5 * IDX_HDIM ** -0.5)
    qb = q.reshape(B, nb, Q_BLOCK, C_HEADS, C_HDIM).transpose(1, 0, 2, 3, 4)
    qib = qi.reshape(B, nb, Q_BLOCK, IDX_HEADS, IDX_HDIM).transpose(1, 0, 2, 3, 4)
    wib = wi.reshape(B, nb, Q_BLOCK, IDX_HEADS).transpose(1, 0, 2, 3)
    kpos = jnp.arange(S)

    def block(args):
        qc, qic, wic, i = args
        qpos = i * Q_BLOCK + jnp.arange(Q_BLOCK)
        dots = jnp.einsum('bqhd,bkd->bqhk', qic, ki).astype(jnp.float32)
        score = jnp.einsum('bqh,bqhk->bqk', wic.astype(jnp.float32), jax.nn.relu(dots))
        score = jnp.where(kpos[None, None, :] <= qpos[None, :, None], score, -jnp.inf)
        _, idx = lax.top_k(score, topk)
        kg = jax.vmap(lambda kb, ib: kb[ib])(k, idx)
        vg = jax.vmap(lambda vb, ib: vb[ib])(v, idx)
        valid = idx <= qpos[None, :, None]
        s = jnp.einsum('bqhd,bqkhd->bqhk', qc, kg).astype(jnp.float32) * scale
        s = jnp.where(valid[:, :, None, :], s, -jnp.inf)
        p = jax.nn.softmax(s, axis=-1)
        return jnp.einsum('bqhk,bqkhd->bqhd', p.astype(vg.dtype), vg)

    o = lax.map(block, (qb, qib, wib, jnp.arange(nb)))
    return o.transpose(1, 0, 2, 3, 4).reshape(B, S, C_WIDTH)


def setup_inputs(seed: int = 0) -> dict:
    key = jax.random.key(seed)
    keys = jax.random.split(key, 40)
    counter = [0]

    def nk():
        kk = keys[counter[0]]
        counter[0] += 1
        return kk

    def nrm(shape, scale):
        return scale * jax.random.normal(nk(), shape, jnp.float32)

    def unif(shape, lo, hi):
        return jax.random.uniform(nk(), shape, jnp.float32, lo, hi)

    L = DEPTH
    Lv = DEPTH - 1
    return {
        "x": nrm((BATCH, SEQ, D_MODEL), 1.0),
        "norm_mix_g": 1.0 + nrm((L, D_MODEL), 0.02),
        "w_in": nrm((L, D_MODEL, D_IN), D_MODEL ** -0.5),
        "lam_q1": nrm((L, A_HDIM), 0.1),
        "lam_k1": nrm((L, A_HDIM), 0.1),
        "lam_q2": nrm((L, A_HDIM), 0.1),
        "lam_k2": nrm((L, A_HDIM), 0.1),
        "diff_subln_g": 1.0 + nrm((L, 2 * A_HDIM), 0.02),
        "rw_mu_rkv": unif((L, 3, B_WIDTH), 0.0, 1.0),
        "rw_mu_wag": unif((L, 3, D_MODEL), 0.0, 1.0),
        "rw_w0": unif((L, B_WIDTH), -4.0, 1.0),
        "rw_w1": nrm((L, D_MODEL, B_DECAY_LORA), D_MODEL ** -0.5),
        "rw_w2": nrm((L, B_DECAY_LORA, B_WIDTH), 0.1 * B_DECAY_LORA ** -0.5),
        "rw_a0": nrm((L, B_WIDTH), 0.1),
        "rw_a1": nrm((L, D_MODEL, B_AAA_LORA), D_MODEL ** -0.5),
        "rw_a2": nrm((L, B_AAA_LORA, B_WIDTH), 0.5 * B_AAA_LORA ** -0.5),
        "rw_g1": nrm((L, D_MODEL, B_GATE_LORA), D_MODEL ** -0.5),
        "rw_g2": nrm((L, B_GATE_LORA, B_WIDTH), B_GATE_LORA ** -0.5),
        "rw_k_k": 0.85 + nrm((L, B_WIDTH), 0.05),
        "rw_k_a": 1.0 + nrm((L, B_WIDTH), 0.05),
        "rw_r_k": nrm((L, B_HEADS, B_HDIM), 0.1),
        "rw_ln_w": 1.0 + nrm((L, B_WIDTH), 0.02),
        "rw_ln_b": nrm((L, B_WIDTH), 0.02),
        "rw_v_mu": unif((Lv, D_MODEL), 0.0, 1.0),
        "rw_v0": 0.5 + nrm((Lv, B_WIDTH), 0.1),
        "rw_v1": nrm((Lv, D_MODEL, B_MV_LORA), D_MODEL ** -0.5),
        "rw_v2": nrm((Lv, B_MV_LORA, B_WIDTH), 0.5 * B_MV_LORA ** -0.5),
        "w_out": nrm((L, D_MIX, D_MODEL), D_MIX ** -0.5),
        "norm_ffn_g": 1.0 + nrm((L, D_MODEL), 0.02),
        "w_up": nrm((L, D_MODEL, D_FF), D_MODEL ** -0.5),
        "w_down": nrm((L, D_FF, D_MODEL), D_FF ** -0.5),
        "norm_final_g": 1.0 + nrm((D_MODEL,), 0.02),
    }


def reference(x, norm_mix_g, w_in, lam_q1, lam_k1, lam_q2, lam_k2, diff_subln_g,
              rw_mu_rkv, rw_mu_wag, rw_w0, rw_w1, rw_w2, rw_a0, rw_a1, rw_a2,
              rw_g1, rw_g2, rw_k_k, rw_k_a, rw_r_k, rw_ln_w, rw_ln_b,
              rw_v_mu, rw_v0, rw_v1, rw_v2, w_out, norm_ffn_g, w_up, w_down,
              norm_final_g):
    cuts = []
    acc = 0
    for s in SPLITS[:-1]:
        acc += s
        cuts.append(acc)
    v_first = None
    for l in range(DEPTH):
        h = rms_norm(x, norm_mix_g[l])
        proj = h @ w_in[l]
        (qa, ka, va, rb, kb, vb, qc, kc, vc, qi, ki, wi) = jnp.split(proj, cuts, axis=-1)
        lam_init = 0.8 - 0.6 * math.exp(-0.3 * l)
        lam = (jnp.exp(jnp.sum(lam_q1[l] * lam_k1[l]).astype(jnp.float32))
               - jnp.exp(jnp.sum(lam_q2[l] * lam_k2[l]).astype(jnp.float32)) + lam_init)
        o_a = diff_attention(qa, ka, va, lam, lam_init, diff_subln_g[l])
        v_res = None if l == 0 else (v_first, rw_v_mu[l - 1], rw_v0[l - 1], rw_v1[l - 1], rw_v2[l - 1])
        o_b, v_out = rwkv7_mixer(h, rb, kb, vb, rw_mu_rkv[l], rw_mu_wag[l], rw_w0[l], rw_w1[l], rw_w2[l],
                                 rw_a0[l], rw_a1[l], rw_a2[l], rw_g1[l], rw_g2[l], rw_k_k[l], rw_k_a[l],
                                 rw_r_k[l], rw_ln_w[l], rw_ln_b[l], v_res)
        if l == 0:
            v_first = v_out
        o_c = dsa_mixer(qc, kc, vc, qi, ki, wi)
        mixed = jnp.concatenate([o_a, o_b.astype(o_a.dtype), o_c], axis=-1)
        x = x + mixed @ w_out[l]
        h2 = rms_norm(x, norm_ffn_g[l])
        x = x + jnp.square(jax.nn.relu(h2 @ w_up[l])) @ w_down[l]
    return rms_norm(x, norm_final_g)
```

```python
import contextlib
import math
import os
import numpy as np
import concourse.bass as bass
import concourse.mybir as mybir
from concourse.bass_utils import run_bass_kernel_spmd

F32 = mybir.dt.float32
BF16 = mybir.dt.bfloat16
AF = mybir.ActivationFunctionType
ALU = mybir.AluOpType
AX = mybir.AxisListType

D = 4096
KC = D // 128
D_IN = 13392
NEG = -3.0e38
DEC = -0.6065306597126334


class _Stop(Exception):
    pass


RW_STOP = int(os.environ.get('RW_STOP', '0'))


_DEAD = [False]


def _chk(k):
    if RW_STOP == k:
        _DEAD[0] = True


class Reg:
    __slots__ = ("w", "rs")

    def __init__(self):
        self.w = None
        self.rs = {}


class Tl:
    def __init__(self, t):
        self.t = t
        self.r = Reg()

    def __getitem__(self, k):
        return self.t[k]


class Sched:
    EPOCH = 16000

    def __init__(self, nc, es, ndma=16):
        self.nc = nc
        self.es = es
        self.E = {"pe": nc.tensor, "act": nc.scalar, "dve": nc.vector, "pool": nc.gpsimd, "sp": nc.sync}
        self.sems = {}
        self.ep = {k: 0 for k in self.E}
        self.cnt = {k: 0 for k in self.E}
        for k in self.E:
            self.sems[(k, 0)] = es.enter_context(nc.semaphore(f"s_{k}_0"))
        self.dsem = {i: es.enter_context(nc.semaphore(f"sd{i}")) for i in range(ndma)}
        self.did = list(range(ndma))
        self.dgen = ndma - 1
        self.ndma = ndma
        self.dcnt = [0] * ndma
        self.dnext = 0
        self.seen = {k: {} for k in self.E}
        self.rr = 0

    def _wait(self, eng, tok):
        if tok is None:
            return
        key, val = tok
        if key[0] == eng and eng == "pe":
            return
        sn = self.seen[eng]
        if key[0] == "d":
            if sn.get(key, 0) >= val:
                return
            self.E[eng].wait_ge(self.dsem[key[1]], val)
            sn[key] = val
            return
        cur = sn.get(key[0], (-1, 0))
        if cur >= (key[1], val):
            return
        self.E[eng].wait_ge(self.sems[key], val)
        sn[key[0]] = (key[1], val)

    def _deps(self, eng, r, w):
        for x in r:
            self._wait(eng, x.r.w)
        for x in w:
            self._wait(eng, x.r.w)
            for t in x.r.rs.values():
                self._wait(eng, t)

    def _mark(self, tok, r, w):
        for x in r:
            x.r.rs[tok[0][0] if tok[0][0] != "d" else tok[0]] = tok
        for x in w:
            x.r.w = tok
            x.r.rs = {}

    def op(self, eng, fn, r=(), w=()):
        if _DEAD[0]:
            return
        self._deps(eng, r, w)
        if self.cnt[eng] >= self.EPOCH:
            self.ep[eng] += 1
            self.cnt[eng] = 0
            self.sems[(eng, self.ep[eng])] = self.es.enter_context(
                self.nc.semaphore(f"s_{eng}_{self.ep[eng]}"))
        inst = fn(self.E[eng])
        self.cnt[eng] += 1
        key = (eng, self.ep[eng])
        inst.then_inc(self.sems[key], 1)
        self._mark((key, self.cnt[eng]), r, w)

    def dma(self, out, in_, r=(), w=(), q=None):
        if _DEAD[0]:
            return
        if q is None:
            q = ("sp", "act")[self.rr % 2]
            self.rr += 1
        i = self.dnext
        self.dnext = (self.dnext + 1) % self.ndma
        self._wait(q, (("d", self.did[i]), 16 * self.dcnt[i]))
        if self.dcnt[i] >= 1000:
            self.dgen += 1
            self.did[i] = self.dgen
            self.dsem[self.dgen] = self.es.enter_context(self.nc.semaphore(f"sd{self.dgen}"))
            self.dcnt[i] = 0
        self._deps(q, r, w)
        self.E[q].dma_start(out=out, in_=in_).then_inc(self.dsem[self.did[i]], 16)
        self.dcnt[i] += 1
        self._mark((("d", self.did[i]), 16 * self.dcnt[i]), r, w)

    def barrier(self, regs=()):
        for e in self.E:
            for k in self.E:
                if k != e and self.cnt[k] > 0:
                    self._wait(e, ((k, self.ep[k]), self.cnt[k]))
            for i in range(self.ndma):
                if self.dcnt[i]:
                    self._wait(e, (("d", self.did[i]), 16 * self.dcnt[i]))


def build(T=8192, DEPTH=2, DFF=16384, debug=(), phases=None):
    nc = bass.Bass("TRN2", target_bir_lowering=False)
    NT = T // 512
    NQ = T // 128
    L = DEPTH

    def din(name, shape, dt=F32):
        return nc.dram_tensor(name, list(shape), dt, kind="ExternalInput").ap()

    def dscr(name, shape, dt=F32):
        kind = "ExternalOutput" if name in debug else "Internal"
        return nc.dram_tensor(name, list(shape), dt, kind=kind).ap()

    xT_in = din("xT", [D, T])
    I = {}
    Lv = max(L - 1, 1)
    for nm, sh in [("norm_mix_g", [L, D]), ("w_in", [L, D, D_IN]), ("lam_q1", [L, 64]), ("lam_k1", [L, 64]),
                   ("lam_q2", [L, 64]), ("lam_k2", [L, 64]), ("diff_subln_g", [L, 128]),
                   ("rw_mu_rkv", [L, 6144]), ("rw_mu_wag", [L, 3, D]), ("rw_w0", [L, 2048]),
                   ("rw_w1", [L, D, 96]), ("rw_w2", [L, 96, 2048]), ("rw_a0", [L, 2048]),
                   ("rw_a1", [L, D, 96]), ("rw_a2", [L, 96, 2048]), ("rw_g1", [L, D, 256]),
                   ("rw_g2", [L, 256, 2048]), ("rw_k_k", [L, 2048]), ("rw_k_a", [L, 2048]),
                   ("rw_r_k", [L, 2048]), ("rw_ln_w", [L, 2048]), ("rw_ln_b", [L, 2048]),
                   ("rw_v_mu", [Lv, D]), ("rw_v0", [Lv, 2048]),
                   ("rw_v1", [Lv, D, 64]), ("rw_v2", [Lv, 64, 2048]),
                   ("w_out", [L, D, D]), ("norm_ffn_g", [L, D]), ("w_up", [L, D, DFF]),
                   ("w_down", [L, DFF, D]), ("norm_final_g", [1, D])]:
        I[nm] = din(nm, sh)
    outT = nc.dram_tensor("outT", [D, T], F32, kind="ExternalOutput").ap()

    xT = dscr("xT_s", [D, T])
    hT = dscr("hT_s", [D, T + 1], BF16)
    qaT = dscr("qaT", [2048, T], BF16)
    va = dscr("va", [T, 1024], BF16)
    rkv = dscr("rkv", [T + 1, 6144])
    qcT = dscr("qcT", [2048, T], BF16)
    vc = dscr("vc", [T, 1024], BF16)
    qiT = dscr("qiT", [1088, T], BF16)
    wi = dscr("wi", [T, 16])
    lo_w = dscr("lo_w", [97, T], BF16)
    lo_a = dscr("lo_a", [97, T], BF16)
    lo_g = dscr("lo_g", [256, T], BF16)
    lo_v = dscr("lo_v", [65, T], BF16)
    vfirst = dscr("vfirst", [T, 2048])
    mixT = dscr("mixT", [D, T], BF16)
    wob = dscr("wob", [D, D], BF16)
    wub = dscr("wub", [D, DFF], BF16)
    wdb = dscr("wdb", [DFF, D], BF16)

    es = contextlib.ExitStack()
    with es:
        es.enter_context(nc.allow_non_contiguous_dma(reason='small strided scratch DMAs'))
        S = Sched(nc, es)
        NEG_R = nc.gpsimd.to_reg(NEG)
        ZERO_R = nc.gpsimd.to_reg(0.0)
        uid = [0]

        def sb(st, name, shape, dt=F32):
            uid[0] += 1
            return Tl(st.enter_context(nc.sbuf_tensor(f"{name}_{uid[0]}", list(shape), dt)))

        banks = [Tl(es.enter_context(nc.psum_tensor(f"pb{i}", [128, 512], F32))) for i in range(7)]
        bank_bf = Tl(es.enter_context(nc.psum_tensor("pbb", [128, 1024], BF16)))

        ones_f = sb(es, "ones_f", [128, 512])
        ones_b = sb(es, "ones_b", [128, 128], BF16)
        ident_f = sb(es, "ident_f", [128, 128])
        ident_b = sb(es, "ident_b", [128, 128], BF16)
        eps_t = sb(es, "eps_t", [128, 1])
        S.op("pool", lambda e: e.memset(ones_f[:], 1.0), w=[ones_f])
        S.op("pool", lambda e: e.memset(ones_b[:], 1.0), w=[ones_b])
        S.op("pool", lambda e: e.memset(eps_t[:], 1e-6), w=[eps_t])
        S.op("pool", lambda e: e.affine_select(out=ident_f[:], in_=ones_f[:, 0:128], pattern=[[1, 128]],
                                               compare_op=ALU.is_equal, fill=ZERO_R, base=0,
                                               channel_multiplier=-1), r=[ones_f], w=[ident_f])
        S.op("pool", lambda e: e.tensor_copy(out=ident_b[:], in_=ident_f[:]), r=[ident_f], w=[ident_b])
        cmask = sb(es, "cmask", [128, 4, 512], BF16)
        for j in range(4):
            S.op("pool", lambda e, j=j: e.affine_select(out=cmask[:, j, :], in_=ones_f[:], pattern=[[1, 512]],
                                                        compare_op=ALU.is_ge, fill=ZERO_R, base=-128 * j,
                                                        channel_multiplier=-1), r=[ones_f], w=[cmask])
        cnt = [0]

        def evac(out_t, out_ap, in_t, in_ap, func=None, scale=None):
            cnt[0] += 1
            if func is not None:
                kw = {} if scale is None else {"scale": scale}
                S.op("act", lambda e: e.activation(out=out_ap, in_=in_ap, func=func, **kw), r=[in_t], w=[out_t])
            elif cnt[0] % 2:
                S.op("act", lambda e: e.copy(out=out_ap, in_=in_ap), r=[in_t], w=[out_t])
            else:
                S.op("dve", lambda e: e.tensor_copy(out=out_ap, in_=in_ap), r=[in_t], w=[out_t])

        def rmsnorm(st, src, g_t, dst, n, tag):
            sq = [sb(st, f"sq{tag}{i}", [128, n], BF16) for i in range(2)]
            rstd = sb(st, "rstd" + tag, [128, n])
            ps = banks[6]
            for kc in range(KC):
                q = sq[kc % 2]
                S.op("act", lambda e, kc=kc, q=q: e.activation(out=q[:], in_=src[:, kc, 0:n], func=AF.Square),
                     r=[src], w=[q])
                S.op("pe", lambda e, kc=kc, q=q: e.matmul(ps[:, 0:n], ones_b[:], q[:], start=(kc == 0),
                                                          stop=(kc == KC - 1)), r=[q, ones_b], w=[ps])
            S.op("act", lambda e: e.activation(out=rstd[:], in_=ps[:, 0:n], func=AF.Sqrt,
                                               bias=eps_t[:], scale=1.0 / D), r=[ps, eps_t], w=[rstd])
            S.op("dve", lambda e: e.reciprocal(out=rstd[:], in_=rstd[:]), r=[rstd], w=[rstd])
            for kc in range(KC):
                S.op("dve", lambda e, kc=kc: e.scalar_tensor_tensor(
                    out=dst[:, kc, 0:n], in0=src[:, kc, 0:n], scalar=g_t[:, kc:kc + 1], in1=rstd[:],
                    op0=ALU.mult, op1=ALU.mult), r=[src, rstd, g_t], w=[dst])

        def load_vec_fm(st, name, ap_row):
            t = sb(st, name, [128, KC])
            S.dma(t[:], ap_row.rearrange("(c p) -> p c", p=128), w=[t])
            return t

        def load_bc(st, name, ap_row2d, n, dt=F32):
            t = sb(st, name, [128, n], dt)
            S.dma(t[:], ap_row2d.to_broadcast([128, n]), w=[t], q="pool" if dt != F32 else None)
            return t

        hv = hT.rearrange("(c p) t -> p c t", p=128)
        xv = xT.rearrange("(c p) t -> p c t", p=128)

        with contextlib.ExitStack() as st:
            buf = [sb(st, f"cpx{i}", [128, 4096]) for i in range(2)]
            ones_row = sb(st, "ones_row", [1, T], BF16)
            zero_f = sb(st, "zero_f", [1, 6144])
            S.op("pool", lambda e: e.memset(ones_row[:], 1.0), w=[ones_row])
            S.op("pool", lambda e: e.memset(zero_f[:], 0.0), w=[zero_f])
            xv_in = xT_in.rearrange("(c p) t -> p c t", p=128)
            i = 0
            for c in range(KC):
                for t0 in range(0, T, 4096):
                    n = min(4096, T - t0)
                    b = buf[i % 2]
                    i += 1
                    S.dma(b[:, 0:n], xv_in[:, c, t0:t0 + n], w=[b])
                    S.dma(xv[:, c, t0:t0 + n], b[:, 0:n], r=[b])
            S.dma(rkv[0:1, :], zero_f[0:1, :], r=[zero_f])
            zb = sb(st, "zb", [128, KC], BF16)
            S.op("pool", lambda e: e.memset(zb[:], 0.0), w=[zb])
            S.dma(hv[:, :, 0:1], zb[:].rearrange("p (c o) -> p c o", o=1), r=[zb])
            S.dma(lo_w[96:97, :], ones_row[:], r=[ones_row])
            S.dma(lo_a[96:97, :], ones_row[:], r=[ones_row])
            S.dma(lo_v[64:65, :], ones_row[:], r=[ones_row])
            S.barrier()

        def phase_norm1(l):
            with contextlib.ExitStack() as st:
                g_t = load_vec_fm(st, "g_mix", I["norm_mix_g"][l])
                x_t = sb(st, "xs", [128, KC, 512])
                hs = [sb(st, f"hs{i}", [128, KC, 512], BF16) for i in range(2)]
                for tt in range(NT):
                    h_t = hs[tt % 2]
                    for c0 in range(0, KC, 8):
                        S.dma(x_t[:, c0:c0 + 8, :], xv[:, c0:c0 + 8, tt * 512:(tt + 1) * 512], w=[x_t])
                    with contextlib.ExitStack() as st2:
                        rmsnorm(st2, x_t, g_t, h_t, 512, "n1")
                        S.barrier()
                    for c0 in range(0, KC, 8):
                        S.dma(hv[:, c0:c0 + 8, 1 + tt * 512:1 + (tt + 1) * 512], h_t[:, c0:c0 + 8, :], r=[h_t])
                S.barrier()

        def load_wblock(stg, wb, src_ap, w):
            sv = src_ap.rearrange("(c p) n -> p c n", p=128)
            for i, c0 in enumerate(range(0, KC, 8)):
                s = stg[i % 2]
                S.dma(s[:, :, 0:w], sv[:, c0:c0 + 8, :], w=[s])
                eng = ("act", "pool")[i % 2]
                if eng == "act":
                    S.op("act", lambda e, s=s, c0=c0: e.copy(out=wb[:, c0:c0 + 8, 0:w], in_=s[:, :, 0:w]),
                         r=[s], w=[wb])
                else:
                    S.op("pool", lambda e, s=s, c0=c0: e.tensor_copy(out=wb[:, c0:c0 + 8, 0:w], in_=s[:, :, 0:w]),
                         r=[s], w=[wb])

        def phase_proj(l):
            blocks = []
            for c0 in range(0, 2048, 512):
                blocks.append(("fm", c0, 512, qaT, c0, BF16))
            for c0 in range(2048, 3072, 512):
                blocks.append(("tm", c0, 512, va, (0, c0 - 2048), BF16))
            for c0 in range(3072, 9216, 512):
                blocks.append(("tm", c0, 512, rkv, (1, c0 - 3072), F32))
            for c0 in range(9216, 11264, 512):
                blocks.append(("fm", c0, 512, qcT, c0 - 9216, BF16))
            for c0 in range(11264, 12288, 512):
                blocks.append(("tm", c0, 512, vc, (0, c0 - 11264), BF16))
            blocks.append(("fm", 12288, 512, qiT, 0, BF16))
            blocks.append(("fm", 12800, 512, qiT, 512, BF16))
            blocks.append(("fm", 13312, 64, qiT, 1024, BF16))
            blocks.append(("tm", 13376, 16, wi, (0, 0), F32))
            with contextlib.ExitStack() as st:
                stg = [sb(st, f"stg{i}", [128, 8, 512]) for i in range(2)]
                wbs = [sb(st, f"wb{i}", [128, KC, 512], BF16) for i in range(2)]
                hts = [sb(st, f"ht{i}", [128, KC, 512], BF16) for i in range(2)]
                obf = [sb(st, f"obf{i}", [128, 512]) for i in range(3)]
                obb = [sb(st, f"obb{i}", [128, 512], BF16) for i in range(3)]
                ib = 0
                io = 0
                ih = 0
                for bi, (lay, c0, w, dst, off, dt) in enumerate(blocks):
                    wb = wbs[bi % 2]
                    load_wblock(stg, wb, I["w_in"][l][:, c0:c0 + w], w)
                    for tt in range(NT):
                        ht = hts[ih % 2]
                        ih += 1
                        for k0 in range(0, KC, 8):
                            S.dma(ht[:, k0:k0 + 8, :], hv[:, k0:k0 + 8, 1 + tt * 512:1 + tt * 512 + 512], w=[ht])
                        if lay == "fm":
                            for m in range((w + 127) // 128):
                                mw = min(128, w - m * 128)
                                ps = banks[ib % 4]
                                ib += 1
                                for kc in range(KC):
                                    S.op("pe", lambda e, kc=kc, ps=ps, m=m, mw=mw, ht=ht: e.matmul(
                                        ps[0:mw, :], wb[:, kc, m * 128:m * 128 + mw], ht[:, kc, :],
                                        start=(kc == 0), stop=(kc == KC - 1)), r=[wb, ht], w=[ps])
                                ob = (obb if dt == BF16 else obf)[io % 3]
                                io += 1
                                evac(ob, ob[0:mw, :], ps, ps[0:mw, :])
                                S.dma(dst[off + m * 128:off + m * 128 + mw, tt * 512:(tt + 1) * 512], ob[0:mw, :],
                                      r=[ob])
                        else:
                            for ts in range(4):
                                ps = banks[ib % 4]
                                ib += 1
                                for kc in range(KC):
                                    S.op("pe", lambda e, kc=kc, ps=ps, ts=ts, ht=ht: e.matmul(
                                        ps[:, 0:w], ht[:, kc, ts * 128:(ts + 1) * 128], wb[:, kc, 0:w],
                                        start=(kc == 0), stop=(kc == KC - 1)), r=[wb, ht], w=[ps])
                                ob = (obb if dt == BF16 else obf)[io % 3]
                                io += 1
                                evac(ob, ob[:, 0:w], ps, ps[:, 0:w])
                                r0 = off[0] + tt * 512 + ts * 128
                                S.dma(dst[r0:r0 + 128, off[1]:off[1] + w], ob[:, 0:w], r=[ob])
                S.barrier()

        def phase_lora(l):
            with contextlib.ExitStack() as st:
                WL = sb(st, "WL", [128, KC, 512], BF16)
                WLm = sb(st, "WLm", [128, KC, 512], BF16)
                specs = [("rw_w1", l, 0, 96, I["rw_mu_wag"][l][0]), ("rw_a1", l, 96, 96, I["rw_mu_wag"][l][1]),
                         ("rw_g1", l, 192, 256, I["rw_mu_wag"][l][2])]
                if l > 0:
                    specs.append(("rw_v1", l - 1, 448, 64, I["rw_v_mu"][l - 1]))
                mus = [load_vec_fm(st, "mu" + sp[0], sp[4]) for sp in specs]
                st_w = contextlib.ExitStack()
                wf = sb(st_w, "WLf", [128, KC, 256])
                for (nm, li, cs, w, mu_ap), mu in zip(specs, mus):
                    S.dma(wf[:, :, 0:w], I[nm][li].rearrange("(c p) n -> p c n", p=128), w=[wf])
                    S.op("act", lambda e, cs=cs, w=w: e.copy(out=WL[:, :, cs:cs + w], in_=wf[:, :, 0:w]),
                         r=[wf], w=[WL])
                    for kc in range(KC):
                        S.op("dve", lambda e, kc=kc, cs=cs, w=w, mu=mu: e.tensor_scalar(
                            out=WLm[:, kc, cs:cs + w], in0=wf[:, kc, 0:w], scalar1=mu[:, kc:kc + 1],
                            scalar2=None, op0=ALU.mult), r=[wf, mu], w=[WLm])
                S.barrier()
                st_w.close()
                outs = [(0, 96, AF.Tanh, lo_w, 0), (96, 96, None, lo_a, 0), (192, 128, AF.Sigmoid, lo_g, 0),
                        (320, 128, AF.Sigmoid, lo_g, 128)]
                if l > 0:
                    outs.append((448, 64, None, lo_v, 0))
                hts = [sb(st, f"lht{i}", [128, KC, 512], BF16) for i in range(1)]
                hp_t = sb(st, "lhp", [128, KC, 512], BF16)
                dh = sb(st, "dh", [128, KC, 512], BF16)
                obb = [sb(st, f"lob{i}", [128, 512], BF16) for i in range(3)]
                io = 0
                for tt in range(NT):
                    ht = hts[0]
                    for k0 in range(0, KC, 8):
                        S.dma(ht[:, k0:k0 + 8, :], hv[:, k0:k0 + 8, 1 + tt * 512:1 + tt * 512 + 512], w=[ht])
                        S.dma(hp_t[:, k0:k0 + 8, :], hv[:, k0:k0 + 8, tt * 512:tt * 512 + 512], w=[hp_t])
                    S.op("dve", lambda e, ht=ht: e.tensor_tensor(out=dh[:], in0=hp_t[:], in1=ht[:],
                                                                 op=ALU.subtract), r=[ht, hp_t], w=[dh])
                    for oi, (cs, mw, fn, dst, ro) in enumerate(outs):
                        ps = banks[oi % 4]
                        for kc in range(KC):
                            S.op("pe", lambda e, kc=kc, ps=ps, cs=cs, mw=mw, ht=ht: e.matmul(
                                ps[0:mw, :], WL[:, kc, cs:cs + mw], ht[:, kc, :], start=(kc == 0), stop=False),
                                r=[WL, ht], w=[ps])
                        for kc in range(KC):
                            S.op("pe", lambda e, kc=kc, ps=ps, cs=cs, mw=mw: e.matmul(
                                ps[0:mw, :], WLm[:, kc, cs:cs + mw], dh[:, kc, :], start=False, stop=(kc == KC - 1)),
                                r=[WLm, dh], w=[ps])
                        ob = obb[io % 3]
                        io += 1
                        evac(ob, ob[0:mw, :], ps, ps[0:mw, :], func=fn)
                        S.dma(dst[ro:ro + mw, tt * 512:(tt + 1) * 512], ob[0:mw, :], r=[ob])
                S.barrier()
        def phase_attn(l):
            lam_init = 0.8 - 0.6 * math.exp(-0.3 * l)
            with contextlib.ExitStack() as st:
                lt = [load_bc(st, f"lam{i}", I[n][l:l + 1, :], 64) for i, n in
                      enumerate(("lam_q1", "lam_k1", "lam_q2", "lam_k2"))]
                t1 = sb(st, "lt1", [128, 64])
                ss = sb(st, "lss", [128, 2])
                nlam = sb(st, "nlam", [128, 1])
                for j in range(2):
                    S.op("dve", lambda e, j=j: e.tensor_tensor(out=t1[:], in0=lt[2 * j][:], in1=lt[2 * j + 1][:],
                                                               op=ALU.mult), r=[lt[2 * j], lt[2 * j + 1]], w=[t1])
                    S.op("dve", lambda e, j=j: e.tensor_reduce(out=ss[:, j:j + 1], in_=t1[:], axis=AX.X, op=ALU.add),
                         r=[t1], w=[ss])
                S.op("act", lambda e: e.activation(out=ss[:], in_=ss[:], func=AF.Exp), r=[ss], w=[ss])
                S.op("dve", lambda e: e.tensor_tensor(out=nlam[:], in0=ss[:, 1:2], in1=ss[:, 0:1], op=ALU.subtract),
                     r=[ss], w=[nlam])
                S.op("dve", lambda e: e.tensor_scalar(out=nlam[:], in0=nlam[:], scalar1=-lam_init, scalar2=None,
                                                      op0=ALU.add), r=[nlam], w=[nlam])
                gsub = sb(st, "gsub", [128, 1])
                S.dma(gsub[:], I["diff_subln_g"][l].rearrange("(p o) -> p o", o=1), w=[gsub])
                S.op("dve", lambda e: e.tensor_scalar(out=gsub[:], in0=gsub[:], scalar1=1.0 - lam_init, scalar2=None,
                                                      op0=ALU.mult), r=[gsub], w=[gsub])
                QT = [sb(st, f"aq{i}", [128, T], BF16) for i in range(2)]
                KT = [sb(st, f"ak{i}", [128, T], BF16) for i in range(2)]
                V = [sb(st, f"av{i}", [128, NQ, 128], BF16) for i in range(2)]
                P = [sb(st, f"ap{i}", [128, 512], BF16) for i in range(3)]
                rs = sb(st, "ars", [128, 512])
                om = [sb(st, f"aom{i}", [128, 512]) for i in range(2)]
                osq = sb(st, "aosq", [128, 512])
                ob = [sb(st, f"aob{i}", [128, 512], BF16) for i in range(2)]
                ip = 0
                for h in range(8):
                    q, k, v = QT[h % 2], KT[h % 2], V[h % 2]
                    S.dma(q[:], qaT[h * 128:(h + 1) * 128, :], w=[q])
                    S.dma(k[:], qaT[1024 + h * 128:1024 + (h + 1) * 128, :], w=[k])
                    S.dma(v[:], va[:, h * 128:(h + 1) * 128].rearrange("(k p) e -> p k e", p=128), w=[v])
                    for qt in range(NT):
                        nkt = 4 * (qt + 1)
                        for m in range(2):
                            O, SM = banks[2 + 2 * m], banks[3 + 2 * m]
                            pr = slice(64 * m, 64 * m + 64)
                            for kt in range(nkt):
                                ps = banks[kt % 2]
                                S.op("pe", lambda e, ps=ps, kt=kt, pr=pr: e.matmul(
                                    ps[:], k[pr, kt * 128:(kt + 1) * 128], q[pr, qt * 512:(qt + 1) * 512],
                                    start=True, stop=True), r=[k, q], w=[ps])
                                pb = P[ip % 3]
                                ip += 1
                                S.op("act", lambda e, ps=ps, pb=pb: e.activation(out=pb[:], in_=ps[:], func=AF.Exp,
                                                                                 scale=0.125), r=[ps], w=[pb])
                                if kt >= 4 * qt:
                                    S.op("dve", lambda e, pb=pb, kt=kt: e.tensor_tensor(
                                        out=pb[:], in0=pb[:], in1=cmask[:, kt - 4 * qt, :], op=ALU.mult),
                                        r=[pb, cmask], w=[pb])
                                S.op("pe", lambda e, pb=pb, kt=kt, O=O: e.matmul(
                                    O[:], v[:, kt, :], pb[:], start=(kt == 0), stop=(kt == nkt - 1)), r=[v, pb], w=[O])
                                S.op("pe", lambda e, pb=pb, kt=kt, SM=SM: e.matmul(
                                    SM[:], ones_b[:], pb[:], start=(kt == 0), stop=(kt == nkt - 1)),
                                    r=[ones_b, pb], w=[SM])
                            S.op("dve", lambda e, SM=SM: e.reciprocal(out=rs[:], in_=SM[:]), r=[SM], w=[rs])
                            S.op("dve", lambda e, O=O, m=m: e.tensor_tensor(out=om[m][:], in0=O[:], in1=rs[:],
                                                                            op=ALU.mult), r=[O, rs], w=[om[m]])
                        S.op("dve", lambda e: e.scalar_tensor_tensor(out=om[0][:], in0=om[1][:], scalar=nlam[:, 0:1],
                                                                     in1=om[0][:], op0=ALU.mult, op1=ALU.add),
                             r=[om[1], nlam, om[0]], w=[om[0]])
                        S.op("act", lambda e: e.activation(out=osq[:], in_=om[0][:], func=AF.Square), r=[om[0]], w=[osq])
                        pn = banks[6]
                        S.op("pe", lambda e: e.matmul(pn[:], ones_f[:, 0:128], osq[:], start=True, stop=True),
                             r=[ones_f, osq], w=[pn])
                        S.op("act", lambda e: e.activation(out=rs[:], in_=pn[:], func=AF.Sqrt, bias=eps_t[:],
                                                           scale=1.0 / 128), r=[pn, eps_t], w=[rs])
                        S.op("dve", lambda e: e.reciprocal(out=rs[:], in_=rs[:]), r=[rs], w=[rs])
                        o = ob[qt % 2]
                        S.op("dve", lambda e, o=o: e.scalar_tensor_tensor(out=o[:], in0=om[0][:], scalar=gsub[:, 0:1],
                                                                          in1=rs[:], op0=ALU.mult, op1=ALU.mult),
                             r=[om[0], gsub, rs], w=[o])
                        S.dma(mixT[h * 128:(h + 1) * 128, qt * 512:(qt + 1) * 512], o[:], r=[o])
                S.barrier()

        def phase_dsa(l):
            with contextlib.ExitStack() as st:
                kiT = sb(st, "kiT", [64, T], BF16)
                S.dma(kiT[:], qiT[1024:1088, :], w=[kiT])
                selT = sb(st, "selT", [128, NQ, 512], mybir.dt.uint8)
                sc = sb(st, "sc", [128, T])
                wk = sb(st, "wk", [128, T])
                sel = sb(st, "sel", [128, T], BF16)
                qi_r = sb(st, "qi_r", [64, 16, 128], BF16)
                wi_r = sb(st, "wi_r", [128, 16])
                wabs = sb(st, "wabs", [128, 16])
                wsgn = sb(st, "wsgn", [128, 16])
                tmp = [sb(st, f"dtmp{i}", [128, 512]) for i in range(2)]
                m8 = sb(st, "m8", [128, 8])
                QT = [sb(st, f"cq{i}", [128, 512], BF16) for i in range(2)]
                KT = [sb(st, f"ck{i}", [128, T], BF16) for i in range(1)]
                V = [sb(st, f"cv{i}", [128, NQ, 128], BF16) for i in range(1)]
                P = [sb(st, f"cp{i}", [128, 512], BF16) for i in range(3)]
                rs = sb(st, "crs", [128, 512])
                ob = [sb(st, f"cob{i}", [128, 512], BF16) for i in range(2)]
                ip = 0
                for qt in range(NT):
                    nkq = 4 * (qt + 1)
                    S.op("pool", lambda e: e.memset(selT[:, 4 * qt:4 * qt + 4, :], 0.0), w=[selT])
                    for r in range(4):
                        g = 4 * qt + r
                        nk = 128 * (g + 1)
                        S.dma(qi_r[:], qiT[0:1024, g * 128:(g + 1) * 128].rearrange("(h d) t -> d h t", d=64), w=[qi_r])
                        S.dma(wi_r[:], wi[g * 128:(g + 1) * 128, :], w=[wi_r])
                        S.op("act", lambda e: e.activation(out=wabs[:], in_=wi_r[:], func=AF.Abs), r=[wi_r], w=[wabs])
                        S.op("act", lambda e: e.activation(out=wsgn[:], in_=wi_r[:], func=AF.Sign), r=[wi_r], w=[wsgn])
                        for kb in range((nk + 511) // 512):
                            kw = min(512, nk - kb * 512)
                            ks = slice(kb * 512, kb * 512 + kw)
                            for ih in range(16):
                                ps = banks[ih % 2]
                                tm = tmp[ih % 2]
                                S.op("pe", lambda e, ps=ps, ih=ih, ks=ks, kw=kw: e.matmul(
                                    ps[:, 0:kw], qi_r[:, ih, :], kiT[:, ks], start=True, stop=True),
                                    r=[qi_r, kiT], w=[ps])
                                S.op("act", lambda e, ps=ps, tm=tm, ih=ih, kw=kw: e.activation(
                                    out=tm[:, 0:kw], in_=ps[:, 0:kw], func=AF.Relu, scale=wabs[:, ih:ih + 1]),
                                    r=[ps, wabs], w=[tm])
                                if ih == 0:
                                    S.op("dve", lambda e, tm=tm, ks=ks, kw=kw: e.tensor_scalar(
                                        out=sc[:, ks], in0=tm[:, 0:kw], scalar1=wsgn[:, 0:1], scalar2=None,
                                        op0=ALU.mult), r=[tm, wsgn], w=[sc])
                                else:
                                    S.op("dve", lambda e, tm=tm, ks=ks, kw=kw, ih=ih: e.scalar_tensor_tensor(
                                        out=sc[:, ks], in0=tm[:, 0:kw], scalar=wsgn[:, ih:ih + 1], in1=sc[:, ks],
                                        op0=ALU.mult, op1=ALU.add), r=[tm, wsgn, sc], w=[sc])
                        dg = slice(g * 128, (g + 1) * 128)
                        S.op("pool", lambda e, dg=dg: e.affine_select(
                            out=sc[:, dg], in_=sc[:, dg], pattern=[[-1, 128]], compare_op=ALU.is_ge, fill=NEG_R,
                            base=0, channel_multiplier=1), r=[sc], w=[sc])
                        if g >= 2:
                            S.op("dve", lambda e, nk=nk: e.max(out=m8[:], in_=sc[:, 0:nk]), r=[sc], w=[m8])
                            S.op("dve", lambda e, nk=nk: e.match_replace(out=wk[:, 0:nk], in_to_replace=m8[:],
                                                                         in_values=sc[:, 0:nk], imm_value=NEG),
                                 r=[sc, m8], w=[wk])
                            for it in range(1, 32):
                                S.op("dve", lambda e, nk=nk: e.max(out=m8[:], in_=wk[:, 0:nk]), r=[wk], w=[m8])
                                if it < 31:
                                    S.op("dve", lambda e, nk=nk: e.match_replace(
                                        out=wk[:, 0:nk], in_to_replace=m8[:], in_values=wk[:, 0:nk], imm_value=NEG),
                                        r=[wk, m8], w=[wk])
                            S.op("dve", lambda e, nk=nk: e.tensor_scalar(out=sel[:, 0:nk], in0=sc[:, 0:nk],
                                                                         scalar1=m8[:, 7:8], scalar2=None,
                                                                         op0=ALU.is_ge), r=[sc, m8], w=[sel])
                        else:
                            S.op("dve", lambda e, nk=nk: e.tensor_scalar(out=sel[:, 0:nk], in0=sc[:, 0:nk],
                                                                         scalar1=-1.0e38, scalar2=None,
                                                                         op0=ALU.is_ge), r=[sc], w=[sel])
                        for k0 in range(0, g + 1, 8):
                            nb = min(8, g + 1 - k0)
                            for j in range(nb):
                                S.op("pe", lambda e, j=j, k0=k0: e.transpose(
                                    bank_bf[:, j * 128:(j + 1) * 128], sel[:, (k0 + j) * 128:(k0 + j + 1) * 128],
                                    ident_b[:]), r=[sel, ident_b], w=[bank_bf])
                            evac(selT, selT[:, k0:k0 + nb, r * 128:(r + 1) * 128], bank_bf,
                                 bank_bf[:, 0:nb * 128].rearrange("p (k q) -> p k q", q=128))
                    for h in range(8):
                        q, k, v = QT[h % 2], KT[0], V[0]
                        S.dma(q[:], qcT[h * 128:(h + 1) * 128, qt * 512:(qt + 1) * 512], w=[q])
                        S.dma(k[:, 0:nkq * 128], qcT[1024 + h * 128:1024 + (h + 1) * 128, 0:nkq * 128], w=[k])
                        S.dma(v[:, 0:nkq, :], vc[0:nkq * 128, h * 128:(h + 1) * 128].rearrange("(k p) e -> p k e", p=128),
                              w=[v])
                        O, SM = banks[2], banks[3]
                        for kt in range(nkq):
                            ps = banks[kt % 2]
                            S.op("pe", lambda e, ps=ps, kt=kt: e.matmul(ps[:], k[:, kt * 128:(kt + 1) * 128], q[:],
                                                                        start=True, stop=True), r=[k, q], w=[ps])
                            pb = P[ip % 3]
                            ip += 1
                            S.op("act", lambda e, ps=ps, pb=pb: e.activation(out=pb[:], in_=ps[:], func=AF.Exp,
                                                                             scale=128 ** -0.5), r=[ps], w=[pb])
                            S.op("dve", lambda e, pb=pb, kt=kt: e.tensor_tensor(out=pb[:], in0=pb[:], in1=selT[:, kt, :],
                                                                                op=ALU.mult), r=[pb, selT], w=[pb])
                            S.op("pe", lambda e, pb=pb, kt=kt: e.matmul(O[:], v[:, kt, :], pb[:], start=(kt == 0),
                                                                        stop=(kt == nkq - 1)), r=[v, pb], w=[O])
                            S.op("pe", lambda e, pb=pb, kt=kt: e.matmul(SM[:], ones_b[:], pb[:], start=(kt == 0),
                                                                        stop=(kt == nkq - 1)), r=[ones_b, pb], w=[SM])
                        S.op("dve", lambda e: e.reciprocal(out=rs[:], in_=SM[:]), r=[SM], w=[rs])
                        o = ob[h % 2]
                        S.op("dve", lambda e, o=o: e.tensor_tensor(out=o[:], in0=O[:], in1=rs[:], op=ALU.mult),
                             r=[O, rs], w=[o])
                        S.dma(mixT[3072 + h * 128:3072 + (h + 1) * 128, qt * 512:(qt + 1) * 512], o[:], r=[o])
                S.barrier()
        def phase_rwkv(l):
            HC = 1024
            with contextlib.ExitStack() as st:
                def tri(name, pat, base, cm, val):
                    t = sb(st, name, [128, 128])
                    S.op("pool", lambda e: e.memset(t[:], val), w=[t])
                    S.op("pool", lambda e: e.affine_select(out=t[:], in_=t[:], pattern=pat, compare_op=ALU.is_ge,
                                                           fill=ZERO_R, base=base, channel_multiplier=cm), r=[t], w=[t])
                    return t
                triI = tri("triI", [[1, 128]], 0, -1, DEC)
                triS = tri("triS", [[1, 128]], -1, -1, DEC)
                triA = tri("triA", [[-1, 128]], -1, 1, DEC)
                negcol = sb(st, "negcol", [128, 1])
                S.op("pool", lambda e: e.memset(negcol[:], DEC), w=[negcol])
                epsg = sb(st, "epsg", [128, 1])
                S.op("pool", lambda e: e.memset(epsg[:], 64e-5), w=[epsg])

                def mask4(name, pat, base, cm):
                    t = sb(st, name, [128, 4, 128])
                    S.op("pool", lambda e: e.memset(t[:], 1.0), w=[t])
                    for j in range(4):
                        S.op("pool", lambda e, j=j: e.affine_select(out=t[:, j, :], in_=t[:, j, :], pattern=pat,
                                                                    compare_op=ALU.is_ge, fill=ZERO_R, base=base,
                                                                    channel_multiplier=cm), r=[t], w=[t])
                    return t
                msS = mask4("msS", [[1, 128]], -1, -1)
                msI = mask4("msI", [[1, 128]], 0, -1)
                msT = mask4("msT", [[-1, 128]], -1, 1)
                id4 = sb(st, "id4", [128, 4, 128])
                for j in range(4):
                    S.op("pool", lambda e, j=j: e.tensor_copy(out=id4[:, j, :], in_=ident_f[:]), r=[ident_f], w=[id4])
                def lw2(name, wnm, li, bnm, K):
                    t = sb(st, name, [K + 1, 2048], BF16)
                    S.dma(t[0:K, :], I[wnm][li], w=[t], q="pool")
                    if bnm is not None:
                        S.dma(t[K:K + 1, :], I[bnm][li:li + 1, :], w=[t], q="pool")
                    return t
                w2e = lw2("w2e", "rw_w2", l, "rw_w0", 96)
                a2e = lw2("a2e", "rw_a2", l, "rw_a0", 96)
                g2a = sb(st, "g2a", [128, 2048], BF16)
                g2b = sb(st, "g2b", [128, 2048], BF16)
                S.dma(g2a[0:128, :], I["rw_g2"][l][0:128, :], w=[g2a], q="pool")
                S.dma(g2b[:], I["rw_g2"][l][128:256, :], w=[g2b], q="pool")
                v2e = lw2("v2e", "rw_v2", l - 1, "rw_v0", 64) if l > 0 else None
                S0T = sb(st, "S0T", [128, 1024])
                S.op("pool", lambda e: e.memset(S0T[:], 0.0), w=[S0T])
                G = [sb(st, f"G{i}", [128, HC]) for i in range(12)]
                cur_r, cur_k, cur_v, P0, P1, P2, E0, E1, E2, E3, A0, G0 = G
                bc = [sb(st, f"bc{i}", [128, HC]) for i in range(2)]
                TT = [sb(st, f"TT{i}", [128, 8, 128]) for i in range(6)]
                BtT, KtT = TT[0], TT[1]
                AtM = (TT[2], TT[3])
                RtM = (TT[4], TT[5])
                for t_ in TT[2:]:
                    S.op("pool", lambda e, t_=t_: e.memset(t_[:], 0.0), w=[t_])
                M = [sb(st, f"M{i}", [128, 8, 128]) for i in range(9)]
                low = sb(st, "low", [97, 128], BF16)
                loa = sb(st, "loa", [97, 128], BF16)
                log = sb(st, "log", [128, 2, 128], BF16)
                lov = sb(st, "lov", [65, 128], BF16)
                sm = [sb(st, f"sm{i}", [128, 16]) for i in range(6)]
                PCf = sb(st, "PCf", [128, 8])
                GTs = sb(st, "GTs", [128, 512])
                UTs = sb(st, "UTs", [128, 512])
                obT = sb(st, "obT", [128, 8, 128], BF16)
                ibc = [0]
                ibk = [0]

                def getbc(ap_row2d):
                    t = bc[ibc[0] % 2]
                    ibc[0] += 1
                    S.dma(t[:], ap_row2d.to_broadcast([128, HC]), w=[t])
                    return t

                def nb():
                    ibk[0] += 1
                    return banks[ibk[0] % 6]

                def tt(o, a, b, op, eng="dve"):
                    S.op(eng, lambda e: e.tensor_tensor(out=o[:], in0=a[:], in1=b[:], op=op), r=[a, b], w=[o])

                def lora2(dst, lhs_list, func):
                    for cb in range(HC // 512):
                        ps = nb()
                        for i, (lh, K, rh, c_off) in enumerate(lhs_list):
                            S.op("pe", lambda e, ps=ps, lh=lh, K=K, rh=rh, c_off=c_off, cb=cb, i=i: e.matmul(
                                ps[:], lh, rh[0:K, c_off + cb * 512:c_off + (cb + 1) * 512], start=(i == 0),
                                stop=(i == len(lhs_list) - 1)), r=[low, loa, log, lov, rh], w=[ps])
                        evac(dst, dst[:, cb * 512:(cb + 1) * 512], ps, ps[:], func=func)

                for c in range(NQ):
                    t0 = c * 128
                    S.dma(low[:], lo_w[:, t0:t0 + 128], w=[low])
                    S.dma(loa[:], lo_a[:, t0:t0 + 128], w=[loa])
                    S.dma(log[:], lo_g[:, t0:t0 + 128].rearrange("(k p) t -> p k t", p=128), w=[log])
                    if l > 0:
                        S.dma(lov[:], lo_v[:, t0:t0 + 128], w=[lov])
                    for hf in range(2):
                        co = hf * HC
                        for qi_, (cu, pv) in enumerate(((cur_r, P0), (cur_k, P1), (cur_v, P2))):
                            cs = qi_ * 2048 + co
                            S.dma(cu[:], rkv[1 + t0:1 + t0 + 128, cs:cs + HC], w=[cu])
                            S.dma(pv[:], rkv[t0:t0 + 128, cs:cs + HC], w=[pv])
                            mu = getbc(I["rw_mu_rkv"][l:l + 1, cs:cs + HC])
                            tt(pv, pv, cu, ALU.subtract, "pool")
                            tt(pv, pv, mu, ALU.mult, "pool")
                            tt(cu, cu, pv, ALU.add, "pool")
                        lora2(P0, [(low[:], 97, w2e, co)], AF.Sigmoid)
                        lora2(A0, [(loa[:], 97, a2e, co)], AF.Sigmoid)
                        lora2(G0, [(log[:, 0, :], 128, g2a, co), (log[:, 1, :], 128, g2b, co)], None)
                        if l > 0:
                            lora2(P1, [(lov[:], 65, v2e, co)], AF.Sigmoid)
                            S.dma(P2[:], vfirst[t0:t0 + 128, co:co + HC], w=[P2])
                            tt(P2, P2, cur_v, ALU.subtract)
                            tt(P2, P2, P1, ALU.mult)
                            tt(cur_v, cur_v, P2, ALU.add)
                        else:
                            S.dma(vfirst[t0:t0 + 128, co:co + HC], cur_v[:], r=[cur_v])
                        _chk(1)
                        for (tri_t, outs) in ((triI, ((E0, 1.0), (E1, -1.0))), (triS, ((E2, 1.0),)), (triA, ((E3, 1.0),))):
                            for cb in range(HC // 512):
                                ps = nb()
                                S.op("pe", lambda e, ps=ps, tri_t=tri_t, cb=cb: e.matmul(
                                    ps[:], tri_t[:], P0[:, cb * 512:(cb + 1) * 512], start=True, stop=True),
                                    r=[tri_t, P0], w=[ps])
                                for (dst, scl) in outs:
                                    evac(dst, dst[:, cb * 512:(cb + 1) * 512], ps, ps[:], func=AF.Exp, scale=scl)
                        psf = nb()
                        for j in range(8):
                            S.op("pe", lambda e, j=j, psf=psf: e.matmul(psf[:, j:j + 1], P0[:, j * 128:(j + 1) * 128],
                                                                         negcol[:], start=True, stop=True),
                                 r=[P0, negcol], w=[psf])
                        evac(PCf, PCf[:], psf, psf[:, 0:8], func=AF.Exp)
                        _chk(2)
                        kkb = getbc(I["rw_k_k"][l:l + 1, co:co + HC])
                        tt(P0, cur_k, kkb, ALU.mult)
                        tt(P1, P0, P0, ALU.mult, "pool")
                        S.op("dve", lambda e: e.tensor_reduce(out=sm[0][:], in_=P1[:].rearrange("p (h d) -> p h d", d=64),
                                                              axis=AX.X, op=ALU.add), r=[P1], w=[sm[0]])
                        S.op("act", lambda e: e.activation(out=sm[0][:], in_=sm[0][:], func=AF.Sqrt), r=[sm[0]], w=[sm[0]])
                        S.op("dve", lambda e: e.tensor_scalar(out=sm[0][:], in0=sm[0][:], scalar1=1e-12, scalar2=None,
                                                              op0=ALU.max), r=[sm[0]], w=[sm[0]])
                        S.op("dve", lambda e: e.reciprocal(out=sm[0][:], in_=sm[0][:]), r=[sm[0]], w=[sm[0]])
                        for h in range(16):
                            S.op("dve", lambda e, h=h: e.tensor_scalar(out=P0[:, h * 64:(h + 1) * 64],
                                                                       in0=P0[:, h * 64:(h + 1) * 64],
                                                                       scalar1=sm[0][:, h:h + 1], scalar2=None,
                                                                       op0=ALU.mult), r=[P0, sm[0]], w=[P0])
                        kab = getbc(I["rw_k_a"][l:l + 1, co:co + HC])
                        S.op("dve", lambda e: e.scalar_tensor_tensor(out=P1[:], in0=A0[:], scalar=-1.0, in1=kab[:],
                                                                     op0=ALU.add, op1=ALU.mult), r=[A0, kab], w=[P1])
                        S.op("dve", lambda e: e.scalar_tensor_tensor(out=P1[:], in0=P1[:], scalar=1.0, in1=cur_k[:],
                                                                     op0=ALU.add, op1=ALU.mult), r=[P1, cur_k], w=[P1])
                        tt(P2, P0, A0, ALU.mult)
                        S.op("dve", lambda e: e.scalar_tensor_tensor(out=E2[:], in0=P0[:], scalar=-1.0, in1=E2[:],
                                                                     op0=ALU.mult, op1=ALU.mult), r=[P0, E2], w=[E2])
                        tt(A0, P2, E1, ALU.mult, "pool")
                        tt(E1, P1, E1, ALU.mult)
                        tt(E0, cur_r, E0, ALU.mult, "pool")
                        tt(P2, P2, E3, ALU.mult)
                        tt(E3, P1, E3, ALU.mult, "pool")
                        rkb = getbc(I["rw_r_k"][l:l + 1, co:co + HC])
                        tt(P0, cur_r, P1, ALU.mult)
                        tt(P0, P0, rkb, ALU.mult)
                        S.op("dve", lambda e: e.tensor_reduce(out=sm[1][:], in_=P0[:].rearrange("p (h d) -> p h d", d=64),
                                                              axis=AX.X, op=ALU.add), r=[P0], w=[sm[1]])
                        _chk(3)
                        for src, dstT in ((E2, AtM), (A0, BtT), (E1, KtT), (E0, RtM)):
                            for j0 in range(0, 8, 4):
                                ps = nb()
                                for j in range(4):
                                    S.op("pe", lambda e, ps=ps, j=j, j0=j0, src=src: e.matmul(
                                        ps[:, j * 128:(j + 1) * 128], src[:, (j0 + j) * 128:(j0 + j + 1) * 128],
                                        ident_f[:], start=True, stop=True), r=[src, ident_f], w=[ps])
                                if isinstance(dstT, tuple):
                                    for par in range(2):
                                        pr = slice(64 * par, 64 * par + 64)
                                        evac(dstT[par], dstT[par][pr, j0:j0 + 4, :], ps,
                                             ps[pr, :].rearrange("p (a b) -> p a b", b=128))
                                else:
                                    evac(dstT, dstT[:, j0:j0 + 4, :], ps, ps[:].rearrange("p (a b) -> p a b", b=128))
                        _chk(4)
                        yb = E0
                        for gi in range(2):
                            N_, NT_, Lak, Mbr, Mkr, An, ATn, Tm, Tmn = M

                            def cc(dst, lh, rh, msk):
                                for half in range(2):
                                    ps = nb()
                                    for hh in range(4):
                                        h = 8 * gi + 4 * half + hh
                                        lh_ = lh[h % 2] if isinstance(lh, tuple) else lh
                                        rh_ = rh[h % 2] if isinstance(rh, tuple) else rh
                                        S.op("pe", lambda e, ps=ps, hh=hh, h=h, lh_=lh_, rh_=rh_: e.matmul(
                                            ps[:, hh * 128:(hh + 1) * 128], lh_[:, h // 2, :], rh_[:, h // 2, :],
                                            start=True, stop=True), r=[lh_, rh_], w=[ps])
                                    S.op("dve", lambda e, ps=ps, half=half: e.tensor_tensor(
                                        out=dst[:, 4 * half:4 * half + 4, :],
                                        in0=ps[:].rearrange("p (a b) -> p a b", b=128), in1=msk[:], op=ALU.mult),
                                        r=[ps, msk], w=[dst])
                            cc(N_, BtT, AtM, msS)
                            cc(NT_, AtM, BtT, msT)
                            cc(Lak, KtT, AtM, msS)
                            cc(Mbr, BtT, RtM, msI)
                            cc(Mkr, KtT, RtM, msI)
                            _chk(5)
                            for half in range(2):
                                S.op("dve", lambda e, half=half: e.tensor_tensor(
                                    out=Tm[:, 4 * half:4 * half + 4, :], in0=N_[:, 4 * half:4 * half + 4, :],
                                    in1=id4[:], op=ALU.add), r=[N_, id4], w=[Tm])
                            A, AT = N_, NT_

                            def mm8(dst, lh, rh, add=None):
                                for half in range(2):
                                    ps = nb()
                                    for hh in range(4):
                                        j = 4 * half + hh
                                        S.op("pe", lambda e, ps=ps, hh=hh, j=j: e.matmul(
                                            ps[:, hh * 128:(hh + 1) * 128], lh[:, j, :], rh[:, j, :], start=True,
                                            stop=True), r=[lh, rh], w=[ps])
                                    pv3 = ps[:].rearrange("p (a b) -> p a b", b=128)
                                    if add is None:
                                        evac(dst, dst[:, 4 * half:4 * half + 4, :], ps, pv3)
                                    else:
                                        S.op("dve", lambda e, half=half, pv3=pv3, ps=ps: e.tensor_tensor(
                                            out=dst[:, 4 * half:4 * half + 4, :], in0=pv3,
                                            in1=add[:, 4 * half:4 * half + 4, :], op=ALU.add), r=[ps, add], w=[dst])
                            for j in range(1, 7):
                                if j < 6:
                                    mm8(An, AT, A)
                                mm8(ATn, A, AT)
                                A, An = An, A
                                AT, ATn = ATn, AT
                                mm8(Tmn, AT, Tm, add=Tm)
                                Tm, Tmn = Tmn, Tm
                            _chk(6)
                            psg = nb()
                            for hh in range(8):
                                h = 8 * gi + hh
                                pr = slice(64 * (h % 2), 64 * (h % 2) + 64)
                                hp = h // 2
                                gp = hf * 8 + hp
                                S.op("pe", lambda e, hh=hh, h=h, hp=hp, gp=gp: e.matmul(
                                    psg[:, hh * 64:(hh + 1) * 64], AtM[h % 2][:, hp, :], S0T[:, gp * 64:(gp + 1) * 64],
                                    start=True, stop=False), r=[AtM[h % 2], S0T], w=[psg])
                                S.op("pe", lambda e, hh=hh, h=h: e.matmul(
                                    psg[:, hh * 64:(hh + 1) * 64], Lak[:, hh, :], cur_v[:, h * 64:(h + 1) * 64],
                                    start=False, stop=True), r=[Lak, cur_v], w=[psg])
                            evac(GTs, GTs[:], psg, psg[:])
                            psu = nb()
                            for hh in range(8):
                                S.op("pe", lambda e, hh=hh: e.matmul(psu[:, hh * 64:(hh + 1) * 64], Tm[:, hh, :],
                                                                     GTs[:, hh * 64:(hh + 1) * 64], start=True, stop=True),
                                     r=[Tm, GTs], w=[psu])
                            evac(UTs, UTs[:], psu, psu[:])
                            psy = nb()
                            for hh in range(8):
                                h = 8 * gi + hh
                                pr = slice(64 * (h % 2), 64 * (h % 2) + 64)
                                hp = h // 2
                                gp = hf * 8 + hp
                                ys = psy[:, hh * 64:(hh + 1) * 64]
                                S.op("pe", lambda e, ys=ys, h=h, hp=hp, gp=gp: e.matmul(
                                    ys, RtM[h % 2][:, hp, :], S0T[:, gp * 64:(gp + 1) * 64], start=True, stop=False),
                                    r=[RtM[h % 2], S0T], w=[psy])
                                S.op("pe", lambda e, ys=ys, hh=hh: e.matmul(ys, Mbr[:, hh, :], UTs[:, hh * 64:(hh + 1) * 64],
                                                                            start=False, stop=False), r=[Mbr, UTs], w=[psy])
                                S.op("pe", lambda e, ys=ys, hh=hh, h=h: e.matmul(ys, Mkr[:, hh, :],
                                                                                 cur_v[:, h * 64:(h + 1) * 64],
                                                                                 start=False, stop=True),
                                     r=[Mkr, cur_v], w=[psy])
                            evac(yb, yb[:, gi * 512:(gi + 1) * 512], psy, psy[:])
                            pss = nb()
                            for pi in range(4):
                                hp = 4 * gi + pi
                                so = pss[:, pi * 128:(pi + 1) * 128]
                                S.op("pe", lambda e, so=so, hp=hp, pi=pi: e.matmul(
                                    so, P2[:, hp * 128:(hp + 1) * 128], UTs[:, pi * 128:(pi + 1) * 128], start=True,
                                    stop=False), r=[P2, UTs], w=[pss])
                                S.op("pe", lambda e, so=so, hp=hp: e.matmul(
                                    so, E3[:, hp * 128:(hp + 1) * 128], cur_v[:, hp * 128:(hp + 1) * 128], start=False,
                                    stop=True), r=[E3, cur_v], w=[pss])
                            for pi in range(4):
                                hp = 4 * gi + pi
                                gp = hf * 8 + hp
                                for par in range(2):
                                    pr = slice(64 * par, 64 * par + 64)
                                    S.op("dve", lambda e, pr=pr, par=par, pi=pi, hp=hp, gp=gp: e.scalar_tensor_tensor(
                                        out=S0T[pr, gp * 64:(gp + 1) * 64], in0=S0T[pr, gp * 64:(gp + 1) * 64],
                                        scalar=PCf[pr, hp:hp + 1],
                                        in1=pss[pr, pi * 128 + par * 64:pi * 128 + par * 64 + 64],
                                        op0=ALU.mult, op1=ALU.add), r=[S0T, PCf, pss], w=[S0T])
                        _chk(7)
                        y3 = yb[:].rearrange("p (h d) -> p h d", d=64)
                        S.op("dve", lambda e: e.tensor_reduce(out=sm[2][:], in_=y3, axis=AX.X, op=ALU.add), r=[yb], w=[sm[2]])
                        tt(P0, yb, yb, ALU.mult, "pool")
                        S.op("dve", lambda e: e.tensor_reduce(out=sm[3][:], in_=P0[:].rearrange("p (h d) -> p h d", d=64),
                                                              axis=AX.X, op=ALU.add), r=[P0], w=[sm[3]])
                        S.op("dve", lambda e: e.tensor_scalar(out=sm[2][:], in0=sm[2][:], scalar1=1.0 / 64, scalar2=None,
                                                              op0=ALU.mult), r=[sm[2]], w=[sm[2]])
                        tt(sm[4], sm[2], sm[2], ALU.mult)
                        S.op("dve", lambda e: e.scalar_tensor_tensor(out=sm[3][:], in0=sm[3][:], scalar=1.0 / 64,
                                                                     in1=sm[4][:], op0=ALU.mult, op1=ALU.subtract),
                             r=[sm[3], sm[4]], w=[sm[3]])
                        S.op("act", lambda e: e.activation(out=sm[3][:], in_=sm[3][:], func=AF.Sqrt, bias=epsg[:]),
                             r=[sm[3], epsg], w=[sm[3]])
                        S.op("dve", lambda e: e.reciprocal(out=sm[3][:], in_=sm[3][:]), r=[sm[3]], w=[sm[3]])
                        for h in range(16):
                            hs = slice(h * 64, (h + 1) * 64)
                            S.op("dve", lambda e, h=h, hs=hs: e.tensor_scalar(
                                out=yb[:, hs], in0=yb[:, hs], scalar1=sm[2][:, h:h + 1], scalar2=sm[3][:, h:h + 1],
                                op0=ALU.subtract, op1=ALU.mult), r=[yb, sm[2], sm[3]], w=[yb])
                        lwb = getbc(I["rw_ln_w"][l:l + 1, co:co + HC])
                        tt(yb, yb, lwb, ALU.mult)
                        lbb = getbc(I["rw_ln_b"][l:l + 1, co:co + HC])
                        tt(yb, yb, lbb, ALU.add)
                        for h in range(16):
                            hs = slice(h * 64, (h + 1) * 64)
                            S.op("dve", lambda e, h=h, hs=hs: e.scalar_tensor_tensor(
                                out=yb[:, hs], in0=cur_v[:, hs], scalar=sm[1][:, h:h + 1], in1=yb[:, hs],
                                op0=ALU.mult, op1=ALU.add), r=[cur_v, sm[1], yb], w=[yb])
                        tt(yb, yb, G0, ALU.mult)
                        for j0 in range(0, 8, 4):
                            ps = nb()
                            for j in range(4):
                                S.op("pe", lambda e, ps=ps, j=j, j0=j0: e.matmul(
                                    ps[:, j * 128:(j + 1) * 128], yb[:, (j0 + j) * 128:(j0 + j + 1) * 128], ident_f[:],
                                    start=True, stop=True), r=[yb, ident_f], w=[ps])
                            evac(obT, obT[:, j0:j0 + 4, :], ps, ps[:].rearrange("p (a b) -> p a b", b=128))
                        S.dma(mixT[1024 + co:1024 + co + HC, t0:t0 + 128].rearrange("(j p) t -> p j t", p=128),
                              obT[:], r=[obT])
                S.barrier()
        def cast_to_dram(src2d, dst2d, rows, cols):
            with contextlib.ExitStack() as st:
                fb = [sb(st, f"cf{i}", [128, 2048]) for i in range(3)]
                bb = [sb(st, f"cb{i}", [128, 2048], BF16) for i in range(3)]
                i = 0
                for r0 in range(0, rows, 128):
                    for c0 in range(0, cols, 2048):
                        w = min(2048, cols - c0)
                        f, b = fb[i % 3], bb[i % 3]
                        S.dma(f[:, 0:w], src2d[r0:r0 + 128, c0:c0 + w], w=[f])
                        eng = ("act", "pool", "dve")[i % 3]
                        if eng == "act":
                            S.op("act", lambda e, f=f, b=b, w=w: e.copy(out=b[:, 0:w], in_=f[:, 0:w]), r=[f], w=[b])
                        else:
                            S.op(eng, lambda e, f=f, b=b, w=w: e.tensor_copy(out=b[:, 0:w], in_=f[:, 0:w]), r=[f], w=[b])
                        S.dma(dst2d[r0:r0 + 128, c0:c0 + w], b[:, 0:w], r=[b])
                        i += 1
                S.barrier()

        def phase_ffn(l):
            cast_to_dram(I["w_out"][l], wob, D, D)
            cast_to_dram(I["w_up"][l], wub, D, DFF)
            cast_to_dram(I["w_down"][l], wdb, DFF, D)
            last = (l == L - 1)
            wov = wob.rearrange("(c p) n -> p c n", p=128)
            wuv = wub.rearrange("(c p) n -> p c n", p=128)
            wdv = wdb.rearrange("(c p) n -> p c n", p=128)
            mv = mixT.rearrange("(c p) t -> p c t", p=128)
            ov = outT.rearrange("(c p) t -> p c t", p=128)
            with contextlib.ExitStack() as st:
                g_t = load_vec_fm(st, "g_ffn", I["norm_ffn_g"][l])
                gf_t = load_vec_fm(st, "g_fin", I["norm_final_g"][0]) if last else None
                acc = sb(st, "acc", [128, KC, 512])
                hb = sb(st, "hb", [128, KC, 512], BF16)
                wbs = [sb(st, f"fw{i}", [128, KC, 512], BF16) for i in range(2)]
                act_t = [sb(st, f"fa{i}", [128, 4, 512], BF16) for i in range(2)]
                iw = 0
                ib = 0
                for tt in range(NT):
                    ts = slice(tt * 512, (tt + 1) * 512)
                    for c0 in range(0, KC, 8):
                        S.dma(acc[:, c0:c0 + 8, :], xv[:, c0:c0 + 8, ts], w=[acc])
                        S.dma(hb[:, c0:c0 + 8, :], mv[:, c0:c0 + 8, ts], w=[hb])
                    for nb in range(D // 512):
                        wb = wbs[iw % 2]
                        iw += 1
                        for c0 in range(0, KC, 8):
                            S.dma(wb[:, c0:c0 + 8, :], wov[:, c0:c0 + 8, nb * 512:(nb + 1) * 512], w=[wb])
                        for m in range(4):
                            ps = banks[ib % 4]
                            ib += 1
                            for kc in range(KC):
                                S.op("pe", lambda e, ps=ps, kc=kc, m=m, wb=wb: e.matmul(
                                    ps[:], wb[:, kc, m * 128:(m + 1) * 128], hb[:, kc, :], start=(kc == 0),
                                    stop=(kc == KC - 1)), r=[wb, hb], w=[ps])
                            mc = nb * 4 + m
                            S.op("dve", lambda e, ps=ps, mc=mc: e.tensor_tensor(out=acc[:, mc, :], in0=ps[:],
                                                                                in1=acc[:, mc, :], op=ALU.add),
                                 r=[ps, acc], w=[acc])
                    with contextlib.ExitStack() as st2:
                        S.barrier()
                        rmsnorm(st2, acc, g_t, hb, 512, "n2")
                        S.barrier()
                    for fb in range(DFF // 512):
                        wu = wbs[iw % 2]
                        iw += 1
                        for c0 in range(0, KC, 8):
                            S.dma(wu[:, c0:c0 + 8, :], wuv[:, c0:c0 + 8, fb * 512:(fb + 1) * 512], w=[wu])
                        a_t = act_t[fb % 2]
                        for m in range(4):
                            ps = banks[ib % 4]
                            ib += 1
                            for kc in range(KC):
                                S.op("pe", lambda e, ps=ps, kc=kc, m=m, wu=wu: e.matmul(
                                    ps[:], wu[:, kc, m * 128:(m + 1) * 128], hb[:, kc, :], start=(kc == 0),
                                    stop=(kc == KC - 1)), r=[wu, hb], w=[ps])
                            S.op("act", lambda e, ps=ps, m=m, a_t=a_t: e.activation(out=a_t[:, m, :], in_=ps[:],
                                                                                   func=AF.Relu), r=[ps], w=[a_t])
                        S.op("pool", lambda e, a_t=a_t: e.tensor_tensor(out=a_t[:], in0=a_t[:], in1=a_t[:],
                                                                        op=ALU.mult), r=[a_t], w=[a_t])
                        wd = wbs[iw % 2]
                        iw += 1
                        wd4 = wd[:].rearrange("p (a b) n -> p a (b n)", a=4)
                        S.dma(wd4, wdv[:, fb * 4:(fb + 1) * 4, :], w=[wd])
                        for mc in range(KC):
                            ps = banks[ib % 4]
                            ib += 1
                            for kc in range(4):
                                S.op("pe", lambda e, ps=ps, kc=kc, mc=mc, wd4=wd4, a_t=a_t: e.matmul(
                                    ps[:], wd4[:, kc, mc * 128:(mc + 1) * 128], a_t[:, kc, :], start=(kc == 0),
                                    stop=(kc == 3)), r=[wd, a_t], w=[ps])
                            S.op("dve", lambda e, ps=ps, mc=mc: e.tensor_tensor(out=acc[:, mc, :], in0=ps[:],
                                                                                in1=acc[:, mc, :], op=ALU.add),
                                 r=[ps, acc], w=[acc])
                    if last:
                        fo = sb(st, "fo", [128, 8, 512]) if tt == 0 else fo
                        with contextlib.ExitStack() as st2:
                            S.barrier()
                            sq = [sb(st2, f"sqf{i}", [128, 512], BF16) for i in range(2)]
                            rstd = sb(st2, "rstdf", [128, 512])
                            ps = banks[6]
                            for kc in range(KC):
                                q = sq[kc % 2]
                                S.op("act", lambda e, kc=kc, q=q: e.activation(out=q[:], in_=acc[:, kc, :],
                                                                               func=AF.Square), r=[acc], w=[q])
                                S.op("pe", lambda e, kc=kc, q=q: e.matmul(ps[:], ones_b[:], q[:], start=(kc == 0),
                                                                          stop=(kc == KC - 1)), r=[q, ones_b], w=[ps])
                            S.op("act", lambda e: e.activation(out=rstd[:], in_=ps[:], func=AF.Sqrt, bias=eps_t[:],
                                                               scale=1.0 / D), r=[ps, eps_t], w=[rstd])
                            S.op("dve", lambda e: e.reciprocal(out=rstd[:], in_=rstd[:]), r=[rstd], w=[rstd])
                            for c0 in range(0, KC, 8):
                                for kc in range(c0, c0 + 8):
                                    S.op("dve", lambda e, kc=kc, c0=c0: e.scalar_tensor_tensor(
                                        out=fo[:, kc - c0, :], in0=acc[:, kc, :], scalar=gf_t[:, kc:kc + 1],
                                        in1=rstd[:], op0=ALU.mult, op1=ALU.mult), r=[acc, rstd, gf_t], w=[fo])
                                S.dma(ov[:, c0:c0 + 8, ts], fo[:], r=[fo])
                            S.barrier()
                    else:
                        for c0 in range(0, KC, 8):
                            S.dma(xv[:, c0:c0 + 8, ts], acc[:, c0:c0 + 8, :], r=[acc])
                S.barrier()

        for l in range(L):
            for nm, fn in (("norm1", phase_norm1), ("proj", phase_proj), ("lora", phase_lora), ("attn", phase_attn),
                           ("dsa", phase_dsa), ("rwkv", phase_rwkv), ("ffn", phase_ffn)):
                if phases is None or nm in phases:
                    _DEAD[0] = False
                    fn(l)
                    _DEAD[0] = False
        S.barrier()
    return nc


def kernel(**inputs):
    T = inputs["x"].shape[1]
    nc = build(T=T, DEPTH=inputs["w_in"].shape[0], DFF=inputs["w_up"].shape[2])
    m = {"xT": np.ascontiguousarray(inputs["x"][0].T)}
    for k, v in inputs.items():
        if k == "x":
            continue
        a = np.asarray(v, dtype=np.float32)
        if k == "rw_mu_rkv":
            a = a.reshape(a.shape[0], -1)
        elif k == "rw_r_k":
            a = a.reshape(a.shape[0], -1)
        elif k == "norm_final_g":
            a = a.reshape(1, -1)
        m[k] = np.ascontiguousarray(a)
    res = run_bass_kernel_spmd(nc, [m], core_ids=[0])
    out = res.results[0]["outT"]
    return np.ascontiguousarray(out.T)[None].astype(np.float32)
```

```python
import contextlib
import math
import os
import numpy as np
import concourse.bass as bass
import concourse.mybir as mybir
from concourse.bass_utils import run_bass_kernel_spmd

F32 = mybir.dt.float32
BF16 = mybir.dt.bfloat16
AF = mybir.ActivationFunctionType
ALU = mybir.AluOpType
AX = mybir.AxisListType

D = 4096
KC = D // 128
D_IN = 13392
NEG = -3.0e38
DEC = -0.6065306597126334


class _Stop(Exception):
    pass


RW_STOP = int(os.environ.get('RW_STOP', '0'))


_DEAD = [False]


def _chk(k):
    if RW_STOP == k:
        _DEAD[0] = True


class Reg:
    __slots__ = ("w", "rs")

    def __init__(self):
        self.w = None
        self.rs = {}


class Tl:
    def __init__(self, t):
        self.t = t
        self.r = Reg()

    def __getitem__(self, k):
        return self.t[k]


class Sched:
    EPOCH = 16000

    def __init__(self, nc, es, ndma=16):
        self.nc = nc
        self.es = es
        self.E = {"pe": nc.tensor, "act": nc.scalar, "dve": nc.vector, "pool": nc.gpsimd, "sp": nc.sync}
        self.sems = {}
        self.ep = {k: 0 for k in self.E}
        self.cnt = {k: 0 for k in self.E}
        for k in self.E:
            self.sems[(k, 0)] = es.enter_context(nc.semaphore(f"s_{k}_0"))
        self.dsem = {i: es.enter_context(nc.semaphore(f"sd{i}")) for i in range(ndma)}
        self.did = list(range(ndma))
        self.dgen = ndma - 1
        self.ndma = ndma
        self.dcnt = [0] * ndma
        self.dnext = 0
        self.seen = {k: {} for k in self.E}
        self.rr = 0

    def _wait(self, eng, tok):
        if tok is None:
            return
        key, val = tok
        if key[0] == eng and eng == "pe":
            return
        sn = self.seen[eng]
        if key[0] == "d":
            if sn.get(key, 0) >= val:
                return
            self.E[eng].wait_ge(self.dsem[key[1]], val)
            sn[key] = val
            return
        cur = sn.get(key[0], (-1, 0))
        if cur >= (key[1], val):
            return
        self.E[eng].wait_ge(self.sems[key], val)
        sn[key[0]] = (key[1], val)

    def _deps(self, eng, r, w):
        for x in r:
            self._wait(eng, x.r.w)
        for x in w:
            self._wait(eng, x.r.w)
            for t in x.r.rs.values():
                self._wait(eng, t)

    def _mark(self, tok, r, w):
        for x in r:
            x.r.rs[tok[0][0] if tok[0][0] != "d" else tok[0]] = tok
        for x in w:
            x.r.w = tok
            x.r.rs = {}

    def op(self, eng, fn, r=(), w=()):
        if _DEAD[0]:
            return
        self._deps(eng, r, w)
        if self.cnt[eng] >= self.EPOCH:
            self.ep[eng] += 1
            self.cnt[eng] = 0
            self.sems[(eng, self.ep[eng])] = self.es.enter_context(
                self.nc.semaphore(f"s_{eng}_{self.ep[eng]}"))
        inst = fn(self.E[eng])
        self.cnt[eng] += 1
        key = (eng, self.ep[eng])
        inst.then_inc(self.sems[key], 1)
        self._mark((key, self.cnt[eng]), r, w)

    def dma(self, out, in_, r=(), w=(), q=None):
        if _DEAD[0]:
            return
        if q is None:
            q = ("sp", "act")[self.rr % 2]
            self.rr += 1
        i = self.dnext
        self.dnext = (self.dnext + 1) % self.ndma
        self._wait(q, (("d", self.did[i]), 16 * self.dcnt[i]))
        if self.dcnt[i] >= 1000:
            self.dgen += 1
            self.did[i] = self.dgen
            self.dsem[self.dgen] = self.es.enter_context(self.nc.semaphore(f"sd{self.dgen}"))
            self.dcnt[i] = 0
        self._deps(q, r, w)
        self.E[q].dma_start(out=out, in_=in_).then_inc(self.dsem[self.did[i]], 16)
        self.dcnt[i] += 1
        self._mark((("d", self.did[i]), 16 * self.dcnt[i]), r, w)

    def barrier(self, regs=()):
        for e in self.E:
            for k in self.E:
                if k != e and self.cnt[k] > 0:
                    self._wait(e, ((k, self.ep[k]), self.cnt[k]))
            for i in range(self.ndma):
                if self.dcnt[i]:
                    self._wait(e, (("d", self.did[i]), 16 * self.dcnt[i]))


def build(T=8192, DEPTH=2, DFF=16384, debug=(), phases=None):
    nc = bass.Bass("TRN2", target_bir_lowering=False)
    NT = T // 512
    NQ = T // 128
    L = DEPTH

    def din(name, shape, dt=F32):
        return nc.dram_tensor(name, list(shape), dt, kind="ExternalInput").ap()

    def dscr(name, shape, dt=F32):
        kind = "ExternalOutput" if name in debug else "Internal"
        return nc.dram_tensor(name, list(shape), dt, kind=kind).ap()

    xT_in = din("xT", [D, T])
    I = {}
    Lv = max(L - 1, 1)
    for nm, sh in [("norm_mix_g", [L, D]), ("w_in", [L, D, D_IN]), ("lam_q1", [L, 64]), ("lam_k1", [L, 64]),
                   ("lam_q2", [L, 64]), ("lam_k2", [L, 64]), ("diff_subln_g", [L, 128]),
                   ("rw_mu_rkv", [L, 6144]), ("rw_mu_wag", [L, 3, D]), ("rw_w0", [L, 2048]),
                   ("rw_w1", [L, D, 96]), ("rw_w2", [L, 96, 2048]), ("rw_a0", [L, 2048]),
                   ("rw_a1", [L, D, 96]), ("rw_a2", [L, 96, 2048]), ("rw_g1", [L, D, 256]),
                   ("rw_g2", [L, 256, 2048]), ("rw_k_k", [L, 2048]), ("rw_k_a", [L, 2048]),
                   ("rw_r_k", [L, 2048]), ("rw_ln_w", [L, 2048]), ("rw_ln_b", [L, 2048]),
                   ("rw_v_mu", [Lv, D]), ("rw_v0", [Lv, 2048]),
                   ("rw_v1", [Lv, D, 64]), ("rw_v2", [Lv, 64, 2048]),
                   ("w_out", [L, D, D]), ("norm_ffn_g", [L, D]), ("w_up", [L, D, DFF]),
                   ("w_down", [L, DFF, D]), ("norm_final_g", [1, D])]:
        I[nm] = din(nm, sh)
    outT = nc.dram_tensor("outT", [D, T], F32, kind="ExternalOutput").ap()

    xT = dscr("xT_s", [D, T])
    hT = dscr("hT_s", [D, T + 1], BF16)
    qaT = dscr("qaT", [2048, T], BF16)
    va = dscr("va", [T, 1024], BF16)
    rkv = dscr("rkv", [T + 1, 6144])
    qcT = dscr("qcT", [2048, T], BF16)
    vc = dscr("vc", [T, 1024], BF16)
    qiT = dscr("qiT", [1088, T], BF16)
    wi = dscr("wi", [T, 16])
    lo_w = dscr("lo_w", [97, T], BF16)
    lo_a = dscr("lo_a", [97, T], BF16)
    lo_g = dscr("lo_g", [256, T], BF16)
    lo_v = dscr("lo_v", [65, T], BF16)
    vfirst = dscr("vfirst", [T, 2048])
    mixT = dscr("mixT", [D, T], BF16)
    wob = dscr("wob", [D, D], BF16)
    wub = dscr("wub", [D, DFF], BF16)
    wdb = dscr("wdb", [DFF, D], BF16)

    es = contextlib.ExitStack()
    with es:
        es.enter_context(nc.allow_non_contiguous_dma(reason='small strided scratch DMAs'))
        S = Sched(nc, es)
        NEG_R = nc.gpsimd.to_reg(NEG)
        ZERO_R = nc.gpsimd.to_reg(0.0)
        uid = [0]

        def sb(st, name, shape, dt=F32):
            uid[0] += 1
            return Tl(st.enter_context(nc.sbuf_tensor(f"{name}_{uid[0]}", list(shape), dt)))

        banks = [Tl(es.enter_context(nc.psum_tensor(f"pb{i}", [128, 512], F32))) for i in range(7)]
        bank_bf = Tl(es.enter_context(nc.psum_tensor("pbb", [128, 1024], BF16)))

        ones_f = sb(es, "ones_f", [128, 512])
        ones_b = sb(es, "ones_b", [128, 128], BF16)
        ident_f = sb(es, "ident_f", [128, 128])
        ident_b = sb(es, "ident_b", [128, 128], BF16)
        eps_t = sb(es, "eps_t", [128, 1])
        S.op("pool", lambda e: e.memset(ones_f[:], 1.0), w=[ones_f])
        S.op("pool", lambda e: e.memset(ones_b[:], 1.0), w=[ones_b])
        S.op("pool", lambda e: e.memset(eps_t[:], 1e-6), w=[eps_t])
        S.op("pool", lambda e: e.affine_select(out=ident_f[:], in_=ones_f[:, 0:128], pattern=[[1, 128]],
                                               compare_op=ALU.is_equal, fill=ZERO_R, base=0,
                                               channel_multiplier=-1), r=[ones_f], w=[ident_f])
        S.op("pool", lambda e: e.tensor_copy(out=ident_b[:], in_=ident_f[:]), r=[ident_f], w=[ident_b])
        cmask = sb(es, "cmask", [128, 4, 512], BF16)
        for j in range(4):
            S.op("pool", lambda e, j=j: e.affine_select(out=cmask[:, j, :], in_=ones_f[:], pattern=[[1, 512]],
                                                        compare_op=ALU.is_ge, fill=ZERO_R, base=-128 * j,
                                                        channel_multiplier=-1), r=[ones_f], w=[cmask])
        cnt = [0]

        def evac(out_t, out_ap, in_t, in_ap, func=None, scale=None):
            cnt[0] += 1
            if func is not None:
                kw = {} if scale is None else {"scale": scale}
                S.op("act", lambda e: e.activation(out=out_ap, in_=in_ap, func=func, **kw), r=[in_t], w=[out_t])
            elif cnt[0] % 2:
                S.op("act", lambda e: e.copy(out=out_ap, in_=in_ap), r=[in_t], w=[out_t])
            else:
                S.op("dve", lambda e: e.tensor_copy(out=out_ap, in_=in_ap), r=[in_t], w=[out_t])

        def rmsnorm(st, src, g_t, dst, n, tag):
            sq = [sb(st, f"sq{tag}{i}", [128, n], BF16) for i in range(2)]
            rstd = sb(st, "rstd" + tag, [128, n])
            ps = banks[6]
            for kc in range(KC):
                q = sq[kc % 2]
                S.op("act", lambda e, kc=kc, q=q: e.activation(out=q[:], in_=src[:, kc, 0:n], func=AF.Square),
                     r=[src], w=[q])
                S.op("pe", lambda e, kc=kc, q=q: e.matmul(ps[:, 0:n], ones_b[:], q[:], start=(kc == 0),
                                                          stop=(kc == KC - 1)), r=[q, ones_b], w=[ps])
            S.op("act", lambda e: e.activation(out=rstd[:], in_=ps[:, 0:n], func=AF.Sqrt,
                                               bias=eps_t[:], scale=1.0 / D), r=[ps, eps_t], w=[rstd])
            S.op("dve", lambda e: e.reciprocal(out=rstd[:], in_=rstd[:]), r=[rstd], w=[rstd])
            for kc in range(KC):
                S.op("dve", lambda e, kc=kc: e.scalar_tensor_tensor(
                    out=dst[:, kc, 0:n], in0=src[:, kc, 0:n], scalar=g_t[:, kc:kc + 1], in1=rstd[:],
                    op0=ALU.mult, op1=ALU.mult), r=[src, rstd, g_t], w=[dst])

        def load_vec_fm(st, name, ap_row):
            t = sb(st, name, [128, KC])
            S.dma(t[:], ap_row.rearrange("(c p) -> p c", p=128), w=[t])
            return t

        def load_bc(st, name, ap_row2d, n, dt=F32):
            t = sb(st, name, [128, n], dt)
            S.dma(t[:], ap_row2d.to_broadcast([128, n]), w=[t], q="pool" if dt != F32 else None)
            return t

        hv = hT.rearrange("(c p) t -> p c t", p=128)
        xv = xT.rearrange("(c p) t -> p c t", p=128)

        with contextlib.ExitStack() as st:
            buf = [sb(st, f"cpx{i}", [128, 4096]) for i in range(2)]
            ones_row = sb(st, "ones_row", [1, T], BF16)
            zero_f = sb(st, "zero_f", [1, 6144])
            S.op("pool", lambda e: e.memset(ones_row[:], 1.0), w=[ones_row])
            S.op("pool", lambda e: e.memset(zero_f[:], 0.0), w=[zero_f])
            xv_in = xT_in.rearrange("(c p) t -> p c t", p=128)
            i = 0
            for c in range(KC):
                for t0 in range(0, T, 4096):
                    n = min(4096, T - t0)
                    b = buf[i % 2]
                    i += 1
                    S.dma(b[:, 0:n], xv_in[:, c, t0:t0 + n], w=[b])
                    S.dma(xv[:, c, t0:t0 + n], b[:, 0:n], r=[b])
            S.dma(rkv[0:1, :], zero_f[0:1, :], r=[zero_f])
            zb = sb(st, "zb", [128, KC], BF16)
            S.op("pool", lambda e: e.memset(zb[:], 0.0), w=[zb])
            S.dma(hv[:, :, 0:1], zb[:].rearrange("p (c o) -> p c o", o=1), r=[zb])
            S.dma(lo_w[96:97, :], ones_row[:], r=[ones_row])
            S.dma(lo_a[96:97, :], ones_row[:], r=[ones_row])
            S.dma(lo_v[64:65, :], ones_row[:], r=[ones_row])
            S.barrier()

        def phase_norm1(l):
            with contextlib.ExitStack() as st:
                g_t = load_vec_fm(st, "g_mix", I["norm_mix_g"][l])
                x_t = sb(st, "xs", [128, KC, 512])
                hs = [sb(st, f"hs{i}", [128, KC, 512], BF16) for i in range(2)]
                for tt in range(NT):
                    h_t = hs[tt % 2]
                    for c0 in range(0, KC, 8):
                        S.dma(x_t[:, c0:c0 + 8, :], xv[:, c0:c0 + 8, tt * 512:(tt + 1) * 512], w=[x_t])
                    with contextlib.ExitStack() as st2:
                        rmsnorm(st2, x_t, g_t, h_t, 512, "n1")
                        S.barrier()
                    for c0 in range(0, KC, 8):
                        S.dma(hv[:, c0:c0 + 8, 1 + tt * 512:1 + (tt + 1) * 512], h_t[:, c0:c0 + 8, :], r=[h_t])
                S.barrier()

        def load_wblock(stg, wb, src_ap, w):
            sv = src_ap.rearrange("(c p) n -> p c n", p=128)
            for i, c0 in enumerate(range(0, KC, 8)):
                s = stg[i % 2]
                S.dma(s[:, :, 0:w], sv[:, c0:c0 + 8, :], w=[s])
                eng = ("act", "pool")[i % 2]
                if eng == "act":
                    S.op("act", lambda e, s=s, c0=c0: e.copy(out=wb[:, c0:c0 + 8, 0:w], in_=s[:, :, 0:w]),
                         r=[s], w=[wb])
                else:
                    S.op("pool", lambda e, s=s, c0=c0: e.tensor_copy(out=wb[:, c0:c0 + 8, 0:w], in_=s[:, :, 0:w]),
                         r=[s], w=[wb])

        def phase_proj(l):
            blocks = []
            for c0 in range(0, 2048, 512):
                blocks.append(("fm", c0, 512, qaT, c0, BF16))
            for c0 in range(2048, 3072, 512):
                blocks.append(("tm", c0, 512, va, (0, c0 - 2048), BF16))
            for c0 in range(3072, 9216, 512):
                blocks.append(("tm", c0, 512, rkv, (1, c0 - 3072), F32))
            for c0 in range(9216, 11264, 512):
                blocks.append(("fm", c0, 512, qcT, c0 - 9216, BF16))
            for c0 in range(11264, 12288, 512):
                blocks.append(("tm", c0, 512, vc, (0, c0 - 11264), BF16))
            blocks.append(("fm", 12288, 512, qiT, 0, BF16))
            blocks.append(("fm", 12800, 512, qiT, 512, BF16))
            blocks.append(("fm", 13312, 64, qiT, 1024, BF16))
            blocks.append(("tm", 13376, 16, wi, (0, 0), F32))
            with contextlib.ExitStack() as st:
                stg = [sb(st, f"stg{i}", [128, 8, 512]) for i in range(2)]
                wbs = [sb(st, f"wb{i}", [128, KC, 512], BF16) for i in range(2)]
                hts = [sb(st, f"ht{i}", [128, KC, 512], BF16) for i in range(2)]
                obf = [sb(st, f"obf{i}", [128, 512]) for i in range(3)]
                obb = [sb(st, f"obb{i}", [128, 512], BF16) for i in range(3)]
                ib = 0
                io = 0
                ih = 0
                for bi, (lay, c0, w, dst, off, dt) in enumerate(blocks):
                    wb = wbs[bi % 2]
                    load_wblock(stg, wb, I["w_in"][l][:, c0:c0 + w], w)
                    for tt in range(NT):
                        ht = hts[ih % 2]
                        ih += 1
                        for k0 in range(0, KC, 8):
                            S.dma(ht[:, k0:k0 + 8, :], hv[:, k0:k0 + 8, 1 + tt * 512:1 + tt * 512 + 512], w=[ht])
                        if lay == "fm":
                            for m in range((w + 127) // 128):
                                mw = min(128, w - m * 128)
                                ps = banks[ib % 4]
                                ib += 1
                                for kc in range(KC):
                                    S.op("pe", lambda e, kc=kc, ps=ps, m=m, mw=mw, ht=ht: e.matmul(
                                        ps[0:mw, :], wb[:, kc, m * 128:m * 128 + mw], ht[:, kc, :],
                                        start=(kc == 0), stop=(kc == KC - 1)), r=[wb, ht], w=[ps])
                                ob = (obb if dt == BF16 else obf)[io % 3]
                                io += 1
                                evac(ob, ob[0:mw, :], ps, ps[0:mw, :])
                                S.dma(dst[off + m * 128:off + m * 128 + mw, tt * 512:(tt + 1) * 512], ob[0:mw, :],
                                      r=[ob])
                        else:
                            for ts in range(4):
                                ps = banks[ib % 4]
                                ib += 1
                                for kc in range(KC):
                                    S.op("pe", lambda e, kc=kc, ps=ps, ts=ts, ht=ht: e.matmul(
                                        ps[:, 0:w], ht[:, kc, ts * 128:(ts + 1) * 128], wb[:, kc, 0:w],
                                        start=(kc == 0), stop=(kc == KC - 1)), r=[wb, ht], w=[ps])
                                ob = (obb if dt == BF16 else obf)[io % 3]
                                io += 1
                                evac(ob, ob[:, 0:w], ps, ps[:, 0:w])
                                r0 = off[0] + tt * 512 + ts * 128
                                S.dma(dst[r0:r0 + 128, off[1]:off[1] + w], ob[:, 0:w], r=[ob])
                S.barrier()

        def phase_lora(l):
            with contextlib.ExitStack() as st:
                WL = sb(st, "WL", [128, KC, 512], BF16)
                WLm = sb(st, "WLm", [128, KC, 512], BF16)
                specs = [("rw_w1", l, 0, 96, I["rw_mu_wag"][l][0]), ("rw_a1", l, 96, 96, I["rw_mu_wag"][l][1]),
                         ("rw_g1", l, 192, 256, I["rw_mu_wag"][l][2])]
                if l > 0:
                    specs.append(("rw_v1", l - 1, 448, 64, I["rw_v_mu"][l - 1]))
                mus = [load_vec_fm(st, "mu" + sp[0], sp[4]) for sp in specs]
                st_w = contextlib.ExitStack()
                wf = sb(st_w, "WLf", [128, KC, 256])
                for (nm, li, cs, w, mu_ap), mu in zip(specs, mus):
                    S.dma(wf[:, :, 0:w], I[nm][li].rearrange("(c p) n -> p c n", p=128), w=[wf])
                    S.op("act", lambda e, cs=cs, w=w: e.copy(out=WL[:, :, cs:cs + w], in_=wf[:, :, 0:w]),
                         r=[wf], w=[WL])
                    for kc in range(KC):
                        S.op("dve", lambda e, kc=kc, cs=cs, w=w, mu=mu: e.tensor_scalar(
                            out=WLm[:, kc, cs:cs + w], in0=wf[:, kc, 0:w], scalar1=mu[:, kc:kc + 1],
                            scalar2=None, op0=ALU.mult), r=[wf, mu], w=[WLm])
                S.barrier()
                st_w.close()
                outs = [(0, 96, AF.Tanh, lo_w, 0), (96, 96, None, lo_a, 0), (192, 128, AF.Sigmoid, lo_g, 0),
                        (320, 128, AF.Sigmoid, lo_g, 128)]
                if l > 0:
                    outs.append((448, 64, None, lo_v, 0))
                hts = [sb(st, f"lht{i}", [128, KC, 512], BF16) for i in range(1)]
                hp_t = sb(st, "lhp", [128, KC, 512], BF16)
                dh = sb(st, "dh", [128, KC, 512], BF16)
                obb = [sb(st, f"lob{i}", [128, 512], BF16) for i in range(3)]
                io = 0
                for tt in range(NT):
                    ht = hts[0]
                    for k0 in range(0, KC, 8):
                        S.dma(ht[:, k0:k0 + 8, :], hv[:, k0:k0 + 8, 1 + tt * 512:1 + tt * 512 + 512], w=[ht])
                        S.dma(hp_t[:, k0:k0 + 8, :], hv[:, k0:k0 + 8, tt * 512:tt * 512 + 512], w=[hp_t])
                    S.op("dve", lambda e, ht=ht: e.tensor_tensor(out=dh[:], in0=hp_t[:], in1=ht[:],
                                                                 op=ALU.subtract), r=[ht, hp_t], w=[dh])
                    for oi, (cs, mw, fn, dst, ro) in enumerate(outs):
                        ps = banks[oi % 4]
                        for kc in range(KC):
                            S.op("pe", lambda e, kc=kc, ps=ps, cs=cs, mw=mw, ht=ht: e.matmul(
                                ps[0:mw, :], WL[:, kc, cs:cs + mw], ht[:, kc, :], start=(kc == 0), stop=False),
                                r=[WL, ht], w=[ps])
                        for kc in range(KC):
                            S.op("pe", lambda e, kc=kc, ps=ps, cs=cs, mw=mw: e.matmul(
                                ps[0:mw, :], WLm[:, kc, cs:cs + mw], dh[:, kc, :], start=False, stop=(kc == KC - 1)),
                                r=[WLm, dh], w=[ps])
                        ob = obb[io % 3]
                        io += 1
                        evac(ob, ob[0:mw, :], ps, ps[0:mw, :], func=fn)
                        S.dma(dst[ro:ro + mw, tt * 512:(tt + 1) * 512], ob[0:mw, :], r=[ob])
                S.barrier()
        def phase_attn(l):
            lam_init = 0.8 - 0.6 * math.exp(-0.3 * l)
            with contextlib.ExitStack() as st:
                lt = [load_bc(st, f"lam{i}", I[n][l:l + 1, :], 64) for i, n in
                      enumerate(("lam_q1", "lam_k1", "lam_q2", "lam_k2"))]
                t1 = sb(st, "lt1", [128, 64])
                ss = sb(st, "lss", [128, 2])
                nlam = sb(st, "nlam", [128, 1])
                for j in range(2):
                    S.op("dve", lambda e, j=j: e.tensor_tensor(out=t1[:], in0=lt[2 * j][:], in1=lt[2 * j + 1][:],
                                                               op=ALU.mult), r=[lt[2 * j], lt[2 * j + 1]], w=[t1])
                    S.op("dve", lambda e, j=j: e.tensor_reduce(out=ss[:, j:j + 1], in_=t1[:], axis=AX.X, op=ALU.add),
                         r=[t1], w=[ss])
                S.op("act", lambda e: e.activation(out=ss[:], in_=ss[:], func=AF.Exp), r=[ss], w=[ss])
                S.op("dve", lambda e: e.tensor_tensor(out=nlam[:], in0=ss[:, 1:2], in1=ss[:, 0:1], op=ALU.subtract),
                     r=[ss], w=[nlam])
                S.op("dve", lambda e: e.tensor_scalar(out=nlam[:], in0=nlam[:], scalar1=-lam_init, scalar2=None,
                                                      op0=ALU.add), r=[nlam], w=[nlam])
                gsub = sb(st, "gsub", [128, 1])
                S.dma(gsub[:], I["diff_subln_g"][l].rearrange("(p o) -> p o", o=1), w=[gsub])
                S.op("dve", lambda e: e.tensor_scalar(out=gsub[:], in0=gsub[:], scalar1=1.0 - lam_init, scalar2=None,
                                                      op0=ALU.mult), r=[gsub], w=[gsub])
                QT = [sb(st, f"aq{i}", [128, T], BF16) for i in range(2)]
                KT = [sb(st, f"ak{i}", [128, T], BF16) for i in range(2)]
                V = [sb(st, f"av{i}", [128, NQ, 128], BF16) for i in range(2)]
                P = [sb(st, f"ap{i}", [128, 512], BF16) for i in range(4)]
                accs = [sb(st, f"aacc{i}", [128, 512]) for i in range(2)]
                rs = sb(st, "ars", [128, 512])
                om = [sb(st, f"aom{i}", [128, 512]) for i in range(2)]
                osq = sb(st, "aosq", [128, 512])
                ob = [sb(st, f"aob{i}", [128, 512], BF16) for i in range(2)]
                ip = 0
                for h in range(8):
                    q, k, v = QT[h % 2], KT[h % 2], V[h % 2]
                    S.dma(q[:], qaT[h * 128:(h + 1) * 128, :], w=[q])
                    S.dma(k[:], qaT[1024 + h * 128:1024 + (h + 1) * 128, :], w=[k])
                    S.dma(v[:], va[:, h * 128:(h + 1) * 128].rearrange("(k p) e -> p k e", p=128), w=[v])
                    for qt in range(NT):
                        nkt = 4 * (qt + 1)
                        for m in range(2):
                            O, SM = banks[4 + m], banks[6]
                            pr = slice(64 * m, 64 * m + 64)
                            acc = accs[m]

                            def issue_S(kt, pr=pr):
                                ps = banks[kt % 4]
                                S.op("pe", lambda e, ps=ps, kt=kt, pr=pr: e.matmul(
                                    ps[:], k[pr, kt * 128:(kt + 1) * 128], q[pr, qt * 512:(qt + 1) * 512],
                                    start=True, stop=True), r=[k, q], w=[ps])
                            for kt in range(min(2, nkt)):
                                issue_S(kt)
                            for kt in range(nkt):
                                if kt + 2 < nkt:
                                    issue_S(kt + 2)
                                ps = banks[kt % 4]
                                pb = P[ip % 4]
                                ip += 1
                                S.op("act", lambda e, ps=ps, pb=pb: e.activation(out=pb[:], in_=ps[:], func=AF.Exp,
                                                                                 scale=0.125), r=[ps], w=[pb])
                                if kt >= 4 * qt:
                                    S.op("dve", lambda e, pb=pb, kt=kt: e.tensor_tensor(
                                        out=pb[:], in0=pb[:], in1=cmask[:, kt - 4 * qt, :], op=ALU.mult),
                                        r=[pb, cmask], w=[pb])
                                S.op("pe", lambda e, pb=pb, kt=kt, O=O: e.matmul(
                                    O[:], v[:, kt, :], pb[:], start=(kt == 0), stop=(kt == nkt - 1)), r=[v, pb], w=[O])
                                if kt == 0:
                                    S.op("pool", lambda e, pb=pb, acc=acc: e.tensor_copy(out=acc[:], in_=pb[:]),
                                         r=[pb], w=[acc])
                                else:
                                    S.op("pool", lambda e, pb=pb, acc=acc: e.tensor_tensor(
                                        out=acc[:], in0=acc[:], in1=pb[:], op=ALU.add), r=[pb, acc], w=[acc])
                            S.op("pe", lambda e, SM=SM, acc=acc: e.matmul(SM[:], ones_f[:, 0:128], acc[:], start=True,
                                                                         stop=True), r=[ones_f, acc], w=[SM])
                            S.op("dve", lambda e, SM=SM: e.reciprocal(out=rs[:], in_=SM[:]), r=[SM], w=[rs])
                            S.op("dve", lambda e, O=O, m=m: e.tensor_tensor(out=om[m][:], in0=O[:], in1=rs[:],
                                                                            op=ALU.mult), r=[O, rs], w=[om[m]])
                        S.op("dve", lambda e: e.scalar_tensor_tensor(out=om[0][:], in0=om[1][:], scalar=nlam[:, 0:1],
                                                                     in1=om[0][:], op0=ALU.mult, op1=ALU.add),
                             r=[om[1], nlam, om[0]], w=[om[0]])
                        S.op("act", lambda e: e.activation(out=osq[:], in_=om[0][:], func=AF.Square), r=[om[0]], w=[osq])
                        pn = banks[6]
                        S.op("pe", lambda e: e.matmul(pn[:], ones_f[:, 0:128], osq[:], start=True, stop=True),
                             r=[ones_f, osq], w=[pn])
                        S.op("act", lambda e: e.activation(out=rs[:], in_=pn[:], func=AF.Sqrt, bias=eps_t[:],
                                                           scale=1.0 / 128), r=[pn, eps_t], w=[rs])
                        S.op("dve", lambda e: e.reciprocal(out=rs[:], in_=rs[:]), r=[rs], w=[rs])
                        o = ob[qt % 2]
                        S.op("dve", lambda e, o=o: e.scalar_tensor_tensor(out=o[:], in0=om[0][:], scalar=gsub[:, 0:1],
                                                                          in1=rs[:], op0=ALU.mult, op1=ALU.mult),
                             r=[om[0], gsub, rs], w=[o])
                        S.dma(mixT[h * 128:(h + 1) * 128, qt * 512:(qt + 1) * 512], o[:], r=[o])
                S.barrier()

        def phase_dsa(l):
            with contextlib.ExitStack() as st:
                kiT = sb(st, "kiT", [64, T], BF16)
                S.dma(kiT[:], qiT[1024:1088, :], w=[kiT])
                selT = sb(st, "selT", [128, NQ, 512], mybir.dt.uint8)
                sc = sb(st, "sc", [128, T])
                wk = sb(st, "wk", [128, T])
                sel = sb(st, "sel", [128, T], BF16)
                qi_r = sb(st, "qi_r", [64, 16, 128], BF16)
                wi_r = sb(st, "wi_r", [128, 16])
                wabs = sb(st, "wabs", [128, 16])
                wsgn = sb(st, "wsgn", [128, 16])
                tmp = [sb(st, f"dtmp{i}", [128, 512]) for i in range(2)]
                m8 = sb(st, "m8", [128, 8])
                QT = [sb(st, f"cq{i}", [128, 512], BF16) for i in range(2)]
                KT = [sb(st, f"ck{i}", [128, T], BF16) for i in range(1)]
                V = [sb(st, f"cv{i}", [128, NQ, 128], BF16) for i in range(1)]
                P = [sb(st, f"cp{i}", [128, 512], BF16) for i in range(4)]
                cacc = sb(st, "cacc", [128, 512])
                rs = sb(st, "crs", [128, 512])
                ob = [sb(st, f"cob{i}", [128, 512], BF16) for i in range(2)]
                ip = 0
                for qt in range(NT):
                    nkq = 4 * (qt + 1)
                    S.op("pool", lambda e: e.memset(selT[:, 4 * qt:4 * qt + 4, :], 0.0), w=[selT])
                    for r in range(4):
                        g = 4 * qt + r
                        nk = 128 * (g + 1)
                        S.dma(qi_r[:], qiT[0:1024, g * 128:(g + 1) * 128].rearrange("(h d) t -> d h t", d=64), w=[qi_r])
                        S.dma(wi_r[:], wi[g * 128:(g + 1) * 128, :], w=[wi_r])
                        S.op("act", lambda e: e.activation(out=wabs[:], in_=wi_r[:], func=AF.Abs), r=[wi_r], w=[wabs])
                        S.op("act", lambda e: e.activation(out=wsgn[:], in_=wi_r[:], func=AF.Sign), r=[wi_r], w=[wsgn])
                        for kb in range((nk + 511) // 512):
                            kw = min(512, nk - kb * 512)
                            ks = slice(kb * 512, kb * 512 + kw)
                            for ih in range(16):
                                ps = banks[ih % 2]
                                tm = tmp[ih % 2]
                                S.op("pe", lambda e, ps=ps, ih=ih, ks=ks, kw=kw: e.matmul(
                                    ps[:, 0:kw], qi_r[:, ih, :], kiT[:, ks], start=True, stop=True),
                                    r=[qi_r, kiT], w=[ps])
                                S.op("act", lambda e, ps=ps, tm=tm, ih=ih, kw=kw: e.activation(
                                    out=tm[:, 0:kw], in_=ps[:, 0:kw], func=AF.Relu, scale=wabs[:, ih:ih + 1]),
                                    r=[ps, wabs], w=[tm])
                                if ih == 0:
                                    S.op("dve", lambda e, tm=tm, ks=ks, kw=kw: e.tensor_scalar(
                                        out=sc[:, ks], in0=tm[:, 0:kw], scalar1=wsgn[:, 0:1], scalar2=None,
                                        op0=ALU.mult), r=[tm, wsgn], w=[sc])
                                else:
                                    S.op("dve", lambda e, tm=tm, ks=ks, kw=kw, ih=ih: e.scalar_tensor_tensor(
                                        out=sc[:, ks], in0=tm[:, 0:kw], scalar=wsgn[:, ih:ih + 1], in1=sc[:, ks],
                                        op0=ALU.mult, op1=ALU.add), r=[tm, wsgn, sc], w=[sc])
                        dg = slice(g * 128, (g + 1) * 128)
                        S.op("pool", lambda e, dg=dg: e.affine_select(
                            out=sc[:, dg], in_=sc[:, dg], pattern=[[-1, 128]], compare_op=ALU.is_ge, fill=NEG_R,
                            base=0, channel_multiplier=1), r=[sc], w=[sc])
                        if g >= 2:
                            S.op("dve", lambda e, nk=nk: e.max(out=m8[:], in_=sc[:, 0:nk]), r=[sc], w=[m8])
                            S.op("dve", lambda e, nk=nk: e.match_replace(out=wk[:, 0:nk], in_to_replace=m8[:],
                                                                         in_values=sc[:, 0:nk], imm_value=NEG),
                                 r=[sc, m8], w=[wk])
                            for it in range(1, 32):
                                S.op("dve", lambda e, nk=nk: e.max(out=m8[:], in_=wk[:, 0:nk]), r=[wk], w=[m8])
                                if it < 31:
                                    S.op("dve", lambda e, nk=nk: e.match_replace(
                                        out=wk[:, 0:nk], in_to_replace=m8[:], in_values=wk[:, 0:nk], imm_value=NEG),
                                        r=[wk, m8], w=[wk])
                            S.op("dve", lambda e, nk=nk: e.tensor_scalar(out=sel[:, 0:nk], in0=sc[:, 0:nk],
                                                                         scalar1=m8[:, 7:8], scalar2=None,
                                                                         op0=ALU.is_ge), r=[sc, m8], w=[sel])
                        else:
                            S.op("dve", lambda e, nk=nk: e.tensor_scalar(out=sel[:, 0:nk], in0=sc[:, 0:nk],
                                                                         scalar1=-1.0e38, scalar2=None,
                                                                         op0=ALU.is_ge), r=[sc], w=[sel])
                        for k0 in range(0, g + 1, 8):
                            nb = min(8, g + 1 - k0)
                            for j in range(nb):
                                S.op("pe", lambda e, j=j, k0=k0: e.transpose(
                                    bank_bf[:, j * 128:(j + 1) * 128], sel[:, (k0 + j) * 128:(k0 + j + 1) * 128],
                                    ident_b[:]), r=[sel, ident_b], w=[bank_bf])
                            evac(selT, selT[:, k0:k0 + nb, r * 128:(r + 1) * 128], bank_bf,
                                 bank_bf[:, 0:nb * 128].rearrange("p (k q) -> p k q", q=128))
                    for h in range(8):
                        q, k, v = QT[h % 2], KT[0], V[0]
                        S.dma(q[:], qcT[h * 128:(h + 1) * 128, qt * 512:(qt + 1) * 512], w=[q])
                        S.dma(k[:, 0:nkq * 128], qcT[1024 + h * 128:1024 + (h + 1) * 128, 0:nkq * 128], w=[k])
                        S.dma(v[:, 0:nkq, :], vc[0:nkq * 128, h * 128:(h + 1) * 128].rearrange("(k p) e -> p k e", p=128),
                              w=[v])
                        O, SM = banks[4], banks[5]

                        def issue_S(kt):
                            ps = banks[kt % 4]
                            S.op("pe", lambda e, ps=ps, kt=kt: e.matmul(ps[:], k[:, kt * 128:(kt + 1) * 128], q[:],
                                                                        start=True, stop=True), r=[k, q], w=[ps])
                        for kt in range(min(2, nkq)):
                            issue_S(kt)
                        for kt in range(nkq):
                            if kt + 2 < nkq:
                                issue_S(kt + 2)
                            ps = banks[kt % 4]
                            pb = P[ip % 4]
                            ip += 1
                            S.op("act", lambda e, ps=ps, pb=pb: e.activation(out=pb[:], in_=ps[:], func=AF.Exp,
                                                                             scale=128 ** -0.5), r=[ps], w=[pb])
                            S.op("dve", lambda e, pb=pb, kt=kt: e.tensor_tensor(out=pb[:], in0=pb[:], in1=selT[:, kt, :],
                                                                                op=ALU.mult), r=[pb, selT], w=[pb])
                            S.op("pe", lambda e, pb=pb, kt=kt: e.matmul(O[:], v[:, kt, :], pb[:], start=(kt == 0),
                                                                        stop=(kt == nkq - 1)), r=[v, pb], w=[O])
                            if kt == 0:
                                S.op("pool", lambda e, pb=pb: e.tensor_copy(out=cacc[:], in_=pb[:]), r=[pb], w=[cacc])
                            else:
                                S.op("pool", lambda e, pb=pb: e.tensor_tensor(out=cacc[:], in0=cacc[:], in1=pb[:],
                                                                              op=ALU.add), r=[pb, cacc], w=[cacc])
                        S.op("pe", lambda e: e.matmul(SM[:], ones_f[:, 0:128], cacc[:], start=True, stop=True),
                             r=[ones_f, cacc], w=[SM])
                        S.op("dve", lambda e: e.reciprocal(out=rs[:], in_=SM[:]), r=[SM], w=[rs])
                        o = ob[h % 2]
                        S.op("dve", lambda e, o=o: e.tensor_tensor(out=o[:], in0=O[:], in1=rs[:], op=ALU.mult),
                             r=[O, rs], w=[o])
                        S.dma(mixT[3072 + h * 128:3072 + (h + 1) * 128, qt * 512:(qt + 1) * 512], o[:], r=[o])
                S.barrier()
        def phase_rwkv(l):
            HC = 1024
            with contextlib.ExitStack() as st:
                def tri(name, pat, base, cm, val):
                    t = sb(st, name, [128, 128])
                    S.op("pool", lambda e: e.memset(t[:], val), w=[t])
                    S.op("pool", lambda e: e.affine_select(out=t[:], in_=t[:], pattern=pat, compare_op=ALU.is_ge,
                                                           fill=ZERO_R, base=base, channel_multiplier=cm), r=[t], w=[t])
                    return t
                triI = tri("triI", [[1, 128]], 0, -1, DEC)
                triS = tri("triS", [[1, 128]], -1, -1, DEC)
                triA = tri("triA", [[-1, 128]], -1, 1, DEC)
                negcol = sb(st, "negcol", [128, 1])
                S.op("pool", lambda e: e.memset(negcol[:], DEC), w=[negcol])
                epsg = sb(st, "epsg", [128, 1])
                S.op("pool", lambda e: e.memset(epsg[:], 64e-5), w=[epsg])

                def mask4(name, pat, base, cm):
                    t = sb(st, name, [128, 4, 128])
                    S.op("pool", lambda e: e.memset(t[:], 1.0), w=[t])
                    for j in range(4):
                        S.op("pool", lambda e, j=j: e.affine_select(out=t[:, j, :], in_=t[:, j, :], pattern=pat,
                                                                    compare_op=ALU.is_ge, fill=ZERO_R, base=base,
                                                                    channel_multiplier=cm), r=[t], w=[t])
                    return t
                msS = mask4("msS", [[1, 128]], -1, -1)
                msI = mask4("msI", [[1, 128]], 0, -1)
                msT = mask4("msT", [[-1, 128]], -1, 1)
                id4 = sb(st, "id4", [128, 4, 128])
                for j in range(4):
                    S.op("pool", lambda e, j=j: e.tensor_copy(out=id4[:, j, :], in_=ident_f[:]), r=[ident_f], w=[id4])
                def lw2(name, wnm, li, bnm, K):
                    t = sb(st, name, [K + 1, 2048], BF16)
                    S.dma(t[0:K, :], I[wnm][li], w=[t], q="pool")
                    if bnm is not None:
                        S.dma(t[K:K + 1, :], I[bnm][li:li + 1, :], w=[t], q="pool")
                    return t
                w2e = lw2("w2e", "rw_w2", l, "rw_w0", 96)
                a2e = lw2("a2e", "rw_a2", l, "rw_a0", 96)
                g2a = sb(st, "g2a", [128, 2048], BF16)
                g2b = sb(st, "g2b", [128, 2048], BF16)
                S.dma(g2a[0:128, :], I["rw_g2"][l][0:128, :], w=[g2a], q="pool")
                S.dma(g2b[:], I["rw_g2"][l][128:256, :], w=[g2b], q="pool")
                v2e = lw2("v2e", "rw_v2", l - 1, "rw_v0", 64) if l > 0 else None
                S0T = sb(st, "S0T", [128, 1024])
                S.op("pool", lambda e: e.memset(S0T[:], 0.0), w=[S0T])
                G = [sb(st, f"G{i}", [128, HC]) for i in range(12)]
                cur_r, cur_k, cur_v, P0, P1, P2, E0, E1, E2, E3, A0, G0 = G
                bc = [sb(st, f"bc{i}", [128, HC]) for i in range(2)]
                TT = [sb(st, f"TT{i}", [128, 8, 128]) for i in range(6)]
                BtT, KtT = TT[0], TT[1]
                AtM = (TT[2], TT[3])
                RtM = (TT[4], TT[5])
                for t_ in TT[2:]:
                    S.op("pool", lambda e, t_=t_: e.memset(t_[:], 0.0), w=[t_])
                M = [sb(st, f"M{i}", [128, 8, 128]) for i in range(9)]
                low = sb(st, "low", [97, 128], BF16)
                loa = sb(st, "loa", [97, 128], BF16)
                log = sb(st, "log", [128, 2, 128], BF16)
                lov = sb(st, "lov", [65, 128], BF16)
                sm = [sb(st, f"sm{i}", [128, 16]) for i in range(6)]
                PCf = sb(st, "PCf", [128, 8])
                GTs = sb(st, "GTs", [128, 512])
                UTs = sb(st, "UTs", [128, 512])
                obT = sb(st, "obT", [128, 8, 128], BF16)
                ibc = [0]
                ibk = [0]

                def getbc(ap_row2d):
                    t = bc[ibc[0] % 2]
                    ibc[0] += 1
                    S.dma(t[:], ap_row2d.to_broadcast([128, HC]), w=[t])
                    return t

                def nb():
                    ibk[0] += 1
                    return banks[ibk[0] % 6]

                def tt(o, a, b, op, eng="dve"):
                    S.op(eng, lambda e: e.tensor_tensor(out=o[:], in0=a[:], in1=b[:], op=op), r=[a, b], w=[o])

                def lora2(dst, lhs_list, func):
                    for cb in range(HC // 512):
                        ps = nb()
                        for i, (lh, K, rh, c_off) in enumerate(lhs_list):
                            S.op("pe", lambda e, ps=ps, lh=lh, K=K, rh=rh, c_off=c_off, cb=cb, i=i: e.matmul(
                                ps[:], lh, rh[0:K, c_off + cb * 512:c_off + (cb + 1) * 512], start=(i == 0),
                                stop=(i == len(lhs_list) - 1)), r=[low, loa, log, lov, rh], w=[ps])
                        evac(dst, dst[:, cb * 512:(cb + 1) * 512], ps, ps[:], func=func)

                for c in range(NQ):
                    t0 = c * 128
                    S.dma(low[:], lo_w[:, t0:t0 + 128], w=[low])
                    S.dma(loa[:], lo_a[:, t0:t0 + 128], w=[loa])
                    S.dma(log[:], lo_g[:, t0:t0 + 128].rearrange("(k p) t -> p k t", p=128), w=[log])
                    if l > 0:
                        S.dma(lov[:], lo_v[:, t0:t0 + 128], w=[lov])
                    for hf in range(2):
                        co = hf * HC
                        for qi_, (cu, pv) in enumerate(((cur_r, P0), (cur_k, P1), (cur_v, P2))):
                            cs = qi_ * 2048 + co
                            S.dma(cu[:], rkv[1 + t0:1 + t0 + 128, cs:cs + HC], w=[cu])
                            S.dma(pv[:], rkv[t0:t0 + 128, cs:cs + HC], w=[pv])
                            mu = getbc(I["rw_mu_rkv"][l:l + 1, cs:cs + HC])
                            tt(pv, pv, cu, ALU.subtract, "pool")
                            tt(pv, pv, mu, ALU.mult, "pool")
                            tt(cu, cu, pv, ALU.add, "pool")
                        lora2(P0, [(low[:], 97, w2e, co)], AF.Sigmoid)
                        lora2(A0, [(loa[:], 97, a2e, co)], AF.Sigmoid)
                        lora2(G0, [(log[:, 0, :], 128, g2a, co), (log[:, 1, :], 128, g2b, co)], None)
                        if l > 0:
                            lora2(P1, [(lov[:], 65, v2e, co)], AF.Sigmoid)
                            S.dma(P2[:], vfirst[t0:t0 + 128, co:co + HC], w=[P2])
                            tt(P2, P2, cur_v, ALU.subtract)
                            tt(P2, P2, P1, ALU.mult)
                            tt(cur_v, cur_v, P2, ALU.add)
                        else:
                            S.dma(vfirst[t0:t0 + 128, co:co + HC], cur_v[:], r=[cur_v])
                        _chk(1)
                        for (tri_t, outs) in ((triI, ((E0, 1.0), (E1, -1.0))), (triS, ((E2, 1.0),)), (triA, ((E3, 1.0),))):
                            for cb in range(HC // 512):
                                ps = nb()
                                S.op("pe", lambda e, ps=ps, tri_t=tri_t, cb=cb: e.matmul(
                                    ps[:], tri_t[:], P0[:, cb * 512:(cb + 1) * 512], start=True, stop=True),
                                    r=[tri_t, P0], w=[ps])
                                for (dst, scl) in outs:
                                    evac(dst, dst[:, cb * 512:(cb + 1) * 512], ps, ps[:], func=AF.Exp, scale=scl)
                        psf = nb()
                        for j in range(8):
                            S.op("pe", lambda e, j=j, psf=psf: e.matmul(psf[:, j:j + 1], P0[:, j * 128:(j + 1) * 128],
                                                                         negcol[:], start=True, stop=True),
                                 r=[P0, negcol], w=[psf])
                        evac(PCf, PCf[:], psf, psf[:, 0:8], func=AF.Exp)
                        _chk(2)
                        kkb = getbc(I["rw_k_k"][l:l + 1, co:co + HC])
                        tt(P0, cur_k, kkb, ALU.mult)
                        tt(P1, P0, P0, ALU.mult, "pool")
                        S.op("dve", lambda e: e.tensor_reduce(out=sm[0][:], in_=P1[:].rearrange("p (h d) -> p h d", d=64),
                                                              axis=AX.X, op=ALU.add), r=[P1], w=[sm[0]])
                        S.op("act", lambda e: e.activation(out=sm[0][:], in_=sm[0][:], func=AF.Sqrt), r=[sm[0]], w=[sm[0]])
                        S.op("dve", lambda e: e.tensor_scalar(out=sm[0][:], in0=sm[0][:], scalar1=1e-12, scalar2=None,
                                                              op0=ALU.max), r=[sm[0]], w=[sm[0]])
                        S.op("dve", lambda e: e.reciprocal(out=sm[0][:], in_=sm[0][:]), r=[sm[0]], w=[sm[0]])
                        for h in range(16):
                            S.op("dve", lambda e, h=h: e.tensor_scalar(out=P0[:, h * 64:(h + 1) * 64],
                                                                       in0=P0[:, h * 64:(h + 1) * 64],
                                                                       scalar1=sm[0][:, h:h + 1], scalar2=None,
                                                                       op0=ALU.mult), r=[P0, sm[0]], w=[P0])
                        kab = getbc(I["rw_k_a"][l:l + 1, co:co + HC])
                        S.op("dve", lambda e: e.scalar_tensor_tensor(out=P1[:], in0=A0[:], scalar=-1.0, in1=kab[:],
                                                                     op0=ALU.add, op1=ALU.mult), r=[A0, kab], w=[P1])
                        S.op("dve", lambda e: e.scalar_tensor_tensor(out=P1[:], in0=P1[:], scalar=1.0, in1=cur_k[:],
                                                                     op0=ALU.add, op1=ALU.mult), r=[P1, cur_k], w=[P1])
                        tt(P2, P0, A0, ALU.mult)
                        S.op("dve", lambda e: e.scalar_tensor_tensor(out=E2[:], in0=P0[:], scalar=-1.0, in1=E2[:],
                                                                     op0=ALU.mult, op1=ALU.mult), r=[P0, E2], w=[E2])
                        tt(A0, P2, E1, ALU.mult, "pool")
                        tt(E1, P1, E1, ALU.mult)
                        tt(E0, cur_r, E0, ALU.mult, "pool")
                        tt(P2, P2, E3, ALU.mult)
                        tt(E3, P1, E3, ALU.mult, "pool")
                        rkb = getbc(I["rw_r_k"][l:l + 1, co:co + HC])
                        tt(P0, cur_r, P1, ALU.mult)
                        tt(P0, P0, rkb, ALU.mult)
                        S.op("dve", lambda e: e.tensor_reduce(out=sm[1][:], in_=P0[:].rearrange("p (h d) -> p h d", d=64),
                                                              axis=AX.X, op=ALU.add), r=[P0], w=[sm[1]])
                        _chk(3)
                        for src, dstT in ((E2, AtM), (A0, BtT), (E1, KtT), (E0, RtM)):
                            for j0 in range(0, 8, 4):
                                ps = nb()
                                for j in range(4):
                                    S.op("pe", lambda e, ps=ps, j=j, j0=j0, src=src: e.matmul(
                                        ps[:, j * 128:(j + 1) * 128], src[:, (j0 + j) * 128:(j0 + j + 1) * 128],
                                        ident_f[:], start=True, stop=True), r=[src, ident_f], w=[ps])
                                if isinstance(dstT, tuple):
                                    for par in range(2):
                                        pr = slice(64 * par, 64 * par + 64)
                                        evac(dstT[par], dstT[par][pr, j0:j0 + 4, :], ps,
                                             ps[pr, :].rearrange("p (a b) -> p a b", b=128))
                                else:
                                    evac(dstT, dstT[:, j0:j0 + 4, :], ps, ps[:].rearrange("p (a b) -> p a b", b=128))
                        _chk(4)
                        yb = E0
                        for gi in range(2):
                            N_, NT_, Lak, Mbr, Mkr, An, ATn, Tm, Tmn = M

                            def cc(dst, lh, rh, msk):
                                for half in range(2):
                                    ps = nb()
                                    for hh in range(4):
                                        h = 8 * gi + 4 * half + hh
                                        lh_ = lh[h % 2] if isinstance(lh, tuple) else lh
                                        rh_ = rh[h % 2] if isinstance(rh, tuple) else rh
                                        S.op("pe", lambda e, ps=ps, hh=hh, h=h, lh_=lh_, rh_=rh_: e.matmul(
                                            ps[:, hh * 128:(hh + 1) * 128], lh_[:, h // 2, :], rh_[:, h // 2, :],
                                            start=True, stop=True), r=[lh_, rh_], w=[ps])
                                    S.op("dve", lambda e, ps=ps, half=half: e.tensor_tensor(
                                        out=dst[:, 4 * half:4 * half + 4, :],
                                        in0=ps[:].rearrange("p (a b) -> p a b", b=128), in1=msk[:], op=ALU.mult),
                                        r=[ps, msk], w=[dst])
                            cc(N_, BtT, AtM, msS)
                            cc(NT_, AtM, BtT, msT)
                            cc(Lak, KtT, AtM, msS)
                            cc(Mbr, BtT, RtM, msI)
                            cc(Mkr, KtT, RtM, msI)
                            _chk(5)
                            for half in range(2):
                                S.op("dve", lambda e, half=half: e.tensor_tensor(
                                    out=Tm[:, 4 * half:4 * half + 4, :], in0=N_[:, 4 * half:4 * half + 4, :],
                                    in1=id4[:], op=ALU.add), r=[N_, id4], w=[Tm])
                            A, AT = N_, NT_

                            def mm8(dst, lh, rh, add=None):
                                for half in range(2):
                                    ps = nb()
                                    for hh in range(4):
                                        j = 4 * half + hh
                                        S.op("pe", lambda e, ps=ps, hh=hh, j=j: e.matmul(
                                            ps[:, hh * 128:(hh + 1) * 128], lh[:, j, :], rh[:, j, :], start=True,
                                            stop=True), r=[lh, rh], w=[ps])
                                    pv3 = ps[:].rearrange("p (a b) -> p a b", b=128)
                                    if add is None:
                                        evac(dst, dst[:, 4 * half:4 * half + 4, :], ps, pv3)
                                    else:
                                        S.op("dve", lambda e, half=half, pv3=pv3, ps=ps: e.tensor_tensor(
                                            out=dst[:, 4 * half:4 * half + 4, :], in0=pv3,
                                            in1=add[:, 4 * half:4 * half + 4, :], op=ALU.add), r=[ps, add], w=[dst])
                            for j in range(1, 7):
                                if j < 6:
                                    mm8(An, AT, A)
                                mm8(ATn, A, AT)
                                A, An = An, A
                                AT, ATn = ATn, AT
                                mm8(Tmn, AT, Tm, add=Tm)
                                Tm, Tmn = Tmn, Tm
                            _chk(6)
                            psg = nb()
                            for hh in range(8):
                                h = 8 * gi + hh
                                pr = slice(64 * (h % 2), 64 * (h % 2) + 64)
                                hp = h // 2
                                gp = hf * 8 + hp
                                S.op("pe", lambda e, hh=hh, h=h, hp=hp, gp=gp: e.matmul(
                                    psg[:, hh * 64:(hh + 1) * 64], AtM[h % 2][:, hp, :], S0T[:, gp * 64:(gp + 1) * 64],
                                    start=True, stop=False), r=[AtM[h % 2], S0T], w=[psg])
                                S.op("pe", lambda e, hh=hh, h=h: e.matmul(
                                    psg[:, hh * 64:(hh + 1) * 64], Lak[:, hh, :], cur_v[:, h * 64:(h + 1) * 64],
                                    start=False, stop=True), r=[Lak, cur_v], w=[psg])
                            evac(GTs, GTs[:], psg, psg[:])
                            psu = nb()
                            for hh in range(8):
                                S.op("pe", lambda e, hh=hh: e.matmul(psu[:, hh * 64:(hh + 1) * 64], Tm[:, hh, :],
                                                                     GTs[:, hh * 64:(hh + 1) * 64], start=True, stop=True),
                                     r=[Tm, GTs], w=[psu])
                            evac(UTs, UTs[:], psu, psu[:])
                            psy = nb()
                            for hh in range(8):
                                h = 8 * gi + hh
                                pr = slice(64 * (h % 2), 64 * (h % 2) + 64)
                                hp = h // 2
                                gp = hf * 8 + hp
                                ys = psy[:, hh * 64:(hh + 1) * 64]
                                S.op("pe", lambda e, ys=ys, h=h, hp=hp, gp=gp: e.matmul(
                                    ys, RtM[h % 2][:, hp, :], S0T[:, gp * 64:(gp + 1) * 64], start=True, stop=False),
                                    r=[RtM[h % 2], S0T], w=[psy])
                                S.op("pe", lambda e, ys=ys, hh=hh: e.matmul(ys, Mbr[:, hh, :], UTs[:, hh * 64:(hh + 1) * 64],
                                                                            start=False, stop=False), r=[Mbr, UTs], w=[psy])
                                S.op("pe", lambda e, ys=ys, hh=hh, h=h: e.matmul(ys, Mkr[:, hh, :],
                                                                                 cur_v[:, h * 64:(h + 1) * 64],
                                                                                 start=False, stop=True),
                                     r=[Mkr, cur_v], w=[psy])
                            evac(yb, yb[:, gi * 512:(gi + 1) * 512], psy, psy[:])
                            pss = nb()
                            for pi in range(4):
                                hp = 4 * gi + pi
                                so = pss[:, pi * 128:(pi + 1) * 128]
                                S.op("pe", lambda e, so=so, hp=hp, pi=pi: e.matmul(
                                    so, P2[:, hp * 128:(hp + 1) * 128], UTs[:, pi * 128:(pi + 1) * 128], start=True,
                                    stop=False), r=[P2, UTs], w=[pss])
                                S.op("pe", lambda e, so=so, hp=hp: e.matmul(
                                    so, E3[:, hp * 128:(hp + 1) * 128], cur_v[:, hp * 128:(hp + 1) * 128], start=False,
                                    stop=True), r=[E3, cur_v], w=[pss])
                            for pi in range(4):
                                hp = 4 * gi + pi
                                gp = hf * 8 + hp
                                for par in range(2):
                                    pr = slice(64 * par, 64 * par + 64)
                                    S.op("dve", lambda e, pr=pr, par=par, pi=pi, hp=hp, gp=gp: e.scalar_tensor_tensor(
                                        out=S0T[pr, gp * 64:(gp + 1) * 64], in0=S0T[pr, gp * 64:(gp + 1) * 64],
                                        scalar=PCf[pr, hp:hp + 1],
                                        in1=pss[pr, pi * 128 + par * 64:pi * 128 + par * 64 + 64],
                                        op0=ALU.mult, op1=ALU.add), r=[S0T, PCf, pss], w=[S0T])
                        _chk(7)
                        y3 = yb[:].rearrange("p (h d) -> p h d", d=64)
                        S.op("dve", lambda e: e.tensor_reduce(out=sm[2][:], in_=y3, axis=AX.X, op=ALU.add), r=[yb], w=[sm[2]])
                        tt(P0, yb, yb, ALU.mult, "pool")
                        S.op("dve", lambda e: e.tensor_reduce(out=sm[3][:], in_=P0[:].rearrange("p (h d) -> p h d", d=64),
                                                              axis=AX.X, op=ALU.add), r=[P0], w=[sm[3]])
                        S.op("dve", lambda e: e.tensor_scalar(out=sm[2][:], in0=sm[2][:], scalar1=1.0 / 64, scalar2=None,
                                                              op0=ALU.mult), r=[sm[2]], w=[sm[2]])
                        tt(sm[4], sm[2], sm[2], ALU.mult)
                        S.op("dve", lambda e: e.scalar_tensor_tensor(out=sm[3][:], in0=sm[3][:], scalar=1.0 / 64,
                                                                     in1=sm[4][:], op0=ALU.mult, op1=ALU.subtract),
                             r=[sm[3], sm[4]], w=[sm[3]])
                        S.op("act", lambda e: e.activation(out=sm[3][:], in_=sm[3][:], func=AF.Sqrt, bias=epsg[:]),
                             r=[sm[3], epsg], w=[sm[3]])
                        S.op("dve", lambda e: e.reciprocal(out=sm[3][:], in_=sm[3][:]), r=[sm[3]], w=[sm[3]])
                        for h in range(16):
                            hs = slice(h * 64, (h + 1) * 64)
                            S.op("dve", lambda e, h=h, hs=hs: e.tensor_scalar(
                                out=yb[:, hs], in0=yb[:, hs], scalar1=sm[2][:, h:h + 1], scalar2=sm[3][:, h:h + 1],
                                op0=ALU.subtract, op1=ALU.mult), r=[yb, sm[2], sm[3]], w=[yb])
                        lwb = getbc(I["rw_ln_w"][l:l + 1, co:co + HC])
                        tt(yb, yb, lwb, ALU.mult)
                        lbb = getbc(I["rw_ln_b"][l:l + 1, co:co + HC])
                        tt(yb, yb, lbb, ALU.add)
                        for h in range(16):
                            hs = slice(h * 64, (h + 1) * 64)
                            S.op("dve", lambda e, h=h, hs=hs: e.scalar_tensor_tensor(
                                out=yb[:, hs], in0=cur_v[:, hs], scalar=sm[1][:, h:h + 1], in1=yb[:, hs],
                                op0=ALU.mult, op1=ALU.add), r=[cur_v, sm[1], yb], w=[yb])
                        tt(yb, yb, G0, ALU.mult)
                        for j0 in range(0, 8, 4):
                            ps = nb()
                            for j in range(4):
                                S.op("pe", lambda e, ps=ps, j=j, j0=j0: e.matmul(
                                    ps[:, j * 128:(j + 1) * 128], yb[:, (j0 + j) * 128:(j0 + j + 1) * 128], ident_f[:],
                                    start=True, stop=True), r=[yb, ident_f], w=[ps])
                            evac(obT, obT[:, j0:j0 + 4, :], ps, ps[:].rearrange("p (a b) -> p a b", b=128))
                        S.dma(mixT[1024 + co:1024 + co + HC, t0:t0 + 128].rearrange("(j p) t -> p j t", p=128),
                              obT[:], r=[obT])
                S.barrier()
        def cast_to_dram(src2d, dst2d, rows, cols):
            with contextlib.ExitStack() as st:
                fb = [sb(st, f"cf{i}", [128, 2048]) for i in range(3)]
                bb = [sb(st, f"cb{i}", [128, 2048], BF16) for i in range(3)]
                i = 0
                for r0 in range(0, rows, 128):
                    for c0 in range(0, cols, 2048):
                        w = min(2048, cols - c0)
                        f, b = fb[i % 3], bb[i % 3]
                        S.dma(f[:, 0:w], src2d[r0:r0 + 128, c0:c0 + w], w=[f])
                        eng = ("act", "pool", "dve")[i % 3]
                        if eng == "act":
                            S.op("act", lambda e, f=f, b=b, w=w: e.copy(out=b[:, 0:w], in_=f[:, 0:w]), r=[f], w=[b])
                        else:
                            S.op(eng, lambda e, f=f, b=b, w=w: e.tensor_copy(out=b[:, 0:w], in_=f[:, 0:w]), r=[f], w=[b])
                        S.dma(dst2d[r0:r0 + 128, c0:c0 + w], b[:, 0:w], r=[b])
                        i += 1
                S.barrier()

        def phase_ffn(l):
            cast_to_dram(I["w_out"][l], wob, D, D)
            cast_to_dram(I["w_up"][l], wub, D, DFF)
            cast_to_dram(I["w_down"][l], wdb, DFF, D)
            last = (l == L - 1)
            wov = wob.rearrange("(c p) n -> p c n", p=128)
            wuv = wub.rearrange("(c p) n -> p c n", p=128)
            wdv = wdb.rearrange("(c p) n -> p c n", p=128)
            mv = mixT.rearrange("(c p) t -> p c t", p=128)
            ov = outT.rearrange("(c p) t -> p c t", p=128)
            with contextlib.ExitStack() as st:
                g_t = load_vec_fm(st, "g_ffn", I["norm_ffn_g"][l])
                gf_t = load_vec_fm(st, "g_fin", I["norm_final_g"][0]) if last else None
                acc = sb(st, "acc", [128, KC, 512])
                hb = sb(st, "hb", [128, KC, 512], BF16)
                wbs = [sb(st, f"fw{i}", [128, KC, 512], BF16) for i in range(2)]
                act_t = [sb(st, f"fa{i}", [128, 4, 512], BF16) for i in range(2)]
                iw = 0
                ib = 0
                for tt in range(NT):
                    ts = slice(tt * 512, (tt + 1) * 512)
                    for c0 in range(0, KC, 8):
                        S.dma(acc[:, c0:c0 + 8, :], xv[:, c0:c0 + 8, ts], w=[acc])
                        S.dma(hb[:, c0:c0 + 8, :], mv[:, c0:c0 + 8, ts], w=[hb])
                    for nb in range(D // 512):
                        wb = wbs[iw % 2]
                        iw += 1
                        for c0 in range(0, KC, 8):
                            S.dma(wb[:, c0:c0 + 8, :], wov[:, c0:c0 + 8, nb * 512:(nb + 1) * 512], w=[wb])
                        for m in range(4):
                            ps = banks[ib % 4]
                            ib += 1
                            for kc in range(KC):
                                S.op("pe", lambda e, ps=ps, kc=kc, m=m, wb=wb: e.matmul(
                                    ps[:], wb[:, kc, m * 128:(m + 1) * 128], hb[:, kc, :], start=(kc == 0),
                                    stop=(kc == KC - 1)), r=[wb, hb], w=[ps])
                            mc = nb * 4 + m
                            S.op("dve", lambda e, ps=ps, mc=mc: e.tensor_tensor(out=acc[:, mc, :], in0=ps[:],
                                                                                in1=acc[:, mc, :], op=ALU.add),
                                 r=[ps, acc], w=[acc])
                    with contextlib.ExitStack() as st2:
                        S.barrier()
                        rmsnorm(st2, acc, g_t, hb, 512, "n2")
                        S.barrier()
                    for fb in range(DFF // 512):
                        wu = wbs[iw % 2]
                        iw += 1
                        for c0 in range(0, KC, 8):
                            S.dma(wu[:, c0:c0 + 8, :], wuv[:, c0:c0 + 8, fb * 512:(fb + 1) * 512], w=[wu])
                        a_t = act_t[fb % 2]
                        for m in range(4):
                            ps = banks[ib % 4]
                            ib += 1
                            for kc in range(KC):
                                S.op("pe", lambda e, ps=ps, kc=kc, m=m, wu=wu: e.matmul(
                                    ps[:], wu[:, kc, m * 128:(m + 1) * 128], hb[:, kc, :], start=(kc == 0),
                                    stop=(kc == KC - 1)), r=[wu, hb], w=[ps])
                            S.op("act", lambda e, ps=ps, m=m, a_t=a_t: e.activation(out=a_t[:, m, :], in_=ps[:],
                                                                                   func=AF.Relu), r=[ps], w=[a_t])
                        S.op("pool", lambda e, a_t=a_t: e.tensor_tensor(out=a_t[:], in0=a_t[:], in1=a_t[:],
                                                                        op=ALU.mult), r=[a_t], w=[a_t])
                        wd = wbs[iw % 2]
                        iw += 1
                        wd4 = wd[:].rearrange("p (a b) n -> p a (b n)", a=4)
                        S.dma(wd4, wdv[:, fb * 4:(fb + 1) * 4, :], w=[wd])
                        for mc in range(KC):
                            ps = banks[ib % 4]
                            ib += 1
                            for kc in range(4):
                                S.op("pe", lambda e, ps=ps, kc=kc, mc=mc, wd4=wd4, a_t=a_t: e.matmul(
                                    ps[:], wd4[:, kc, mc * 128:(mc + 1) * 128], a_t[:, kc, :], start=(kc == 0),
                                    stop=(kc == 3)), r=[wd, a_t], w=[ps])
                            S.op("dve", lambda e, ps=ps, mc=mc: e.tensor_tensor(out=acc[:, mc, :], in0=ps[:],
                                                                                in1=acc[:, mc, :], op=ALU.add),
                                 r=[ps, acc], w=[acc])
                    if last:
                        fo = sb(st, "fo", [128, 8, 512]) if tt == 0 else fo
                        with contextlib.ExitStack() as st2:
                            S.barrier()
                            sq = [sb(st2, f"sqf{i}", [128, 512], BF16) for i in range(2)]
                            rstd = sb(st2, "rstdf", [128, 512])
                            ps = banks[6]
                            for kc in range(KC):
                                q = sq[kc % 2]
                                S.op("act", lambda e, kc=kc, q=q: e.activation(out=q[:], in_=acc[:, kc, :],
                                                                               func=AF.Square), r=[acc], w=[q])
                                S.op("pe", lambda e, kc=kc, q=q: e.matmul(ps[:], ones_b[:], q[:], start=(kc == 0),
                                                                          stop=(kc == KC - 1)), r=[q, ones_b], w=[ps])
                            S.op("act", lambda e: e.activation(out=rstd[:], in_=ps[:], func=AF.Sqrt, bias=eps_t[:],
                                                               scale=1.0 / D), r=[ps, eps_t], w=[rstd])
                            S.op("dve", lambda e: e.reciprocal(out=rstd[:], in_=rstd[:]), r=[rstd], w=[rstd])
                            for c0 in range(0, KC, 8):
                                for kc in range(c0, c0 + 8):
                                    S.op("dve", lambda e, kc=kc, c0=c0: e.scalar_tensor_tensor(
                                        out=fo[:, kc - c0, :], in0=acc[:, kc, :], scalar=gf_t[:, kc:kc + 1],
                                        in1=rstd[:], op0=ALU.mult, op1=ALU.mult), r=[acc, rstd, gf_t], w=[fo])
                                S.dma(ov[:, c0:c0 + 8, ts], fo[:], r=[fo])
                            S.barrier()
                    else:
                        for c0 in range(0, KC, 8):
                            S.dma(xv[:, c0:c0 + 8, ts], acc[:, c0:c0 + 8, :], r=[acc])
                S.barrier()

        for l in range(L):
            for nm, fn in (("norm1", phase_norm1), ("proj", phase_proj), ("lora", phase_lora), ("attn", phase_attn),
                           ("dsa", phase_dsa), ("rwkv", phase_rwkv), ("ffn", phase_ffn)):
                if phases is None or nm in phases:
                    _DEAD[0] = False
                    fn(l)
                    _DEAD[0] = False
        S.barrier()
    return nc


def kernel(**inputs):
    T = inputs["x"].shape[1]
    nc = build(T=T, DEPTH=inputs["w_in"].shape[0], DFF=inputs["w_up"].shape[2])
    m = {"xT": np.ascontiguousarray(inputs["x"][0].T)}
    for k, v in inputs.items():
        if k == "x":
            continue
        a = np.asarray(v, dtype=np.float32)
        if k == "rw_mu_rkv":
            a = a.reshape(a.shape[0], -1)
        elif k == "rw_r_k":
            a = a.reshape(a.shape[0], -1)
        elif k == "norm_final_g":
            a = a.reshape(1, -1)
        m[k] = np.ascontiguousarray(a)
    res = run_bass_kernel_spmd(nc, [m], core_ids=[0])
    out = res.results[0]["outT"]
    return np.ascontiguousarray(out.T)[None].astype(np.float32)
```

```python
import contextlib
import math
import os
import numpy as np
import concourse.bass as bass
import concourse.mybir as mybir
from concourse.bass_utils import run_bass_kernel_spmd

F32 = mybir.dt.float32
BF16 = mybir.dt.bfloat16
AF = mybir.ActivationFunctionType
ALU = mybir.AluOpType
AX = mybir.AxisListType

D = 4096
KC = D // 128
D_IN = 13392
NEG = -3.0e38
DEC = -0.6065306597126334


class _Stop(Exception):
    pass


RW_STOP = int(os.environ.get('RW_STOP', '0'))


_DEAD = [False]


def _chk(k):
    if RW_STOP == k:
        _DEAD[0] = True


class Reg:
    __slots__ = ("w", "rs")

    def __init__(self):
        self.w = None
        self.rs = {}


class Tl:
    def __init__(self, t):
        self.t = t
        self.r = Reg()

    def __getitem__(self, k):
        return self.t[k]


class Sched:
    EPOCH = 16000

    def __init__(self, nc, es, ndma=16):
        self.nc = nc
        self.es = es
        self.E = {"pe": nc.tensor, "act": nc.scalar, "dve": nc.vector, "pool": nc.gpsimd, "sp": nc.sync}
        self.sems = {}
        self.ep = {k: 0 for k in self.E}
        self.cnt = {k: 0 for k in self.E}
        for k in self.E:
            self.sems[(k, 0)] = es.enter_context(nc.semaphore(f"s_{k}_0"))
        self.dsem = {i: es.enter_context(nc.semaphore(f"sd{i}")) for i in range(ndma)}
        self.did = list(range(ndma))
        self.dgen = ndma - 1
        self.ndma = ndma
        self.dcnt = [0] * ndma
        self.dnext = 0
        self.seen = {k: {} for k in self.E}
        self.rr = 0

    def _wait(self, eng, tok):
        if tok is None:
            return
        key, val = tok
        if key[0] == eng and eng == "pe":
            return
        sn = self.seen[eng]
        if key[0] == "d":
            if sn.get(key, 0) >= val:
                return
            self.E[eng].wait_ge(self.dsem[key[1]], val)
            sn[key] = val
            return
        cur = sn.get(key[0], (-1, 0))
        if cur >= (key[1], val):
            return
        self.E[eng].wait_ge(self.sems[key], val)
        sn[key[0]] = (key[1], val)

    def _deps(self, eng, r, w):
        for x in r:
            self._wait(eng, x.r.w)
        for x in w:
            self._wait(eng, x.r.w)
            for t in x.r.rs.values():
                self._wait(eng, t)

    def _mark(self, tok, r, w):
        for x in r:
            x.r.rs[tok[0][0] if tok[0][0] != "d" else tok[0]] = tok
        for x in w:
            x.r.w = tok
            x.r.rs = {}

    def op(self, eng, fn, r=(), w=()):
        if _DEAD[0]:
            return
        self._deps(eng, r, w)
        if self.cnt[eng] >= self.EPOCH:
            self.ep[eng] += 1
            self.cnt[eng] = 0
            self.sems[(eng, self.ep[eng])] = self.es.enter_context(
                self.nc.semaphore(f"s_{eng}_{self.ep[eng]}"))
        inst = fn(self.E[eng])
        self.cnt[eng] += 1
        key = (eng, self.ep[eng])
        inst.then_inc(self.sems[key], 1)
        self._mark((key, self.cnt[eng]), r, w)

    def dma(self, out, in_, r=(), w=(), q=None):
        if _DEAD[0]:
            return
        if q is None:
            q = ("sp", "act")[self.rr % 2]
            self.rr += 1
        i = self.dnext
        self.dnext = (self.dnext + 1) % self.ndma
        self._wait(q, (("d", self.did[i]), 16 * self.dcnt[i]))
        if self.dcnt[i] >= 1000:
            self.dgen += 1
            self.did[i] = self.dgen
            self.dsem[self.dgen] = self.es.enter_context(self.nc.semaphore(f"sd{self.dgen}"))
            self.dcnt[i] = 0
        self._deps(q, r, w)
        self.E[q].dma_start(out=out, in_=in_).then_inc(self.dsem[self.did[i]], 16)
        self.dcnt[i] += 1
        self._mark((("d", self.did[i]), 16 * self.dcnt[i]), r, w)

    def barrier(self, regs=()):
        for e in self.E:
            for k in self.E:
                if k != e and self.cnt[k] > 0:
                    self._wait(e, ((k, self.ep[k]), self.cnt[k]))
            for i in range(self.ndma):
                if self.dcnt[i]:
                    self._wait(e, (("d", self.did[i]), 16 * self.dcnt[i]))


def build(T=8192, DEPTH=2, DFF=16384, debug=(), phases=None):
    nc = bass.Bass("TRN2", target_bir_lowering=False)
    NT = T // 512
    NQ = T // 128
    L = DEPTH

    def din(name, shape, dt=F32):
        return nc.dram_tensor(name, list(shape), dt, kind="ExternalInput").ap()

    def dscr(name, shape, dt=F32):
        kind = "ExternalOutput" if name in debug else "Internal"
        return nc.dram_tensor(name, list(shape), dt, kind=kind).ap()

    xT_in = din("xT", [D, T])
    I = {}
    Lv = max(L - 1, 1)
    for nm, sh in [("norm_mix_g", [L, D]), ("w_in", [L, D, D_IN]), ("lam_q1", [L, 64]), ("lam_k1", [L, 64]),
                   ("lam_q2", [L, 64]), ("lam_k2", [L, 64]), ("diff_subln_g", [L, 128]),
                   ("rw_mu_rkv", [L, 6144]), ("rw_mu_wag", [L, 3, D]), ("rw_w0", [L, 2048]),
                   ("rw_w1", [L, D, 96]), ("rw_w2", [L, 96, 2048]), ("rw_a0", [L, 2048]),
                   ("rw_a1", [L, D, 96]), ("rw_a2", [L, 96, 2048]), ("rw_g1", [L, D, 256]),
                   ("rw_g2", [L, 256, 2048]), ("rw_k_k", [L, 2048]), ("rw_k_a", [L, 2048]),
                   ("rw_r_k", [L, 2048]), ("rw_ln_w", [L, 2048]), ("rw_ln_b", [L, 2048]),
                   ("rw_v_mu", [Lv, D]), ("rw_v0", [Lv, 2048]),
                   ("rw_v1", [Lv, D, 64]), ("rw_v2", [Lv, 64, 2048]),
                   ("w_out", [L, D, D]), ("norm_ffn_g", [L, D]), ("w_up", [L, D, DFF]),
                   ("w_down", [L, DFF, D]), ("norm_final_g", [1, D])]:
        I[nm] = din(nm, sh)
    outT = nc.dram_tensor("outT", [D, T], F32, kind="ExternalOutput").ap()

    xT = dscr("xT_s", [D, T])
    hT = dscr("hT_s", [D, T + 1], BF16)
    qaT = dscr("qaT", [2048, T], BF16)
    va = dscr("va", [T, 1024], BF16)
    rkv = dscr("rkv", [T + 1, 6144])
    qcT = dscr("qcT", [2048, T], BF16)
    vc = dscr("vc", [T, 1024], BF16)
    qiT = dscr("qiT", [1088, T], BF16)
    wi = dscr("wi", [T, 16])
    lo_w = dscr("lo_w", [97, T], BF16)
    lo_a = dscr("lo_a", [97, T], BF16)
    lo_g = dscr("lo_g", [256, T], BF16)
    lo_v = dscr("lo_v", [65, T], BF16)
    vfirst = dscr("vfirst", [T, 2048])
    mixT = dscr("mixT", [D, T], BF16)
    wob = dscr("wob", [D, D], BF16)
    wub = dscr("wub", [D, DFF], BF16)
    wdb = dscr("wdb", [DFF, D], BF16)

    es = contextlib.ExitStack()
    with es:
        es.enter_context(nc.allow_non_contiguous_dma(reason='small strided scratch DMAs'))
        S = Sched(nc, es)
        NEG_R = nc.gpsimd.to_reg(NEG)
        ZERO_R = nc.gpsimd.to_reg(0.0)
        uid = [0]

        def sb(st, name, shape, dt=F32):
            uid[0] += 1
            return Tl(st.enter_context(nc.sbuf_tensor(f"{name}_{uid[0]}", list(shape), dt)))

        banks = [Tl(es.enter_context(nc.psum_tensor(f"pb{i}", [128, 512], F32))) for i in range(7)]
        bank_bf = Tl(es.enter_context(nc.psum_tensor("pbb", [128, 1024], BF16)))

        ones_f = sb(es, "ones_f", [128, 512])
        ones_b = sb(es, "ones_b", [128, 128], BF16)
        ident_f = sb(es, "ident_f", [128, 128])
        ident_b = sb(es, "ident_b", [128, 128], BF16)
        eps_t = sb(es, "eps_t", [128, 1])
        S.op("pool", lambda e: e.memset(ones_f[:], 1.0), w=[ones_f])
        S.op("pool", lambda e: e.memset(ones_b[:], 1.0), w=[ones_b])
        S.op("pool", lambda e: e.memset(eps_t[:], 1e-6), w=[eps_t])
        S.op("pool", lambda e: e.affine_select(out=ident_f[:], in_=ones_f[:, 0:128], pattern=[[1, 128]],
                                               compare_op=ALU.is_equal, fill=ZERO_R, base=0,
                                               channel_multiplier=-1), r=[ones_f], w=[ident_f])
        S.op("pool", lambda e: e.tensor_copy(out=ident_b[:], in_=ident_f[:]), r=[ident_f], w=[ident_b])
        cmask = sb(es, "cmask", [128, 4, 512], BF16)
        for j in range(4):
            S.op("pool", lambda e, j=j: e.affine_select(out=cmask[:, j, :], in_=ones_f[:], pattern=[[1, 512]],
                                                        compare_op=ALU.is_ge, fill=ZERO_R, base=-128 * j,
                                                        channel_multiplier=-1), r=[ones_f], w=[cmask])
        cnt = [0]

        def evac(out_t, out_ap, in_t, in_ap, func=None, scale=None):
            cnt[0] += 1
            if func is not None:
                kw = {} if scale is None else {"scale": scale}
                S.op("act", lambda e: e.activation(out=out_ap, in_=in_ap, func=func, **kw), r=[in_t], w=[out_t])
            elif cnt[0] % 2:
                S.op("act", lambda e: e.copy(out=out_ap, in_=in_ap), r=[in_t], w=[out_t])
            else:
                S.op("dve", lambda e: e.tensor_copy(out=out_ap, in_=in_ap), r=[in_t], w=[out_t])

        def rmsnorm(st, src, g_t, dst, n, tag):
            sq = [sb(st, f"sq{tag}{i}", [128, n], BF16) for i in range(2)]
            rstd = sb(st, "rstd" + tag, [128, n])
            ps = banks[6]
            for kc in range(KC):
                q = sq[kc % 2]
                S.op("act", lambda e, kc=kc, q=q: e.activation(out=q[:], in_=src[:, kc, 0:n], func=AF.Square),
                     r=[src], w=[q])
                S.op("pe", lambda e, kc=kc, q=q: e.matmul(ps[:, 0:n], ones_b[:], q[:], start=(kc == 0),
                                                          stop=(kc == KC - 1)), r=[q, ones_b], w=[ps])
            S.op("act", lambda e: e.activation(out=rstd[:], in_=ps[:, 0:n], func=AF.Sqrt,
                                               bias=eps_t[:], scale=1.0 / D), r=[ps, eps_t], w=[rstd])
            S.op("dve", lambda e: e.reciprocal(out=rstd[:], in_=rstd[:]), r=[rstd], w=[rstd])
            for kc in range(KC):
                S.op("dve", lambda e, kc=kc: e.scalar_tensor_tensor(
                    out=dst[:, kc, 0:n], in0=src[:, kc, 0:n], scalar=g_t[:, kc:kc + 1], in1=rstd[:],
                    op0=ALU.mult, op1=ALU.mult), r=[src, rstd, g_t], w=[dst])

        def load_vec_fm(st, name, ap_row):
            t = sb(st, name, [128, KC])
            S.dma(t[:], ap_row.rearrange("(c p) -> p c", p=128), w=[t])
            return t

        def load_bc(st, name, ap_row2d, n, dt=F32):
            t = sb(st, name, [128, n], dt)
            S.dma(t[:], ap_row2d.to_broadcast([128, n]), w=[t], q="pool" if dt != F32 else None)
            return t

        hv = hT.rearrange("(c p) t -> p c t", p=128)
        xv = xT.rearrange("(c p) t -> p c t", p=128)

        with contextlib.ExitStack() as st:
            buf = [sb(st, f"cpx{i}", [128, 4096]) for i in range(2)]
            ones_row = sb(st, "ones_row", [1, T], BF16)
            zero_f = sb(st, "zero_f", [1, 6144])
            S.op("pool", lambda e: e.memset(ones_row[:], 1.0), w=[ones_row])
            S.op("pool", lambda e: e.memset(zero_f[:], 0.0), w=[zero_f])
            xv_in = xT_in.rearrange("(c p) t -> p c t", p=128)
            i = 0
            for c in range(KC):
                for t0 in range(0, T, 4096):
                    n = min(4096, T - t0)
                    b = buf[i % 2]
                    i += 1
                    S.dma(b[:, 0:n], xv_in[:, c, t0:t0 + n], w=[b])
                    S.dma(xv[:, c, t0:t0 + n], b[:, 0:n], r=[b])
            S.dma(rkv[0:1, :], zero_f[0:1, :], r=[zero_f])
            zb = sb(st, "zb", [128, KC], BF16)
            S.op("pool", lambda e: e.memset(zb[:], 0.0), w=[zb])
            S.dma(hv[:, :, 0:1], zb[:].rearrange("p (c o) -> p c o", o=1), r=[zb])
            S.dma(lo_w[96:97, :], ones_row[:], r=[ones_row])
            S.dma(lo_a[96:97, :], ones_row[:], r=[ones_row])
            S.dma(lo_v[64:65, :], ones_row[:], r=[ones_row])
            S.barrier()

        def phase_norm1(l):
            with contextlib.ExitStack() as st:
                g_t = load_vec_fm(st, "g_mix", I["norm_mix_g"][l])
                x_t = sb(st, "xs", [128, KC, 512])
                hs = [sb(st, f"hs{i}", [128, KC, 512], BF16) for i in range(2)]
                for tt in range(NT):
                    h_t = hs[tt % 2]
                    for c0 in range(0, KC, 8):
                        S.dma(x_t[:, c0:c0 + 8, :], xv[:, c0:c0 + 8, tt * 512:(tt + 1) * 512], w=[x_t])
                    with contextlib.ExitStack() as st2:
                        rmsnorm(st2, x_t, g_t, h_t, 512, "n1")
                        S.barrier()
                    for c0 in range(0, KC, 8):
                        S.dma(hv[:, c0:c0 + 8, 1 + tt * 512:1 + (tt + 1) * 512], h_t[:, c0:c0 + 8, :], r=[h_t])
                S.barrier()

        def load_wblock(stg, wb, src_ap, w):
            sv = src_ap.rearrange("(c p) n -> p c n", p=128)
            for i, c0 in enumerate(range(0, KC, 8)):
                s = stg[i % 2]
                S.dma(s[:, :, 0:w], sv[:, c0:c0 + 8, :], w=[s])
                eng = ("act", "pool")[i % 2]
                if eng == "act":
                    S.op("act", lambda e, s=s, c0=c0: e.copy(out=wb[:, c0:c0 + 8, 0:w], in_=s[:, :, 0:w]),
                         r=[s], w=[wb])
                else:
                    S.op("pool", lambda e, s=s, c0=c0: e.tensor_copy(out=wb[:, c0:c0 + 8, 0:w], in_=s[:, :, 0:w]),
                         r=[s], w=[wb])

        def phase_proj(l):
            blocks = []
            for c0 in range(0, 2048, 512):
                blocks.append(("fm", c0, 512, qaT, c0, BF16))
            for c0 in range(2048, 3072, 512):
                blocks.append(("tm", c0, 512, va, (0, c0 - 2048), BF16))
            for c0 in range(3072, 9216, 512):
                blocks.append(("tm", c0, 512, rkv, (1, c0 - 3072), F32))
            for c0 in range(9216, 11264, 512):
                blocks.append(("fm", c0, 512, qcT, c0 - 9216, BF16))
            for c0 in range(11264, 12288, 512):
                blocks.append(("tm", c0, 512, vc, (0, c0 - 11264), BF16))
            blocks.append(("fm", 12288, 512, qiT, 0, BF16))
            blocks.append(("fm", 12800, 512, qiT, 512, BF16))
            blocks.append(("fm", 13312, 64, qiT, 1024, BF16))
            blocks.append(("tm", 13376, 16, wi, (0, 0), F32))
            with contextlib.ExitStack() as st:
                stg = [sb(st, f"stg{i}", [128, 8, 512]) for i in range(2)]
                wbs = [sb(st, f"wb{i}", [128, KC, 512], BF16) for i in range(2)]
                hts = [sb(st, f"ht{i}", [128, KC, 512], BF16) for i in range(2)]
                obf = [sb(st, f"obf{i}", [128, 512]) for i in range(3)]
                obb = [sb(st, f"obb{i}", [128, 512], BF16) for i in range(3)]
                ib = 0
                io = 0
                ih = 0
                for bi, (lay, c0, w, dst, off, dt) in enumerate(blocks):
                    wb = wbs[bi % 2]
                    load_wblock(stg, wb, I["w_in"][l][:, c0:c0 + w], w)
                    for tt in range(NT):
                        ht = hts[ih % 2]
                        ih += 1
                        for k0 in range(0, KC, 8):
                            S.dma(ht[:, k0:k0 + 8, :], hv[:, k0:k0 + 8, 1 + tt * 512:1 + tt * 512 + 512], w=[ht])
                        if lay == "fm":
                            for m in range((w + 127) // 128):
                                mw = min(128, w - m * 128)
                                ps = banks[ib % 4]
                                ib += 1
                                for kc in range(KC):
                                    S.op("pe", lambda e, kc=kc, ps=ps, m=m, mw=mw, ht=ht: e.matmul(
                                        ps[0:mw, :], wb[:, kc, m * 128:m * 128 + mw], ht[:, kc, :],
                                        start=(kc == 0), stop=(kc == KC - 1)), r=[wb, ht], w=[ps])
                                ob = (obb if dt == BF16 else obf)[io % 3]
                                io += 1
                                evac(ob, ob[0:mw, :], ps, ps[0:mw, :])
                                S.dma(dst[off + m * 128:off + m * 128 + mw, tt * 512:(tt + 1) * 512], ob[0:mw, :],
                                      r=[ob])
                        else:
                            for ts in range(4):
                                ps = banks[ib % 4]
                                ib += 1
                                for kc in range(KC):
                                    S.op("pe", lambda e, kc=kc, ps=ps, ts=ts, ht=ht: e.matmul(
                                        ps[:, 0:w], ht[:, kc, ts * 128:(ts + 1) * 128], wb[:, kc, 0:w],
                                        start=(kc == 0), stop=(kc == KC - 1)), r=[wb, ht], w=[ps])
                                ob = (obb if dt == BF16 else obf)[io % 3]
                                io += 1
                                evac(ob, ob[:, 0:w], ps, ps[:, 0:w])
                                r0 = off[0] + tt * 512 + ts * 128
                                S.dma(dst[r0:r0 + 128, off[1]:off[1] + w], ob[:, 0:w], r=[ob])
                S.barrier()

        def phase_lora(l):
            with contextlib.ExitStack() as st:
                WL = sb(st, "WL", [128, KC, 512], BF16)
                WLm = sb(st, "WLm", [128, KC, 512], BF16)
                specs = [("rw_w1", l, 0, 96, I["rw_mu_wag"][l][0]), ("rw_a1", l, 96, 96, I["rw_mu_wag"][l][1]),
                         ("rw_g1", l, 192, 256, I["rw_mu_wag"][l][2])]
                if l > 0:
                    specs.append(("rw_v1", l - 1, 448, 64, I["rw_v_mu"][l - 1]))
                mus = [load_vec_fm(st, "mu" + sp[0], sp[4]) for sp in specs]
                st_w = contextlib.ExitStack()
                wf = sb(st_w, "WLf", [128, KC, 256])
                for (nm, li, cs, w, mu_ap), mu in zip(specs, mus):
                    S.dma(wf[:, :, 0:w], I[nm][li].rearrange("(c p) n -> p c n", p=128), w=[wf])
                    S.op("act", lambda e, cs=cs, w=w: e.copy(out=WL[:, :, cs:cs + w], in_=wf[:, :, 0:w]),
                         r=[wf], w=[WL])
                    for kc in range(KC):
                        S.op("dve", lambda e, kc=kc, cs=cs, w=w, mu=mu: e.tensor_scalar(
                            out=WLm[:, kc, cs:cs + w], in0=wf[:, kc, 0:w], scalar1=mu[:, kc:kc + 1],
                            scalar2=None, op0=ALU.mult), r=[wf, mu], w=[WLm])
                S.barrier()
                st_w.close()
                outs = [(0, 96, AF.Tanh, lo_w, 0), (96, 96, None, lo_a, 0), (192, 128, AF.Sigmoid, lo_g, 0),
                        (320, 128, AF.Sigmoid, lo_g, 128)]
                if l > 0:
                    outs.append((448, 64, None, lo_v, 0))
                hts = [sb(st, f"lht{i}", [128, KC, 512], BF16) for i in range(1)]
                hp_t = sb(st, "lhp", [128, KC, 512], BF16)
                dh = sb(st, "dh", [128, KC, 512], BF16)
                obb = [sb(st, f"lob{i}", [128, 512], BF16) for i in range(3)]
                io = 0
                for tt in range(NT):
                    ht = hts[0]
                    for k0 in range(0, KC, 8):
                        S.dma(ht[:, k0:k0 + 8, :], hv[:, k0:k0 + 8, 1 + tt * 512:1 + tt * 512 + 512], w=[ht])
                        S.dma(hp_t[:, k0:k0 + 8, :], hv[:, k0:k0 + 8, tt * 512:tt * 512 + 512], w=[hp_t])
                    S.op("dve", lambda e, ht=ht: e.tensor_tensor(out=dh[:], in0=hp_t[:], in1=ht[:],
                                                                 op=ALU.subtract), r=[ht, hp_t], w=[dh])
                    for oi, (cs, mw, fn, dst, ro) in enumerate(outs):
                        ps = banks[oi % 4]
                        for kc in range(KC):
                            S.op("pe", lambda e, kc=kc, ps=ps, cs=cs, mw=mw, ht=ht: e.matmul(
                                ps[0:mw, :], WL[:, kc, cs:cs + mw], ht[:, kc, :], start=(kc == 0), stop=False),
                                r=[WL, ht], w=[ps])
                        for kc in range(KC):
                            S.op("pe", lambda e, kc=kc, ps=ps, cs=cs, mw=mw: e.matmul(
                                ps[0:mw, :], WLm[:, kc, cs:cs + mw], dh[:, kc, :], start=False, stop=(kc == KC - 1)),
                                r=[WLm, dh], w=[ps])
                        ob = obb[io % 3]
                        io += 1
                        evac(ob, ob[0:mw, :], ps, ps[0:mw, :], func=fn)
                        S.dma(dst[ro:ro + mw, tt * 512:(tt + 1) * 512], ob[0:mw, :], r=[ob])
                S.barrier()
        def phase_attn(l):
            lam_init = 0.8 - 0.6 * math.exp(-0.3 * l)
            with contextlib.ExitStack() as st:
                lt = [load_bc(st, f"lam{i}", I[n][l:l + 1, :], 64) for i, n in
                      enumerate(("lam_q1", "lam_k1", "lam_q2", "lam_k2"))]
                t1 = sb(st, "lt1", [128, 64])
                ss = sb(st, "lss", [128, 2])
                nlam = sb(st, "nlam", [128, 1])
                for j in range(2):
                    S.op("dve", lambda e, j=j: e.tensor_tensor(out=t1[:], in0=lt[2 * j][:], in1=lt[2 * j + 1][:],
                                                               op=ALU.mult), r=[lt[2 * j], lt[2 * j + 1]], w=[t1])
                    S.op("dve", lambda e, j=j: e.tensor_reduce(out=ss[:, j:j + 1], in_=t1[:], axis=AX.X, op=ALU.add),
                         r=[t1], w=[ss])
                S.op("act", lambda e: e.activation(out=ss[:], in_=ss[:], func=AF.Exp), r=[ss], w=[ss])
                S.op("dve", lambda e: e.tensor_tensor(out=nlam[:], in0=ss[:, 1:2], in1=ss[:, 0:1], op=ALU.subtract),
                     r=[ss], w=[nlam])
                S.op("dve", lambda e: e.tensor_scalar(out=nlam[:], in0=nlam[:], scalar1=-lam_init, scalar2=None,
                                                      op0=ALU.add), r=[nlam], w=[nlam])
                gsub = sb(st, "gsub", [128, 1])
                S.dma(gsub[:], I["diff_subln_g"][l].rearrange("(p o) -> p o", o=1), w=[gsub])
                S.op("dve", lambda e: e.tensor_scalar(out=gsub[:], in0=gsub[:], scalar1=1.0 - lam_init, scalar2=None,
                                                      op0=ALU.mult), r=[gsub], w=[gsub])
                QT = [sb(st, f"aq{i}", [128, T], BF16) for i in range(2)]
                KT = [sb(st, f"ak{i}", [128, T], BF16) for i in range(2)]
                V = [sb(st, f"av{i}", [128, NQ, 128], BF16) for i in range(2)]
                P = [sb(st, f"ap{i}", [128, 512], BF16) for i in range(4)]
                accs = [sb(st, f"aacc{i}", [128, 512]) for i in range(2)]
                rs = sb(st, "ars", [128, 512])
                om = [sb(st, f"aom{i}", [128, 512]) for i in range(2)]
                osq = sb(st, "aosq", [128, 512])
                ob = [sb(st, f"aob{i}", [128, 512], BF16) for i in range(2)]
                ip = 0
                for h in range(8):
                    q, k, v = QT[h % 2], KT[h % 2], V[h % 2]
                    S.dma(q[:], qaT[h * 128:(h + 1) * 128, :], w=[q])
                    S.dma(k[:], qaT[1024 + h * 128:1024 + (h + 1) * 128, :], w=[k])
                    S.dma(v[:], va[:, h * 128:(h + 1) * 128].rearrange("(k p) e -> p k e", p=128), w=[v])
                    for qt in range(NT):
                        nkt = 4 * (qt + 1)
                        for m in range(2):
                            O, SM = banks[4 + m], banks[6]
                            pr = slice(64 * m, 64 * m + 64)
                            acc = accs[m]

                            def issue_S(kt, pr=pr):
                                ps = banks[kt % 4]
                                S.op("pe", lambda e, ps=ps, kt=kt, pr=pr: e.matmul(
                                    ps[:], k[pr, kt * 128:(kt + 1) * 128], q[pr, qt * 512:(qt + 1) * 512],
                                    start=True, stop=True), r=[k, q], w=[ps])
                            for kt in range(min(2, nkt)):
                                issue_S(kt)
                            for kt in range(nkt):
                                if kt + 2 < nkt:
                                    issue_S(kt + 2)
                                ps = banks[kt % 4]
                                pb = P[ip % 4]
                                ip += 1
                                S.op("act", lambda e, ps=ps, pb=pb: e.activation(out=pb[:], in_=ps[:], func=AF.Exp,
                                                                                 scale=0.125), r=[ps], w=[pb])
                                if kt >= 4 * qt:
                                    S.op("dve", lambda e, pb=pb, kt=kt: e.tensor_tensor(
                                        out=pb[:], in0=pb[:], in1=cmask[:, kt - 4 * qt, :], op=ALU.mult),
                                        r=[pb, cmask], w=[pb])
                                S.op("pe", lambda e, pb=pb, kt=kt, O=O: e.matmul(
                                    O[:], v[:, kt, :], pb[:], start=(kt == 0), stop=(kt == nkt - 1)), r=[v, pb], w=[O])
                                if kt == 0:
                                    S.op("dve", lambda e, pb=pb, acc=acc: e.tensor_copy(out=acc[:], in_=pb[:]),
                                         r=[pb], w=[acc])
                                else:
                                    S.op("dve", lambda e, pb=pb, acc=acc: e.tensor_tensor(
                                        out=acc[:], in0=acc[:], in1=pb[:], op=ALU.add), r=[pb, acc], w=[acc])
                            S.op("pe", lambda e, SM=SM, acc=acc: e.matmul(SM[:], ones_f[:, 0:128], acc[:], start=True,
                                                                         stop=True), r=[ones_f, acc], w=[SM])
                            S.op("dve", lambda e, SM=SM: e.reciprocal(out=rs[:], in_=SM[:]), r=[SM], w=[rs])
                            S.op("dve", lambda e, O=O, m=m: e.tensor_tensor(out=om[m][:], in0=O[:], in1=rs[:],
                                                                            op=ALU.mult), r=[O, rs], w=[om[m]])
                        S.op("dve", lambda e: e.scalar_tensor_tensor(out=om[0][:], in0=om[1][:], scalar=nlam[:, 0:1],
                                                                     in1=om[0][:], op0=ALU.mult, op1=ALU.add),
                             r=[om[1], nlam, om[0]], w=[om[0]])
                        S.op("act", lambda e: e.activation(out=osq[:], in_=om[0][:], func=AF.Square), r=[om[0]], w=[osq])
                        pn = banks[6]
                        S.op("pe", lambda e: e.matmul(pn[:], ones_f[:, 0:128], osq[:], start=True, stop=True),
                             r=[ones_f, osq], w=[pn])
                        S.op("act", lambda e: e.activation(out=rs[:], in_=pn[:], func=AF.Sqrt, bias=eps_t[:],
                                                           scale=1.0 / 128), r=[pn, eps_t], w=[rs])
                        S.op("dve", lambda e: e.reciprocal(out=rs[:], in_=rs[:]), r=[rs], w=[rs])
                        o = ob[qt % 2]
                        S.op("dve", lambda e, o=o: e.scalar_tensor_tensor(out=o[:], in0=om[0][:], scalar=gsub[:, 0:1],
                                                                          in1=rs[:], op0=ALU.mult, op1=ALU.mult),
                             r=[om[0], gsub, rs], w=[o])
                        S.dma(mixT[h * 128:(h + 1) * 128, qt * 512:(qt + 1) * 512], o[:], r=[o])
                S.barrier()

        def phase_dsa(l):
            with contextlib.ExitStack() as st:
                kiT = sb(st, "kiT", [64, T], BF16)
                S.dma(kiT[:], qiT[1024:1088, :], w=[kiT])
                selT = sb(st, "selT", [128, NQ, 512], mybir.dt.uint8)
                sc = sb(st, "sc", [128, T])
                wk = sb(st, "wk", [128, T])
                sel = sb(st, "sel", [128, T], BF16)
                qi_r = sb(st, "qi_r", [64, 16, 128], BF16)
                wi_r = sb(st, "wi_r", [128, 16])
                wabs = sb(st, "wabs", [128, 16])
                wsgn = sb(st, "wsgn", [128, 16])
                tmp = [sb(st, f"dtmp{i}", [128, 512]) for i in range(2)]
                m8 = sb(st, "m8", [128, 8])
                QT = [sb(st, f"cq{i}", [128, 512], BF16) for i in range(2)]
                KT = [sb(st, f"ck{i}", [128, T], BF16) for i in range(1)]
                V = [sb(st, f"cv{i}", [128, NQ, 128], BF16) for i in range(1)]
                P = [sb(st, f"cp{i}", [128, 512], BF16) for i in range(4)]
                cacc = sb(st, "cacc", [128, 512])
                rs = sb(st, "crs", [128, 512])
                ob = [sb(st, f"cob{i}", [128, 512], BF16) for i in range(2)]
                ip = 0
                for qt in range(NT):
                    nkq = 4 * (qt + 1)
                    S.op("pool", lambda e: e.memset(selT[:, 4 * qt:4 * qt + 4, :], 0.0), w=[selT])
                    for r in range(4):
                        g = 4 * qt + r
                        nk = 128 * (g + 1)
                        S.dma(qi_r[:], qiT[0:1024, g * 128:(g + 1) * 128].rearrange("(h d) t -> d h t", d=64), w=[qi_r])
                        S.dma(wi_r[:], wi[g * 128:(g + 1) * 128, :], w=[wi_r])
                        S.op("act", lambda e: e.activation(out=wabs[:], in_=wi_r[:], func=AF.Abs), r=[wi_r], w=[wabs])
                        S.op("act", lambda e: e.activation(out=wsgn[:], in_=wi_r[:], func=AF.Sign), r=[wi_r], w=[wsgn])
                        for kb in range((nk + 511) // 512):
                            kw = min(512, nk - kb * 512)
                            ks = slice(kb * 512, kb * 512 + kw)
                            for ih in range(16):
                                ps = banks[ih % 2]
                                tm = tmp[ih % 2]
                                S.op("pe", lambda e, ps=ps, ih=ih, ks=ks, kw=kw: e.matmul(
                                    ps[:, 0:kw], qi_r[:, ih, :], kiT[:, ks], start=True, stop=True),
                                    r=[qi_r, kiT], w=[ps])
                                S.op("act", lambda e, ps=ps, tm=tm, ih=ih, kw=kw: e.activation(
                                    out=tm[:, 0:kw], in_=ps[:, 0:kw], func=AF.Relu, scale=wabs[:, ih:ih + 1]),
                                    r=[ps, wabs], w=[tm])
                                if ih == 0:
                                    S.op("dve", lambda e, tm=tm, ks=ks, kw=kw: e.tensor_scalar(
                                        out=sc[:, ks], in0=tm[:, 0:kw], scalar1=wsgn[:, 0:1], scalar2=None,
                                        op0=ALU.mult), r=[tm, wsgn], w=[sc])
                                else:
                                    S.op("dve", lambda e, tm=tm, ks=ks, kw=kw, ih=ih: e.scalar_tensor_tensor(
                                        out=sc[:, ks], in0=tm[:, 0:kw], scalar=wsgn[:, ih:ih + 1], in1=sc[:, ks],
                                        op0=ALU.mult, op1=ALU.add), r=[tm, wsgn, sc], w=[sc])
                        dg = slice(g * 128, (g + 1) * 128)
                        S.op("pool", lambda e, dg=dg: e.affine_select(
                            out=sc[:, dg], in_=sc[:, dg], pattern=[[-1, 128]], compare_op=ALU.is_ge, fill=NEG_R,
                            base=0, channel_multiplier=1), r=[sc], w=[sc])
                        if g >= 2:
                            S.op("dve", lambda e, nk=nk: e.max(out=m8[:], in_=sc[:, 0:nk]), r=[sc], w=[m8])
                            S.op("dve", lambda e, nk=nk: e.match_replace(out=wk[:, 0:nk], in_to_replace=m8[:],
                                                                         in_values=sc[:, 0:nk], imm_value=NEG),
                                 r=[sc, m8], w=[wk])
                            for it in range(1, 32):
                                S.op("dve", lambda e, nk=nk: e.max(out=m8[:], in_=wk[:, 0:nk]), r=[wk], w=[m8])
                                if it < 31:
                                    S.op("dve", lambda e, nk=nk: e.match_replace(
                                        out=wk[:, 0:nk], in_to_replace=m8[:], in_values=wk[:, 0:nk], imm_value=NEG),
                                        r=[wk, m8], w=[wk])
                            S.op("dve", lambda e, nk=nk: e.tensor_scalar(out=sel[:, 0:nk], in0=sc[:, 0:nk],
                                                                         scalar1=m8[:, 7:8], scalar2=None,
                                                                         op0=ALU.is_ge), r=[sc, m8], w=[sel])
                        else:
                            S.op("dve", lambda e, nk=nk: e.tensor_scalar(out=sel[:, 0:nk], in0=sc[:, 0:nk],
                                                                         scalar1=-1.0e38, scalar2=None,
                                                                         op0=ALU.is_ge), r=[sc], w=[sel])
                        for k0 in range(0, g + 1, 8):
                            nb = min(8, g + 1 - k0)
                            for j in range(nb):
                                S.op("pe", lambda e, j=j, k0=k0: e.transpose(
                                    bank_bf[:, j * 128:(j + 1) * 128], sel[:, (k0 + j) * 128:(k0 + j + 1) * 128],
                                    ident_b[:]), r=[sel, ident_b], w=[bank_bf])
                            evac(selT, selT[:, k0:k0 + nb, r * 128:(r + 1) * 128], bank_bf,
                                 bank_bf[:, 0:nb * 128].rearrange("p (k q) -> p k q", q=128))
                    for h in range(8):
                        q, k, v = QT[h % 2], KT[0], V[0]
                        S.dma(q[:], qcT[h * 128:(h + 1) * 128, qt * 512:(qt + 1) * 512], w=[q])
                        S.dma(k[:, 0:nkq * 128], qcT[1024 + h * 128:1024 + (h + 1) * 128, 0:nkq * 128], w=[k])
                        S.dma(v[:, 0:nkq, :], vc[0:nkq * 128, h * 128:(h + 1) * 128].rearrange("(k p) e -> p k e", p=128),
                              w=[v])
                        O, SM = banks[4], banks[5]

                        def issue_S(kt):
                            ps = banks[kt % 4]
                            S.op("pe", lambda e, ps=ps, kt=kt: e.matmul(ps[:], k[:, kt * 128:(kt + 1) * 128], q[:],
                                                                        start=True, stop=True), r=[k, q], w=[ps])
                        for kt in range(min(2, nkq)):
                            issue_S(kt)
                        for kt in range(nkq):
                            if kt + 2 < nkq:
                                issue_S(kt + 2)
                            ps = banks[kt % 4]
                            pb = P[ip % 4]
                            ip += 1
                            S.op("act", lambda e, ps=ps, pb=pb: e.activation(out=pb[:], in_=ps[:], func=AF.Exp,
                                                                             scale=128 ** -0.5), r=[ps], w=[pb])
                            S.op("dve", lambda e, pb=pb, kt=kt: e.tensor_tensor(out=pb[:], in0=pb[:], in1=selT[:, kt, :],
                                                                                op=ALU.mult), r=[pb, selT], w=[pb])
                            S.op("pe", lambda e, pb=pb, kt=kt: e.matmul(O[:], v[:, kt, :], pb[:], start=(kt == 0),
                                                                        stop=(kt == nkq - 1)), r=[v, pb], w=[O])
                            if kt == 0:
                                S.op("pool", lambda e, pb=pb: e.tensor_copy(out=cacc[:], in_=pb[:]), r=[pb], w=[cacc])
                            else:
                                S.op("pool", lambda e, pb=pb: e.tensor_tensor(out=cacc[:], in0=cacc[:], in1=pb[:],
                                                                              op=ALU.add), r=[pb, cacc], w=[cacc])
                        S.op("pe", lambda e: e.matmul(SM[:], ones_f[:, 0:128], cacc[:], start=True, stop=True),
                             r=[ones_f, cacc], w=[SM])
                        S.op("dve", lambda e: e.reciprocal(out=rs[:], in_=SM[:]), r=[SM], w=[rs])
                        o = ob[h % 2]
                        S.op("dve", lambda e, o=o: e.tensor_tensor(out=o[:], in0=O[:], in1=rs[:], op=ALU.mult),
                             r=[O, rs], w=[o])
                        S.dma(mixT[3072 + h * 128:3072 + (h + 1) * 128, qt * 512:(qt + 1) * 512], o[:], r=[o])
                S.barrier()
        def phase_rwkv(l):
            HC = 1024
            with contextlib.ExitStack() as st:
                def tri(name, pat, base, cm, val):
                    t = sb(st, name, [128, 128])
                    S.op("pool", lambda e: e.memset(t[:], val), w=[t])
                    S.op("pool", lambda e: e.affine_select(out=t[:], in_=t[:], pattern=pat, compare_op=ALU.is_ge,
                                                           fill=ZERO_R, base=base, channel_multiplier=cm), r=[t], w=[t])
                    return t
                triI = tri("triI", [[1, 128]], 0, -1, DEC)
                triS = tri("triS", [[1, 128]], -1, -1, DEC)
                triA = tri("triA", [[-1, 128]], -1, 1, DEC)
                negcol = sb(st, "negcol", [128, 1])
                S.op("pool", lambda e: e.memset(negcol[:], DEC), w=[negcol])
                epsg = sb(st, "epsg", [128, 1])
                S.op("pool", lambda e: e.memset(epsg[:], 64e-5), w=[epsg])

                def mask4(name, pat, base, cm):
                    t = sb(st, name, [128, 4, 128])
                    S.op("pool", lambda e: e.memset(t[:], 1.0), w=[t])
                    for j in range(4):
                        S.op("pool", lambda e, j=j: e.affine_select(out=t[:, j, :], in_=t[:, j, :], pattern=pat,
                                                                    compare_op=ALU.is_ge, fill=ZERO_R, base=base,
                                                                    channel_multiplier=cm), r=[t], w=[t])
                    return t
                msS = mask4("msS", [[1, 128]], -1, -1)
                msI = mask4("msI", [[1, 128]], 0, -1)
                msT = mask4("msT", [[-1, 128]], -1, 1)
                id4 = sb(st, "id4", [128, 4, 128])
                for j in range(4):
                    S.op("pool", lambda e, j=j: e.tensor_copy(out=id4[:, j, :], in_=ident_f[:]), r=[ident_f], w=[id4])
                def lw2(name, wnm, li, bnm, K):
                    t = sb(st, name, [K + 1, 2048], BF16)
                    S.dma(t[0:K, :], I[wnm][li], w=[t], q="pool")
                    if bnm is not None:
                        S.dma(t[K:K + 1, :], I[bnm][li:li + 1, :], w=[t], q="pool")
                    return t
                w2e = lw2("w2e", "rw_w2", l, "rw_w0", 96)
                a2e = lw2("a2e", "rw_a2", l, "rw_a0", 96)
                g2a = sb(st, "g2a", [128, 2048], BF16)
                g2b = sb(st, "g2b", [128, 2048], BF16)
                S.dma(g2a[0:128, :], I["rw_g2"][l][0:128, :], w=[g2a], q="pool")
                S.dma(g2b[:], I["rw_g2"][l][128:256, :], w=[g2b], q="pool")
                v2e = lw2("v2e", "rw_v2", l - 1, "rw_v0", 64) if l > 0 else None
                S0T = sb(st, "S0T", [128, 1024])
                S.op("pool", lambda e: e.memset(S0T[:], 0.0), w=[S0T])
                G = [sb(st, f"G{i}", [128, HC]) for i in range(12)]
                cur_r, cur_k, cur_v, P0, P1, P2, E0, E1, E2, E3, A0, G0 = G
                bc = [sb(st, f"bc{i}", [128, HC]) for i in range(2)]
                TT = [sb(st, f"TT{i}", [128, 8, 128]) for i in range(6)]
                BtT, KtT = TT[0], TT[1]
                AtM = (TT[2], TT[3])
                RtM = (TT[4], TT[5])
                for t_ in TT[2:]:
                    S.op("pool", lambda e, t_=t_: e.memset(t_[:], 0.0), w=[t_])
                M = [sb(st, f"M{i}", [128, 8, 128]) for i in range(9)]
                low = sb(st, "low", [97, 128], BF16)
                loa = sb(st, "loa", [97, 128], BF16)
                log = sb(st, "log", [128, 2, 128], BF16)
                lov = sb(st, "lov", [65, 128], BF16)
                sm = [sb(st, f"sm{i}", [128, 16]) for i in range(6)]
                PCf = sb(st, "PCf", [128, 8])
                GTs = sb(st, "GTs", [128, 512])
                UTs = sb(st, "UTs", [128, 512])
                obT = sb(st, "obT", [128, 8, 128], BF16)
                ibc = [0]
                ibk = [0]

                def getbc(ap_row2d):
                    t = bc[ibc[0] % 2]
                    ibc[0] += 1
                    S.dma(t[:], ap_row2d.to_broadcast([128, HC]), w=[t])
                    return t

                def nb():
                    ibk[0] += 1
                    return banks[ibk[0] % 6]

                def tt(o, a, b, op, eng="dve"):
                    S.op(eng, lambda e: e.tensor_tensor(out=o[:], in0=a[:], in1=b[:], op=op), r=[a, b], w=[o])

                def lora2(dst, lhs_list, func):
                    for cb in range(HC // 512):
                        ps = nb()
                        for i, (lh, K, rh, c_off) in enumerate(lhs_list):
                            S.op("pe", lambda e, ps=ps, lh=lh, K=K, rh=rh, c_off=c_off, cb=cb, i=i: e.matmul(
                                ps[:], lh, rh[0:K, c_off + cb * 512:c_off + (cb + 1) * 512], start=(i == 0),
                                stop=(i == len(lhs_list) - 1)), r=[low, loa, log, lov, rh], w=[ps])
                        evac(dst, dst[:, cb * 512:(cb + 1) * 512], ps, ps[:], func=func)

                for c in range(NQ):
                    t0 = c * 128
                    S.dma(low[:], lo_w[:, t0:t0 + 128], w=[low])
                    S.dma(loa[:], lo_a[:, t0:t0 + 128], w=[loa])
                    S.dma(log[:], lo_g[:, t0:t0 + 128].rearrange("(k p) t -> p k t", p=128), w=[log])
                    if l > 0:
                        S.dma(lov[:], lo_v[:, t0:t0 + 128], w=[lov])
                    for hf in range(2):
                        co = hf * HC
                        for qi_, (cu, pv) in enumerate(((cur_r, P0), (cur_k, P1), (cur_v, P2))):
                            cs = qi_ * 2048 + co
                            S.dma(cu[:], rkv[1 + t0:1 + t0 + 128, cs:cs + HC], w=[cu])
                            S.dma(pv[:], rkv[t0:t0 + 128, cs:cs + HC], w=[pv])
                            mu = getbc(I["rw_mu_rkv"][l:l + 1, cs:cs + HC])
                            tt(pv, pv, cu, ALU.subtract, "pool")
                            tt(pv, pv, mu, ALU.mult, "pool")
                            tt(cu, cu, pv, ALU.add, "pool")
                        lora2(P0, [(low[:], 97, w2e, co)], AF.Sigmoid)
                        lora2(A0, [(loa[:], 97, a2e, co)], AF.Sigmoid)
                        lora2(G0, [(log[:, 0, :], 128, g2a, co), (log[:, 1, :], 128, g2b, co)], None)
                        if l > 0:
                            lora2(P1, [(lov[:], 65, v2e, co)], AF.Sigmoid)
                            S.dma(P2[:], vfirst[t0:t0 + 128, co:co + HC], w=[P2])
                            tt(P2, P2, cur_v, ALU.subtract)
                            tt(P2, P2, P1, ALU.mult)
                            tt(cur_v, cur_v, P2, ALU.add)
                        else:
                            S.dma(vfirst[t0:t0 + 128, co:co + HC], cur_v[:], r=[cur_v])
                        _chk(1)
                        for (tri_t, outs) in ((triI, ((E0, 1.0), (E1, -1.0))), (triS, ((E2, 1.0),)), (triA, ((E3, 1.0),))):
                            for cb in range(HC // 512):
                                ps = nb()
                                S.op("pe", lambda e, ps=ps, tri_t=tri_t, cb=cb: e.matmul(
                                    ps[:], tri_t[:], P0[:, cb * 512:(cb + 1) * 512], start=True, stop=True),
                                    r=[tri_t, P0], w=[ps])
                                for (dst, scl) in outs:
                                    evac(dst, dst[:, cb * 512:(cb + 1) * 512], ps, ps[:], func=AF.Exp, scale=scl)
                        psf = nb()
                        for j in range(8):
                            S.op("pe", lambda e, j=j, psf=psf: e.matmul(psf[:, j:j + 1], P0[:, j * 128:(j + 1) * 128],
                                                                         negcol[:], start=True, stop=True),
                                 r=[P0, negcol], w=[psf])
                        evac(PCf, PCf[:], psf, psf[:, 0:8], func=AF.Exp)
                        _chk(2)
                        kkb = getbc(I["rw_k_k"][l:l + 1, co:co + HC])
                        tt(P0, cur_k, kkb, ALU.mult)
                        tt(P1, P0, P0, ALU.mult, "pool")
                        S.op("dve", lambda e: e.tensor_reduce(out=sm[0][:], in_=P1[:].rearrange("p (h d) -> p h d", d=64),
                                                              axis=AX.X, op=ALU.add), r=[P1], w=[sm[0]])
                        S.op("act", lambda e: e.activation(out=sm[0][:], in_=sm[0][:], func=AF.Sqrt), r=[sm[0]], w=[sm[0]])
                        S.op("dve", lambda e: e.tensor_scalar(out=sm[0][:], in0=sm[0][:], scalar1=1e-12, scalar2=None,
                                                              op0=ALU.max), r=[sm[0]], w=[sm[0]])
                        S.op("dve", lambda e: e.reciprocal(out=sm[0][:], in_=sm[0][:]), r=[sm[0]], w=[sm[0]])
                        for h in range(16):
                            S.op("dve", lambda e, h=h: e.tensor_scalar(out=P0[:, h * 64:(h + 1) * 64],
                                                                       in0=P0[:, h * 64:(h + 1) * 64],
                                                                       scalar1=sm[0][:, h:h + 1], scalar2=None,
                                                                       op0=ALU.mult), r=[P0, sm[0]], w=[P0])
                        kab = getbc(I["rw_k_a"][l:l + 1, co:co + HC])
                        S.op("dve", lambda e: e.scalar_tensor_tensor(out=P1[:], in0=A0[:], scalar=-1.0, in1=kab[:],
                                                                     op0=ALU.add, op1=ALU.mult), r=[A0, kab], w=[P1])
                        S.op("dve", lambda e: e.scalar_tensor_tensor(out=P1[:], in0=P1[:], scalar=1.0, in1=cur_k[:],
                                                                     op0=ALU.add, op1=ALU.mult), r=[P1, cur_k], w=[P1])
                        tt(P2, P0, A0, ALU.mult)
                        S.op("dve", lambda e: e.scalar_tensor_tensor(out=E2[:], in0=P0[:], scalar=-1.0, in1=E2[:],
                                                                     op0=ALU.mult, op1=ALU.mult), r=[P0, E2], w=[E2])
                        tt(A0, P2, E1, ALU.mult, "pool")
                        tt(E1, P1, E1, ALU.mult)
                        tt(E0, cur_r, E0, ALU.mult, "pool")
                        tt(P2, P2, E3, ALU.mult)
                        tt(E3, P1, E3, ALU.mult, "pool")
                        rkb = getbc(I["rw_r_k"][l:l + 1, co:co + HC])
                        tt(P0, cur_r, P1, ALU.mult)
                        tt(P0, P0, rkb, ALU.mult)
                        S.op("dve", lambda e: e.tensor_reduce(out=sm[1][:], in_=P0[:].rearrange("p (h d) -> p h d", d=64),
                                                              axis=AX.X, op=ALU.add), r=[P0], w=[sm[1]])
                        _chk(3)
                        for src, dstT in ((E2, AtM), (A0, BtT), (E1, KtT), (E0, RtM)):
                            for j0 in range(0, 8, 4):
                                ps = nb()
                                for j in range(4):
                                    S.op("pe", lambda e, ps=ps, j=j, j0=j0, src=src: e.matmul(
                                        ps[:, j * 128:(j + 1) * 128], src[:, (j0 + j) * 128:(j0 + j + 1) * 128],
                                        ident_f[:], start=True, stop=True), r=[src, ident_f], w=[ps])
                                if isinstance(dstT, tuple):
                                    for par in range(2):
                                        pr = slice(64 * par, 64 * par + 64)
                                        evac(dstT[par], dstT[par][pr, j0:j0 + 4, :], ps,
                                             ps[pr, :].rearrange("p (a b) -> p a b", b=128))
                                else:
                                    evac(dstT, dstT[:, j0:j0 + 4, :], ps, ps[:].rearrange("p (a b) -> p a b", b=128))
                        _chk(4)
                        yb = E0
                        for gi in range(2):
                            N_, NT_, Lak, Mbr, Mkr, An, ATn, Tm, Tmn = M

                            def cc(dst, lh, rh, msk):
                                for half in range(2):
                                    ps = nb()
                                    for hh in range(4):
                                        h = 8 * gi + 4 * half + hh
                                        lh_ = lh[h % 2] if isinstance(lh, tuple) else lh
                                        rh_ = rh[h % 2] if isinstance(rh, tuple) else rh
                                        S.op("pe", lambda e, ps=ps, hh=hh, h=h, lh_=lh_, rh_=rh_: e.matmul(
                                            ps[:, hh * 128:(hh + 1) * 128], lh_[:, h // 2, :], rh_[:, h // 2, :],
                                            start=True, stop=True), r=[lh_, rh_], w=[ps])
                                    S.op("dve", lambda e, ps=ps, half=half: e.tensor_tensor(
                                        out=dst[:, 4 * half:4 * half + 4, :],
                                        in0=ps[:].rearrange("p (a b) -> p a b", b=128), in1=msk[:], op=ALU.mult),
                                        r=[ps, msk], w=[dst])
                            cc(N_, BtT, AtM, msS)
                            cc(NT_, AtM, BtT, msT)
                            cc(Lak, KtT, AtM, msS)
                            cc(Mbr, BtT, RtM, msI)
                            cc(Mkr, KtT, RtM, msI)
                            _chk(5)
                            for half in range(2):
                                S.op("dve", lambda e, half=half: e.tensor_tensor(
                                    out=Tm[:, 4 * half:4 * half + 4, :], in0=N_[:, 4 * half:4 * half + 4, :],
                                    in1=id4[:], op=ALU.add), r=[N_, id4], w=[Tm])
                            A, AT = N_, NT_

                            def mm8(dst, lh, rh, add=None):
                                for half in range(2):
                                    ps = nb()
                                    for hh in range(4):
                                        j = 4 * half + hh
                                        S.op("pe", lambda e, ps=ps, hh=hh, j=j: e.matmul(
                                            ps[:, hh * 128:(hh + 1) * 128], lh[:, j, :], rh[:, j, :], start=True,
                                            stop=True), r=[lh, rh], w=[ps])
                                    pv3 = ps[:].rearrange("p (a b) -> p a b", b=128)
                                    if add is None:
                                        evac(dst, dst[:, 4 * half:4 * half + 4, :], ps, pv3)
                                    else:
                                        S.op("dve", lambda e, half=half, pv3=pv3, ps=ps: e.tensor_tensor(
                                            out=dst[:, 4 * half:4 * half + 4, :], in0=pv3,
                                            in1=add[:, 4 * half:4 * half + 4, :], op=ALU.add), r=[ps, add], w=[dst])
                            for j in range(1, 7):
                                if j < 6:
                                    mm8(An, AT, A)
                                mm8(ATn, A, AT)
                                A, An = An, A
                                AT, ATn = ATn, AT
                                mm8(Tmn, AT, Tm, add=Tm)
                                Tm, Tmn = Tmn, Tm
                            _chk(6)
                            psg = nb()
                            for hh in range(8):
                                h = 8 * gi + hh
                                pr = slice(64 * (h % 2), 64 * (h % 2) + 64)
                                hp = h // 2
                                gp = hf * 8 + hp
                                S.op("pe", lambda e, hh=hh, h=h, hp=hp, gp=gp: e.matmul(
                                    psg[:, hh * 64:(hh + 1) * 64], AtM[h % 2][:, hp, :], S0T[:, gp * 64:(gp + 1) * 64],
                                    start=True, stop=False), r=[AtM[h % 2], S0T], w=[psg])
                                S.op("pe", lambda e, hh=hh, h=h: e.matmul(
                                    psg[:, hh * 64:(hh + 1) * 64], Lak[:, hh, :], cur_v[:, h * 64:(h + 1) * 64],
                                    start=False, stop=True), r=[Lak, cur_v], w=[psg])
                            evac(GTs, GTs[:], psg, psg[:])
                            psu = nb()
                            for hh in range(8):
                                S.op("pe", lambda e, hh=hh: e.matmul(psu[:, hh * 64:(hh + 1) * 64], Tm[:, hh, :],
                                                                     GTs[:, hh * 64:(hh + 1) * 64], start=True, stop=True),
                                     r=[Tm, GTs], w=[psu])
                            evac(UTs, UTs[:], psu, psu[:])
                            psy = nb()
                            for hh in range(8):
                                h = 8 * gi + hh
                                pr = slice(64 * (h % 2), 64 * (h % 2) + 64)
                                hp = h // 2
                                gp = hf * 8 + hp
                                ys = psy[:, hh * 64:(hh + 1) * 64]
                                S.op("pe", lambda e, ys=ys, h=h, hp=hp, gp=gp: e.matmul(
                                    ys, RtM[h % 2][:, hp, :], S0T[:, gp * 64:(gp + 1) * 64], start=True, stop=False),
                                    r=[RtM[h % 2], S0T], w=[psy])
                                S.op("pe", lambda e, ys=ys, hh=hh: e.matmul(ys, Mbr[:, hh, :], UTs[:, hh * 64:(hh + 1) * 64],
                                                                            start=False, stop=False), r=[Mbr, UTs], w=[psy])
                                S.op("pe", lambda e, ys=ys, hh=hh, h=h: e.matmul(ys, Mkr[:, hh, :],
                                                                                 cur_v[:, h * 64:(h + 1) * 64],
                                                                                 start=False, stop=True),
                                     r=[Mkr, cur_v], w=[psy])
                            evac(yb, yb[:, gi * 512:(gi + 1) * 512], psy, psy[:])
                            pss = nb()
                            for pi in range(4):
                                hp = 4 * gi + pi
                                so = pss[:, pi * 128:(pi + 1) * 128]
                                S.op("pe", lambda e, so=so, hp=hp, pi=pi: e.matmul(
                                    so, P2[:, hp * 128:(hp + 1) * 128], UTs[:, pi * 128:(pi + 1) * 128], start=True,
                                    stop=False), r=[P2, UTs], w=[pss])
                                S.op("pe", lambda e, so=so, hp=hp: e.matmul(
                                    so, E3[:, hp * 128:(hp + 1) * 128], cur_v[:, hp * 128:(hp + 1) * 128], start=False,
                                    stop=True), r=[E3, cur_v], w=[pss])
                            for pi in range(4):
                                hp = 4 * gi + pi
                                gp = hf * 8 + hp
                                for par in range(2):
                                    pr = slice(64 * par, 64 * par + 64)
                                    S.op("dve", lambda e, pr=pr, par=par, pi=pi, hp=hp, gp=gp: e.scalar_tensor_tensor(
                                        out=S0T[pr, gp * 64:(gp + 1) * 64], in0=S0T[pr, gp * 64:(gp + 1) * 64],
                                        scalar=PCf[pr, hp:hp + 1],
                                        in1=pss[pr, pi * 128 + par * 64:pi * 128 + par * 64 + 64],
                                        op0=ALU.mult, op1=ALU.add), r=[S0T, PCf, pss], w=[S0T])
                        _chk(7)
                        y3 = yb[:].rearrange("p (h d) -> p h d", d=64)
                        S.op("dve", lambda e: e.tensor_reduce(out=sm[2][:], in_=y3, axis=AX.X, op=ALU.add), r=[yb], w=[sm[2]])
                        tt(P0, yb, yb, ALU.mult, "pool")
                        S.op("dve", lambda e: e.tensor_reduce(out=sm[3][:], in_=P0[:].rearrange("p (h d) -> p h d", d=64),
                                                              axis=AX.X, op=ALU.add), r=[P0], w=[sm[3]])
                        S.op("dve", lambda e: e.tensor_scalar(out=sm[2][:], in0=sm[2][:], scalar1=1.0 / 64, scalar2=None,
                                                              op0=ALU.mult), r=[sm[2]], w=[sm[2]])
                        tt(sm[4], sm[2], sm[2], ALU.mult)
                        S.op("dve", lambda e: e.scalar_tensor_tensor(out=sm[3][:], in0=sm[3][:], scalar=1.0 / 64,
                                                                     in1=sm[4][:], op0=ALU.mult, op1=ALU.subtract),
                             r=[sm[3], sm[4]], w=[sm[3]])
                        S.op("act", lambda e: e.activation(out=sm[3][:], in_=sm[3][:], func=AF.Sqrt, bias=epsg[:]),
                             r=[sm[3], epsg], w=[sm[3]])
                        S.op("dve", lambda e: e.reciprocal(out=sm[3][:], in_=sm[3][:]), r=[sm[3]], w=[sm[3]])
                        for h in range(16):
                            hs = slice(h * 64, (h + 1) * 64)
                            S.op("dve", lambda e, h=h, hs=hs: e.tensor_scalar(
                                out=yb[:, hs], in0=yb[:, hs], scalar1=sm[2][:, h:h + 1], scalar2=sm[3][:, h:h + 1],
                                op0=ALU.subtract, op1=ALU.mult), r=[yb, sm[2], sm[3]], w=[yb])
                        lwb = getbc(I["rw_ln_w"][l:l + 1, co:co + HC])
                        tt(yb, yb, lwb, ALU.mult)
                        lbb = getbc(I["rw_ln_b"][l:l + 1, co:co + HC])
                        tt(yb, yb, lbb, ALU.add)
                        for h in range(16):
                            hs = slice(h * 64, (h + 1) * 64)
                            S.op("dve", lambda e, h=h, hs=hs: e.scalar_tensor_tensor(
                                out=yb[:, hs], in0=cur_v[:, hs], scalar=sm[1][:, h:h + 1], in1=yb[:, hs],
                                op0=ALU.mult, op1=ALU.add), r=[cur_v, sm[1], yb], w=[yb])
                        tt(yb, yb, G0, ALU.mult)
                        for j0 in range(0, 8, 4):
                            ps = nb()
                            for j in range(4):
                                S.op("pe", lambda e, ps=ps, j=j, j0=j0: e.matmul(
                                    ps[:, j * 128:(j + 1) * 128], yb[:, (j0 + j) * 128:(j0 + j + 1) * 128], ident_f[:],
                                    start=True, stop=True), r=[yb, ident_f], w=[ps])
                            evac(obT, obT[:, j0:j0 + 4, :], ps, ps[:].rearrange("p (a b) -> p a b", b=128))
                        S.dma(mixT[1024 + co:1024 + co + HC, t0:t0 + 128].rearrange("(j p) t -> p j t", p=128),
                              obT[:], r=[obT])
                S.barrier()
        def cast_to_dram(src2d, dst2d, rows, cols):
            with contextlib.ExitStack() as st:
                fb = [sb(st, f"cf{i}", [128, 2048]) for i in range(3)]
                bb = [sb(st, f"cb{i}", [128, 2048], BF16) for i in range(3)]
                i = 0
                for r0 in range(0, rows, 128):
                    for c0 in range(0, cols, 2048):
                        w = min(2048, cols - c0)
                        f, b = fb[i % 3], bb[i % 3]
                        S.dma(f[:, 0:w], src2d[r0:r0 + 128, c0:c0 + w], w=[f])
                        eng = ("act", "pool", "dve")[i % 3]
                        if eng == "act":
                            S.op("act", lambda e, f=f, b=b, w=w: e.copy(out=b[:, 0:w], in_=f[:, 0:w]), r=[f], w=[b])
                        else:
                            S.op(eng, lambda e, f=f, b=b, w=w: e.tensor_copy(out=b[:, 0:w], in_=f[:, 0:w]), r=[f], w=[b])
                        S.dma(dst2d[r0:r0 + 128, c0:c0 + w], b[:, 0:w], r=[b])
                        i += 1
                S.barrier()

        def phase_ffn(l):
            cast_to_dram(I["w_out"][l], wob, D, D)
            cast_to_dram(I["w_up"][l], wub, D, DFF)
            cast_to_dram(I["w_down"][l], wdb, DFF, D)
            last = (l == L - 1)
            wov = wob.rearrange("(c p) n -> p c n", p=128)
            wuv = wub.rearrange("(c p) n -> p c n", p=128)
            wdv = wdb.rearrange("(c p) n -> p c n", p=128)
            mv = mixT.rearrange("(c p) t -> p c t", p=128)
            ov = outT.rearrange("(c p) t -> p c t", p=128)
            with contextlib.ExitStack() as st:
                g_t = load_vec_fm(st, "g_ffn", I["norm_ffn_g"][l])
                gf_t = load_vec_fm(st, "g_fin", I["norm_final_g"][0]) if last else None
                acc = sb(st, "acc", [128, KC, 512])
                hb = sb(st, "hb", [128, KC, 512], BF16)
                wbs = [sb(st, f"fw{i}", [128, KC, 512], BF16) for i in range(2)]
                act_t = [sb(st, f"fa{i}", [128, 4, 512], BF16) for i in range(2)]
                iw = 0
                ib = 0
                for tt in range(NT):
                    ts = slice(tt * 512, (tt + 1) * 512)
                    for c0 in range(0, KC, 8):
                        S.dma(acc[:, c0:c0 + 8, :], xv[:, c0:c0 + 8, ts], w=[acc])
                        S.dma(hb[:, c0:c0 + 8, :], mv[:, c0:c0 + 8, ts], w=[hb])
                    for nb in range(D // 512):
                        wb = wbs[iw % 2]
                        iw += 1
                        for c0 in range(0, KC, 8):
                            S.dma(wb[:, c0:c0 + 8, :], wov[:, c0:c0 + 8, nb * 512:(nb + 1) * 512], w=[wb])
                        for m in range(4):
                            ps = banks[ib % 4]
                            ib += 1
                            for kc in range(KC):
                                S.op("pe", lambda e, ps=ps, kc=kc, m=m, wb=wb: e.matmul(
                                    ps[:], wb[:, kc, m * 128:(m + 1) * 128], hb[:, kc, :], start=(kc == 0),
                                    stop=(kc == KC - 1)), r=[wb, hb], w=[ps])
                            mc = nb * 4 + m
                            S.op("dve", lambda e, ps=ps, mc=mc: e.tensor_tensor(out=acc[:, mc, :], in0=ps[:],
                                                                                in1=acc[:, mc, :], op=ALU.add),
                                 r=[ps, acc], w=[acc])
                    with contextlib.ExitStack() as st2:
                        S.barrier()
                        rmsnorm(st2, acc, g_t, hb, 512, "n2")
                        S.barrier()
                    for fb in range(DFF // 512):
                        wu = wbs[iw % 2]
                        iw += 1
                        for c0 in range(0, KC, 8):
                            S.dma(wu[:, c0:c0 + 8, :], wuv[:, c0:c0 + 8, fb * 512:(fb + 1) * 512], w=[wu])
                        a_t = act_t[fb % 2]
                        for m in range(4):
                            ps = banks[ib % 4]
                            ib += 1
                            for kc in range(KC):
                                S.op("pe", lambda e, ps=ps, kc=kc, m=m, wu=wu: e.matmul(
                                    ps[:], wu[:, kc, m * 128:(m + 1) * 128], hb[:, kc, :], start=(kc == 0),
                                    stop=(kc == KC - 1)), r=[wu, hb], w=[ps])
                            S.op("act", lambda e, ps=ps, m=m, a_t=a_t: e.activation(out=a_t[:, m, :], in_=ps[:],
                                                                                   func=AF.Relu), r=[ps], w=[a_t])
                        S.op("pool", lambda e, a_t=a_t: e.tensor_tensor(out=a_t[:], in0=a_t[:], in1=a_t[:],
                                                                        op=ALU.mult), r=[a_t], w=[a_t])
                        wd = wbs[iw % 2]
                        iw += 1
                        wd4 = wd[:].rearrange("p (a b) n -> p a (b n)", a=4)
                        S.dma(wd4, wdv[:, fb * 4:(fb + 1) * 4, :], w=[wd])
                        for mc in range(KC):
                            ps = banks[ib % 4]
                            ib += 1
                            for kc in range(4):
                                S.op("pe", lambda e, ps=ps, kc=kc, mc=mc, wd4=wd4, a_t=a_t: e.matmul(
                                    ps[:], wd4[:, kc, mc * 128:(mc + 1) * 128], a_t[:, kc, :], start=(kc == 0),
                                    stop=(kc == 3)), r=[wd, a_t], w=[ps])
                            S.op("dve", lambda e, ps=ps, mc=mc: e.tensor_tensor(out=acc[:, mc, :], in0=ps[:],
                                                                                in1=acc[:, mc, :], op=ALU.add),
                                 r=[ps, acc], w=[acc])
                    if last:
                        fo = sb(st, "fo", [128, 8, 512]) if tt == 0 else fo
                        with contextlib.ExitStack() as st2:
                            S.barrier()
                            sq = [sb(st2, f"sqf{i}", [128, 512], BF16) for i in range(2)]
                            rstd = sb(st2, "rstdf", [128, 512])
                            ps = banks[6]
                            for kc in range(KC):
                                q = sq[kc % 2]
                                S.op("act", lambda e, kc=kc, q=q: e.activation(out=q[:], in_=acc[:, kc, :],
                                                                               func=AF.Square), r=[acc], w=[q])
                                S.op("pe", lambda e, kc=kc, q=q: e.matmul(ps[:], ones_b[:], q[:], start=(kc == 0),
                                                                          stop=(kc == KC - 1)), r=[q, ones_b], w=[ps])
                            S.op("act", lambda e: e.activation(out=rstd[:], in_=ps[:], func=AF.Sqrt, bias=eps_t[:],
                                                               scale=1.0 / D), r=[ps, eps_t], w=[rstd])
                            S.op("dve", lambda e: e.reciprocal(out=rstd[:], in_=rstd[:]), r=[rstd], w=[rstd])
                            for c0 in range(0, KC, 8):
                                for kc in range(c0, c0 + 8):
                                    S.op("dve", lambda e, kc=kc, c0=c0: e.scalar_tensor_tensor(
                                        out=fo[:, kc - c0, :], in0=acc[:, kc, :], scalar=gf_t[:, kc:kc + 1],
                                        in1=rstd[:], op0=ALU.mult, op1=ALU.mult), r=[acc, rstd, gf_t], w=[fo])
                                S.dma(ov[:, c0:c0 + 8, ts], fo[:], r=[fo])
                            S.barrier()
                    else:
                        for c0 in range(0, KC, 8):
                            S.dma(xv[:, c0:c0 + 8, ts], acc[:, c0:c0 + 8, :], r=[acc])
                S.barrier()

        for l in range(L):
            for nm, fn in (("norm1", phase_norm1), ("proj", phase_proj), ("lora", phase_lora), ("attn", phase_attn),
                           ("dsa", phase_dsa), ("rwkv", phase_rwkv), ("ffn", phase_ffn)):
                if phases is None or nm in phases:
                    _DEAD[0] = False
                    fn(l)
                    _DEAD[0] = False
        S.barrier()
    return nc


def kernel(**inputs):
    T = inputs["x"].shape[1]
    nc = build(T=T, DEPTH=inputs["w_in"].shape[0], DFF=inputs["w_up"].shape[2])
    m = {"xT": np.ascontiguousarray(inputs["x"][0].T)}
    for k, v in inputs.items():
        if k == "x":
            continue
        a = np.asarray(v, dtype=np.float32)
        if k == "rw_mu_rkv":
            a = a.reshape(a.shape[0], -1)
        elif k == "rw_r_k":
            a = a.reshape(a.shape[0], -1)
        elif k == "norm_final_g":
            a = a.reshape(1, -1)
        m[k] = np.ascontiguousarray(a)
    res = run_bass_kernel_spmd(nc, [m], core_ids=[0])
    out = res.results[0]["outT"]
    return np.ascontiguousarray(out.T)[None].astype(np.float32)
```
